# Optimizing a Trainium2 kernel written in Bass

```python
import jax
import jax.numpy as jnp
from jax import lax
import numpy as np

D_MODEL = 1024
BATCH = 4
SEQ = 4096
DEPTH = 1
DEC_BATCH = 128
DEC_SEQ = 1
PAST_LEN = 2048
PAGE_SIZE = 128

HEAD_DIM = 64
DN_HEADS = 8
NSA_HEADS = 8
NSA_KV_HEADS = 2
NSA_GROUP = NSA_HEADS // NSA_KV_HEADS
DN_WIDTH = DN_HEADS * HEAD_DIM
NSA_WIDTH = NSA_HEADS * HEAD_DIM
MIX_WIDTH = DN_WIDTH + NSA_WIDTH
CONV_WIDTH = 4
DN_CONV_DIM = 3 * DN_WIDTH
DN_CHUNK = 64
CMP_BLOCK = 64
SEL_BLOCK = 64
TOP_N = 16
N_LOCAL_BLOCKS = 2
WINDOW = 512
CMP_HIDDEN = 128
Q_BLOCK = 128
NSA_KV_KINDS = 6
MEM_LEN = 256
MEM_HEADS = 4
MEM_HEAD_DIM = D_MODEL // MEM_HEADS
D_FF = 4 * D_MODEL
RMS_EPS = 1e-6
FORCE_SCORE = 1e3
NEG_INF = -1e30
ATTN_SCALE = HEAD_DIM ** -0.5

OFF_DN_QKV = 0
OFF_DN_Z = OFF_DN_QKV + DN_CONV_DIM
OFF_DN_B = OFF_DN_Z + DN_WIDTH
OFF_DN_A = OFF_DN_B + DN_HEADS
OFF_NSA_Q = OFF_DN_A + DN_HEADS
OFF_NSA_KV = OFF_NSA_Q + NSA_WIDTH
OFF_NSA_G = OFF_NSA_KV + NSA_KV_KINDS * NSA_KV_HEADS * HEAD_DIM
IN_COLS = OFF_NSA_G + 3 * NSA_HEADS

kernel_name = 'hymba_deltanet_nsa_decode_step'


def rms_norm(x, gain):
    xf = x.astype(jnp.float32)
    y = xf * lax.rsqrt(jnp.mean(xf * xf, axis=-1, keepdims=True) + RMS_EPS)
    return (y * gain.astype(jnp.float32)).astype(x.dtype)


def masked_softmax(s, mask):
    s = jnp.where(mask, s.astype(jnp.float32), NEG_INF)
    p = jnp.where(mask, jnp.exp(s - jnp.max(s, axis=-1, keepdims=True)), 0.0)
    return p / jnp.maximum(jnp.sum(p, axis=-1, keepdims=True), 1e-30)


def l2_normalize(x):
    return x * lax.rsqrt(jnp.sum(x * x, axis=-1, keepdims=True) + 1e-6)


def causal_depthwise_conv(x_ext, w):
    return lax.conv_general_dilated(
        x_ext, w[:, None, :].astype(x_ext.dtype), window_strides=(1,), padding='VALID',
        dimension_numbers=('NWC', 'WIO', 'NWC'), feature_group_count=x_ext.shape[-1])


def gated_delta_chunked(q, k, v, g, beta, s0):
    B, T, H, _ = q.shape
    C = min(DN_CHUNK, T)
    n = -(-T // C)
    pad = n * C - T

    def to_chunks(a):
        a = jnp.pad(a, [(0, 0), (0, pad)] + [(0, 0)] * (a.ndim - 2))
        a = a.reshape((B, n, C) + a.shape[2:])
        return jnp.swapaxes(jnp.moveaxis(a, 1, 0), 2, 3)

    qc, kc, vc, gc, bc = (to_chunks(a) for a in (q, k, v, g, beta))
    gcum = jnp.cumsum(gc, axis=-1)
    pos = jnp.arange(C)
    incl = pos[:, None] >= pos[None, :]
    strict = pos[:, None] > pos[None, :]
    decay = jnp.exp(jnp.where(incl, gcum[..., :, None] - gcum[..., None, :], NEG_INF))
    kb = kc * bc[..., None]
    a = jnp.where(strict, jnp.einsum('nbhid,nbhjd->nbhij', kb, kc) * decay, 0.0)
    m = a + jnp.eye(C, dtype=a.dtype)
    u = lax.linalg.triangular_solve(m, vc * bc[..., None], left_side=True, lower=True, unit_diagonal=True)
    w = lax.linalg.triangular_solve(m, kb * jnp.exp(gcum)[..., None], left_side=True, lower=True,
                                    unit_diagonal=True)
    aqk = jnp.where(incl, jnp.einsum('nbhid,nbhjd->nbhij', qc, kc) * decay, 0.0)

    def step(S, inp):
        q_i, k_i, u_i, w_i, g_i, aqk_i = inp
        v_new = u_i - jnp.einsum('bhck,bhkv->bhcv', w_i, S)
        o_i = (jnp.einsum('bhck,bhkv->bhcv', q_i * jnp.exp(g_i)[..., None], S)
               + jnp.einsum('bhij,bhjv->bhiv', aqk_i, v_new))
        g_last = g_i[..., -1:]
        S = (S * jnp.exp(g_last)[..., None]
             + jnp.einsum('bhck,bhcv->bhkv', k_i * jnp.exp(g_last - g_i)[..., None], v_new))
        return S, o_i

    s_final, o = lax.scan(step, s0, (qc, kc, u, w, gcum, aqk))
    o = jnp.moveaxis(jnp.swapaxes(o, 2, 3), 0, 1).reshape(B, n * C, H, -1)[:, :T]
    return o, s_final


def deltanet_mixer(p_qkv, p_z, p_b, p_a, conv_state, rec_state, conv_w, a_log, dt_bias, norm_gain):
    B, T, _ = p_qkv.shape
    x_ext = jnp.concatenate([conv_state.astype(p_qkv.dtype), p_qkv], axis=1)
    c = jax.nn.silu(causal_depthwise_conv(x_ext, conv_w).astype(jnp.float32))
    c = c.reshape(B, T, 3, DN_HEADS, HEAD_DIM)
    q = l2_normalize(c[:, :, 0]) * HEAD_DIM ** -0.5
    k = l2_normalize(c[:, :, 1])
    v = c[:, :, 2]
    beta = jax.nn.sigmoid(p_b.astype(jnp.float32))
    g = -jnp.exp(a_log.astype(jnp.float32)) * jax.nn.softplus(
        p_a.astype(jnp.float32) + dt_bias.astype(jnp.float32))
    o, new_rec = gated_delta_chunked(q, k, v, g, beta, rec_state.astype(jnp.float32))
    z = p_z.astype(jnp.float32).reshape(B, T, DN_HEADS, HEAD_DIM)
    o = rms_norm(o, norm_gain) * jax.nn.silu(z)
    return o.reshape(B, T, DN_WIDTH).astype(p_qkv.dtype), x_ext[:, -(CONV_WIDTH - 1):], new_rec


def compress_blocks(k, pe, w1, b1, w2):
    B, T, G, dh = k.shape
    nb = T // CMP_BLOCK
    kb = k[:, :nb * CMP_BLOCK].reshape(B, nb, CMP_BLOCK, G, dh) + pe[:, None, :]
    flat = kb.transpose(0, 1, 3, 2, 4).reshape(B, nb, G, CMP_BLOCK * dh)
    return jax.nn.relu(flat @ w1 + b1) @ w2


def slc_win_block(qg_t, q_pos, sel_idx, kv_slc, kv_win, win_pos):
    B, G, Sq, n = sel_idx.shape
    bi = jnp.arange(B)[:, None, None, None]
    gi = jnp.arange(G)[None, :, None, None]
    kv = kv_slc[bi, gi, sel_idx].reshape(B, G, Sq, n * SEL_BLOCK, 2, HEAD_DIM)
    kpos = (sel_idx[..., None] * SEL_BLOCK + jnp.arange(SEL_BLOCK)).reshape(B, G, Sq, n * SEL_BLOCK)
    s = jnp.einsum('bgrqd,bgqkd->bgrqk', qg_t, kv[..., 0, :], preferred_element_type=jnp.float32) * ATTN_SCALE
    p = masked_softmax(s, (kpos <= q_pos[None, None, :, None])[:, :, None])
    o_slc = jnp.einsum('bgrqk,bgqkd->bqgrd', p, kv[..., 1, :].astype(jnp.float32))
    s = jnp.einsum('bgrqd,bkgd->bgrqk', qg_t, kv_win[:, :, 0], preferred_element_type=jnp.float32) * ATTN_SCALE
    dpos = q_pos[:, None] - win_pos[None, :]
    wmask = (dpos >= 0) & (dpos < WINDOW) & (win_pos >= 0)[None, :]
    p = masked_softmax(s, wmask)
    o_win = jnp.einsum('bgrqk,bkgd->bqgrd', p, kv_win[:, :, 1].astype(jnp.float32))
    return o_slc, o_win


def nsa_mixer(q, gate_logits, kv_all, win_ctx, q_pos, win_pos, banded, cmp_pe, cmp_w1, cmp_b1, cmp_w2):
    B, Sq, _, _ = q.shape
    T = kv_all.shape[1]
    k_cmp = compress_blocks(kv_all[:, :, 0], cmp_pe[0], cmp_w1[0], cmp_b1[0], cmp_w2[0])
    v_cmp = compress_blocks(kv_all[:, :, 1], cmp_pe[1], cmp_w1[1], cmp_b1[1], cmp_w2[1])
    nb = k_cmp.shape[1]
    qg = q.reshape(B, Sq, NSA_KV_HEADS, NSA_GROUP, HEAD_DIM)
    s = jnp.einsum('bqgrd,bngd->bgrqn', qg, k_cmp, preferred_element_type=jnp.float32) * ATTN_SCALE
    blk_end = (jnp.arange(nb) + 1) * CMP_BLOCK - 1
    p = masked_softmax(s, blk_end[None, :] <= q_pos[:, None])
    o_cmp = jnp.einsum('bgrqn,bngd->bqgrd', p, v_cmp.astype(jnp.float32))
    n_sel = -(-T // SEL_BLOCK)
    imp = jnp.pad(p.sum(axis=2), ((0, 0), (0, 0), (0, 0), (0, n_sel - nb)))
    blk = jnp.arange(n_sel)[None, :]
    cur = (q_pos // SEL_BLOCK)[:, None]
    valid = blk <= cur
    forced = valid & ((blk == 0) | (cur - blk < N_LOCAL_BLOCKS))
    score = jnp.where(valid, imp + jnp.where(forced, FORCE_SCORE, 0.0), -1.0)
    _, sel_idx = lax.top_k(score, min(TOP_N, n_sel))
    kv_slc = jnp.pad(kv_all[:, :, 2:4], ((0, 0), (0, n_sel * SEL_BLOCK - T), (0, 0), (0, 0), (0, 0)))
    kv_slc = kv_slc.reshape(B, n_sel, SEL_BLOCK, 2, NSA_KV_HEADS, HEAD_DIM).transpose(0, 4, 1, 2, 3, 5)
    qg_t = qg.transpose(0, 2, 3, 1, 4)
    if banded:
        nqb = Sq // Q_BLOCK
        kv_win_pad = jnp.pad(win_ctx, ((0, 0), (WINDOW, 0), (0, 0), (0, 0), (0, 0)))

        def body(args):
            qb, posb, idxb, start = args
            kvw = lax.dynamic_slice_in_dim(kv_win_pad, start, WINDOW + Q_BLOCK, axis=1)
            wpos = start - WINDOW + jnp.arange(WINDOW + Q_BLOCK)
            return slc_win_block(qb, posb, idxb, kv_slc, kvw, wpos)

        qb = qg_t.reshape(B, NSA_KV_HEADS, NSA_GROUP, nqb, Q_BLOCK, HEAD_DIM).transpose(3, 0, 1, 2, 4, 5)
        posb = q_pos.reshape(nqb, Q_BLOCK)
        idxb = sel_idx.reshape(B, NSA_KV_HEADS, nqb, Q_BLOCK, -1).transpose(2, 0, 1, 3, 4)
        starts = jnp.arange(nqb) * Q_BLOCK
        o_slc, o_win = lax.map(body, (qb, posb, idxb, starts))
        o_slc = jnp.moveaxis(o_slc, 0, 1).reshape(B, Sq, NSA_KV_HEADS, NSA_GROUP, HEAD_DIM)
        o_win = jnp.moveaxis(o_win, 0, 1).reshape(B, Sq, NSA_KV_HEADS, NSA_GROUP, HEAD_DIM)
    else:
        o_slc, o_win = slc_win_block(qg_t, q_pos, sel_idx, kv_slc, win_ctx, win_pos)
    gates = jax.nn.sigmoid(gate_logits.astype(jnp.float32)).reshape(B, Sq, NSA_KV_HEADS, NSA_GROUP, 3)
    o = gates[..., 0:1] * o_cmp + gates[..., 1:2] * o_slc + gates[..., 2:3] * o_win
    return o.reshape(B, Sq, NSA_WIDTH).astype(q.dtype)


def mem_cross_attn(h, mem_kv, w_q, w_o):
    B, T, _ = h.shape
    q = (h @ w_q).reshape(B, T, MEM_HEADS, MEM_HEAD_DIM)
    s = jnp.einsum('bthd,bmhd->bhtm', q, mem_kv[:, :, 0], preferred_element_type=jnp.float32) * MEM_HEAD_DIM ** -0.5
    p = jax.nn.softmax(s, axis=-1)
    o = jnp.einsum('bhtm,bmhd->bthd', p, mem_kv[:, :, 1].astype(jnp.float32))
    return o.reshape(B, T, MEM_HEADS * MEM_HEAD_DIM).astype(h.dtype) @ w_o


def trunk_layer(x, mem_kv, conv_state, rec_state, kv_past, win_past, q_pos, win_pos, win_keep, banded,
                ln_mix, w_in, dn_conv_w, dn_a_log, dn_dt_bias, dn_norm, cmp_pe, cmp_w1, cmp_b1, cmp_w2,
                w_out, ln_mem, w_mem_q, w_mem_o, ln_ffn, w_up, w_down):
    B, T, _ = x.shape
    proj = rms_norm(x, ln_mix) @ w_in
    dn_out, new_conv, new_rec = deltanet_mixer(
        proj[..., OFF_DN_QKV:OFF_DN_Z], proj[..., OFF_DN_Z:OFF_DN_B], proj[..., OFF_DN_B:OFF_DN_A],
        proj[..., OFF_DN_A:OFF_NSA_Q], conv_state, rec_state, dn_conv_w, dn_a_log, dn_dt_bias, dn_norm)
    q = proj[..., OFF_NSA_Q:OFF_NSA_KV].reshape(B, T, NSA_HEADS, HEAD_DIM)
    kv_new = proj[..., OFF_NSA_KV:OFF_NSA_G].reshape(B, T, NSA_KV_KINDS, NSA_KV_HEADS, HEAD_DIM)
    kv_rows = kv_new[:, :, :4]
    win_rows = kv_new[:, :, 4:]
    kv_all = kv_rows if kv_past is None else jnp.concatenate([kv_past.astype(kv_rows.dtype), kv_rows], axis=1)
    win_ctx = win_rows if win_past is None else jnp.concatenate([win_past.astype(win_rows.dtype), win_rows], axis=1)
    nsa_out = nsa_mixer(q, proj[..., OFF_NSA_G:], kv_all, win_ctx, q_pos, win_pos, banded,
                        cmp_pe, cmp_w1, cmp_b1, cmp_w2)
    x = x + jnp.concatenate([dn_out, nsa_out], axis=-1) @ w_out
    x = x + mem_cross_attn(rms_norm(x, ln_mem), mem_kv, w_mem_q, w_mem_o)
    x = x + jnp.square(jax.nn.relu(rms_norm(x, ln_ffn) @ w_up)) @ w_down
    return x, new_conv, new_rec, kv_rows, win_ctx[:, win_ctx.shape[1] - win_keep:]


def setup_inputs(seed: int = 0) -> dict:
    key = jax.random.key(seed)
    ks = jax.random.split(key, 32)
    f32 = jnp.float32

    def nrm(k, shape, scale):
        return jax.random.normal(k, shape, f32) * scale

    def gain(k, shape):
        return 1.0 + 0.05 * jax.random.normal(k, shape, f32)

    n_pages = PAST_LEN // PAGE_SIZE
    n_used = DEC_BATCH * n_pages
    n_phys = n_used + max(1, n_used // 4)
    wbuf = min(WINDOW, PAST_LEN)
    page_table = jax.random.permutation(ks[0], n_phys)[:n_used].reshape(DEC_BATCH, n_pages).astype(jnp.int32)
    dt = jnp.exp(jax.random.uniform(ks[1], (DEPTH, DN_HEADS), f32, float(np.log(1e-3)), float(np.log(1e-1))))
    return {
        'x_prompt': nrm(ks[2], (BATCH, SEQ, D_MODEL), 1.0),
        'x_sample': nrm(ks[3], (DEC_BATCH, DEC_SEQ, D_MODEL), 1.0),
        'mem_prompt': nrm(ks[4], (BATCH, MEM_LEN, D_MODEL), 1.0),
        'cache_nsa_kv': nrm(ks[5], (DEPTH, n_phys, PAGE_SIZE, 4, NSA_KV_HEADS, HEAD_DIM), 1.0),
        'cache_win_kv': nrm(ks[6], (DEPTH, DEC_BATCH, wbuf, 2, NSA_KV_HEADS, HEAD_DIM), 1.0),
        'cache_mem_kv': nrm(ks[7], (DEPTH, DEC_BATCH, MEM_LEN, 2, MEM_HEADS, MEM_HEAD_DIM), 1.0),
        'state_dn_conv': nrm(ks[8], (DEPTH, DEC_BATCH, CONV_WIDTH - 1, DN_CONV_DIM), 1.0),
        'state_dn_rec': nrm(ks[9], (DEPTH, DEC_BATCH, DN_HEADS, HEAD_DIM, HEAD_DIM), HEAD_DIM ** -0.5),
        'page_table': page_table,
        'ln_mix': gain(ks[10], (DEPTH, D_MODEL)),
        'w_in': nrm(ks[11], (DEPTH, D_MODEL, IN_COLS), D_MODEL ** -0.5),
        'dn_conv_w': nrm(ks[12], (DEPTH, CONV_WIDTH, DN_CONV_DIM), CONV_WIDTH ** -0.5),
        'dn_a_log': jnp.log(jax.random.uniform(ks[13], (DEPTH, DN_HEADS), f32, 1.0, 16.0)),
        'dn_dt_bias': dt + jnp.log(-jnp.expm1(-dt)),
        'dn_norm': gain(ks[14], (DEPTH, HEAD_DIM)),
        'cmp_pe': nrm(ks[15], (DEPTH, 2, CMP_BLOCK, HEAD_DIM), 0.5),
        'cmp_w1': nrm(ks[16], (DEPTH, 2, CMP_BLOCK * HEAD_DIM, CMP_HIDDEN), (CMP_BLOCK * HEAD_DIM) ** -0.5),
        'cmp_b1': nrm(ks[17], (DEPTH, 2, CMP_HIDDEN), 0.01),
        'cmp_w2': nrm(ks[18], (DEPTH, 2, CMP_HIDDEN, HEAD_DIM), (2.0 / CMP_HIDDEN) ** 0.5),
        'w_out': nrm(ks[19], (DEPTH, MIX_WIDTH, D_MODEL), MIX_WIDTH ** -0.5),
        'ln_mem': gain(ks[20], (DEPTH, D_MODEL)),
        'ln_memkv': gain(ks[21], (DEPTH, D_MODEL)),
        'w_mem_q': nrm(ks[22], (DEPTH, D_MODEL, MEM_HEADS * MEM_HEAD_DIM), D_MODEL ** -0.5),
        'w_mem_kv': nrm(ks[23], (DEPTH, D_MODEL, 2 * MEM_HEADS * MEM_HEAD_DIM), D_MODEL ** -0.5),
        'w_mem_o': nrm(ks[24], (DEPTH, MEM_HEADS * MEM_HEAD_DIM, D_MODEL), (MEM_HEADS * MEM_HEAD_DIM) ** -0.5),
        'ln_ffn': gain(ks[25], (DEPTH, D_MODEL)),
        'w_up': nrm(ks[26], (DEPTH, D_MODEL, D_FF), D_MODEL ** -0.5),
        'w_down': nrm(ks[27], (DEPTH, D_FF, D_MODEL), D_FF ** -0.5),
        'ln_final': gain(ks[28], (D_MODEL,)),
    }


def reference(x_prompt, x_sample, mem_prompt, cache_nsa_kv, cache_win_kv, cache_mem_kv, state_dn_conv,
              state_dn_rec, page_table, ln_mix, w_in, dn_conv_w, dn_a_log, dn_dt_bias, dn_norm, cmp_pe,
              cmp_w1, cmp_b1, cmp_w2, w_out, ln_mem, ln_memkv, w_mem_q, w_mem_kv, w_mem_o, ln_ffn, w_up,
              w_down, ln_final):
    B, S, _ = x_prompt.shape
    DB, DS, _ = x_sample.shape
    past = page_table.shape[1] * PAGE_SIZE
    wbuf = cache_win_kv.shape[2]
    mem_len = mem_prompt.shape[1]
    q_pos_p = jnp.arange(S)
    q_pos_s = past + jnp.arange(DS)
    win_pos_s = past - wbuf + jnp.arange(wbuf + DS)
    xp, xs = x_prompt, x_sample
    p_nsa, p_win, p_mem, p_conv, p_rec = [], [], [], [], []
    s_nsa, s_win, s_conv, s_rec = [], [], [], []
    for l in range(DEPTH):
        lw = (ln_mix[l], w_in[l], dn_conv_w[l], dn_a_log[l], dn_dt_bias[l], dn_norm[l], cmp_pe[l], cmp_w1[l],
              cmp_b1[l], cmp_w2[l], w_out[l], ln_mem[l], w_mem_q[l], w_mem_o[l], ln_ffn[l], w_up[l], w_down[l])
        mem_kv = (rms_norm(mem_prompt, ln_memkv[l]) @ w_mem_kv[l]).reshape(B, mem_len, 2, MEM_HEADS, MEM_HEAD_DIM)
        xp, conv_n, rec_n, nsa_n, win_n = trunk_layer(
            xp, mem_kv, jnp.zeros((B, CONV_WIDTH - 1, DN_CONV_DIM), xp.dtype),
            jnp.zeros((B, DN_HEADS, HEAD_DIM, HEAD_DIM), jnp.float32), None, None,
            q_pos_p, q_pos_p, min(WINDOW, S), True, *lw)
        p_nsa.append(nsa_n)
        p_win.append(win_n)
        p_mem.append(mem_kv)
        p_conv.append(conv_n)
        p_rec.append(rec_n)
        kv_past = cache_nsa_kv[l][page_table].reshape(DB, past, 4, NSA_KV_HEADS, HEAD_DIM)
        xs, conv_n, rec_n, nsa_n, win_n = trunk_layer(
            xs, cache_mem_kv[l], state_dn_conv[l], state_dn_rec[l], kv_past, cache_win_kv[l],
            q_pos_s, win_pos_s, wbuf, False, *lw)
        s_nsa.append(nsa_n)
        s_win.append(win_n)
        s_conv.append(conv_n)
        s_rec.append(rec_n)
    y_prompt = rms_norm(xp, ln_final)
    y_sample = rms_norm(xs, ln_final)
    return (y_prompt, y_sample, jnp.stack(p_nsa), jnp.stack(p_win), jnp.stack(p_mem), jnp.stack(p_conv),
            jnp.stack(p_rec), jnp.stack(s_nsa), jnp.stack(s_win), jnp.stack(s_conv), jnp.stack(s_rec))
```

```python
import contextlib
import numpy as np
import concourse.bass as bass
import concourse.mybir as mybir
from concourse.bass_utils import run_bass_kernel_spmd

F32 = mybir.dt.float32
BF16 = mybir.dt.bfloat16
I32 = mybir.dt.int32
AF = mybir.ActivationFunctionType
ALU = mybir.AluOpType
AX = mybir.AxisListType

D = 1024
NT = 33
NOWN = 16
NS = 16
IN_COLS = 3368
OFF_Z, OFF_B, OFF_A, OFF_Q, OFF_KV, OFF_G = 1536, 2048, 2056, 2064, 2576, 3344
EPS = 1e-6
import os
DBG = os.environ.get('KDBG', 'abcdpsh')


class Res:
    __slots__ = ("w", "r", "dsem", "dcnt", "name")

    def __init__(self, name=""):
        self.w = {}
        self.r = {}
        self.dsem = None
        self.dcnt = 0
        self.name = name


class T:
    def __init__(self, t, res):
        self.t = t
        self.res = res

    def __getitem__(self, k):
        return self.t[k]


class KB:
    ENG = ("pe", "act", "dve", "pool", "sp")

    def __init__(self):
        self.nc = bass.Bass("TRN2", target_bir_lowering=False)
        nc = self.nc
        self.es = contextlib.ExitStack()
        self.eng = {"pe": nc.tensor, "act": nc.scalar, "dve": nc.vector, "pool": nc.gpsimd, "sp": nc.sync}
        self.sem = {}
        self.cnt = {}
        self.known = {e: {} for e in self.ENG}
        self.semobj = {}
        self.nsem = 0
        for e in self.ENG:
            self._newsem(e)
        self.finals = {}
        self.uid = 0
        self.psum = []
        self.psi = 0
        self.psn = 8

    def _alloc_sem(self):
        self.nsem += 1
        s = self.es.enter_context(self.nc.semaphore(f"s{self.nsem}"))
        key = self.nsem
        self.semobj[key] = s
        return key

    def _newsem(self, e):
        self.sem[e] = self._alloc_sem()
        self.cnt[e] = 0

    def sb(self, shape, dtype, stack=None, name=None):
        self.uid += 1
        st = stack if stack is not None else self.es
        t = st.enter_context(self.nc.sbuf_tensor(name or f"t{self.uid}", list(shape), dtype))
        return T(t, Res(name or f"t{self.uid}"))

    def init_psum(self):
        for i in range(8):
            t = self.es.enter_context(self.nc.psum_tensor(f"ps{i}", [128, 512], F32))
            self.psum.append(T(t, Res(f"ps{i}")))

    def ps(self):
        p = self.psum[self.psi % self.psn]
        self.psi += 1
        return p

    def _wait(self, e, deps):
        eng = self.eng[e]
        kn = self.known[e]
        for key, val in deps.items():
            if e == "pe" and key == self.sem["pe"]:
                continue
            if kn.get(key, 0) >= val:
                continue
            eng.wait_ge(self.semobj[key], val)
            kn[key] = val

    @staticmethod
    def _merge(d, key, val):
        if d.get(key, 0) < val:
            d[key] = val

    def _deps(self, r, w):
        deps = {}
        for x in r:
            for k, v in x.w.items():
                self._merge(deps, k, v)
        for x in w:
            for k, v in x.w.items():
                self._merge(deps, k, v)
            for k, v in x.r.items():
                self._merge(deps, k, v)
        return deps

    @staticmethod
    def _res(lst):
        return [x.res if isinstance(x, T) else x for x in lst]

    def op(self, e, fn, r=(), w=()):
        r = self._res(r)
        w = self._res(w)
        self._wait(e, self._deps(r, w))
        if self.cnt[e] >= 30000:
            self._newsem(e)
        ins = fn(self.eng[e])
        self.cnt[e] += 1
        ins.then_inc(self.semobj[self.sem[e]], 1)
        key, val = self.sem[e], self.cnt[e]
        for x in r:
            self._merge(x.r, key, val)
        for x in w:
            x.w = {key: val}
            x.r = {}
        return ins

    def dma(self, q, pairs, r=(), w=(), final=False, **kw):
        r = self._res(r)
        w = self._res(w)
        self._wait(q, self._deps(r, w))
        owner = w[0] if w else (r[0] if r else None)
        if owner is None:
            owner = Res("anon")
        if owner.dsem is None or owner.dcnt >= 30000 * 16:
            owner.dsem = self._alloc_sem()
            owner.dcnt = 0
        for (o, i) in pairs:
            self.eng[q].dma_start(out=o, in_=i, **kw).then_inc(self.semobj[owner.dsem], 16)
            owner.dcnt += 16
        key, val = owner.dsem, owner.dcnt
        for x in r:
            self._merge(x.r, key, val)
        for x in w:
            x.w = {key: val}
            x.r = {}
        if final:
            self._merge(self.finals, key, val)

    def barrier(self):
        allv = {self.sem[e]: self.cnt[e] for e in self.ENG if self.cnt[e] > 0}
        for e in self.ENG:
            self._wait(e, dict(allv))

    def finish(self):
        self._wait("sp", dict(self.finals))
        self.barrier()
        self.es.close()


def kb_gather(kb, PG, pti, cnsa, si):
    deps = kb._deps([pti.res], [PG.res])
    kb._wait("pool", deps)
    if PG.res.dsem is None:
        PG.res.dsem = kb._alloc_sem()
    for pg_ in range(16):
        c = si * 16 + pg_
        kb.nc.gpsimd.indirect_dma_start(out=PG[:, pg_, :], out_offset=None, in_=cnsa[0:128, :],
                                        in_offset=bass.IndirectOffsetOnAxis(ap=pti[:, c:c + 1], axis=0)).then_inc(kb.semobj[PG.res.dsem], 16)
        PG.res.dcnt += 16
    PG.res.w = {PG.res.dsem: PG.res.dcnt}
    PG.res.r = {}
    kb._merge(pti.res.r, PG.res.dsem, PG.res.dcnt)


def bcast(ap, shape):
    return ap.to_broadcast(list(shape))


def build():
    kb = KB()
    nc = kb.nc
    kb.init_psum()

    def din(name, shape, dt=F32):
        return nc.dram_tensor(name, list(shape), dt, kind="ExternalInput").ap()

    def dout(name, shape, dt=F32):
        return nc.dram_tensor(name, list(shape), dt, kind="ExternalOutput").ap()

    xp = din("xp", [NT * 128, D])
    xs = din("xs", [NS, D])
    memp = din("memp", [256, D])
    w_in = din("w_in", [D, IN_COLS])
    w_mem_kv = din("w_mem_kv", [D, 2048])
    lnv = din("lnv", [128, 6, 8])
    cwin = din("cwin", [NS, 512, 256])
    sconv = din("sconv", [NS, 3, 1536])
    conv_w = din("conv_w", [4, 1536])
    abrep = din("abrep", [128, 2])
    dnnorm = din("dnnorm", [128, 64])
    srec = din("srec", [128, 64, 64])

    o_nsa = dout("o_nsa", [NOWN, 128, 512])
    o_win = dout("o_win", [5, 128, 256])
    o_mem = dout("o_mem", [256, 2048])
    o_conv = dout("o_conv", [3, 1536])
    o_snsa = dout("o_snsa", [NS, 512])
    o_swin = dout("o_swin", [NS, 512, 256])
    o_sconv = dout("o_sconv", [NS, 3, 1536])
    o_srec = dout("o_srec", [128, 64, 64])
    o_y = dout("o_y", [NOWN, 128, D])
    o_prec = dout("o_prec", [8, 64, 64])
    dncin = din("dncin", [128, 1602])
    if "s" in DBG:
        cnsa = din("cnsa", [2560 * 128, 512])
        ptab = din("ptab", [1, NS * 16], I32)
        iotap = din("iotap", [128, 1])
        sconst = din("sconst", [1, 512])
    if "p" in DBG:
        w1rep = din("w1rep", [128, 2, 64, 128])
        perep = din("perep", [128, 256])
        eomin = din("eomin", [128, 2])
        b1in = din("b1in", [128, 2])
        w2in = din("w2in", [128, 2, 64])
        wq_p = din("wq_p", [D, 512])
        Ein = din("Ein", [66, NT * 128])
        mk3in = din("mk3in", [128, 3, 512])
        cmkin = din("cmkin", [NOWN, 66, 512])
        addcin = din("addcin", [NOWN, 128, 2, 66])
    o_ys = dout("o_ys", [NS, D])
    w_out = din("w_out", [D, D])
    w_mem_q = din("w_mem_q", [D, D])
    w_mem_o = din("w_mem_o", [D, D])
    w_up = din("w_up", [D, 4096])
    w_down = din("w_down", [4096, D])
    ln_fin = din("ln_fin", [1, D])
    cmem = din("cmem", [NS, 256, 2, 1024])
    if "m" in DBG or "n" in DBG:
        dbg_mixT = din("dbg_mixT", [128, 8, NOWN * 128 + NS])

    ident_bf = kb.sb([128, 128], BF16, name="ident_bf")
    ident_f = kb.sb([128, 128], F32, name="ident_f")
    kb.op("pool", lambda e: e.memset(ident_f[:], 0.0), w=[ident_f])
    kb.op("pool", lambda e: e.affine_select(out=ident_f[:], in_=ident_f[:], pattern=[[1, 128]],
                                            compare_op=ALU.not_equal, fill=1.0, base=0, channel_multiplier=-1),
          r=[ident_f], w=[ident_f])
    kb.op("dve", lambda e: e.tensor_copy(ident_bf[:], ident_f[:]), r=[ident_f], w=[ident_bf])
    epsc = kb.sb([128, 1], F32, name="epsc")
    kb.op("pool", lambda e: e.memset(epsc[:], EPS), w=[epsc])
    onec = kb.sb([128, 1], F32, name="onec")
    kb.op("pool", lambda e: e.memset(onec[:], 1.0), w=[onec])
    lns = kb.sb([128, 6, 8], F32, name="lns")
    kb.dma("sp", [(lns[:], lnv[:, :, :])], w=[lns])

    def norm_transpose(src_ap, rows, stage, dst_ap, dst_res, ln_idx, gq="sp", sb_src=None):
        xst, ss, rstd, xbf = stage
        if sb_src is None:
            kb.dma(gq, [(xst[0:rows, :], src_ap)], w=[xst])
        else:
            xst = sb_src
        xin = src_ap if sb_src is not None else xst[0:rows, :]
        kb.op("act", lambda e: e.activation(out=xbf[0:rows, :], in_=xin, func=AF.Square,
                                            accum_out=ss[0:rows, 0:1]), r=[xst], w=[xbf, ss])
        kb.op("act", lambda e: e.activation(out=rstd[0:rows, :], in_=ss[0:rows, :], func=AF.Sqrt, scale=1.0 / D,
                                            bias=epsc[0:rows, 0:1]), r=[ss, epsc], w=[rstd])
        kb.op("dve", lambda e: e.reciprocal(rstd[0:rows, :], rstd[0:rows, :]), r=[rstd], w=[rstd])
        kb.op("act", lambda e: e.activation(out=xbf[0:rows, :], in_=xin, func=AF.Copy,
                                            scale=rstd[0:rows, 0:1]), r=[xst, rstd], w=[xbf])
        pt = kb.ps()
        ptv = pt[:].bitcast(BF16)
        for kc in range(8):
            kb.op("pe", lambda e, kc=kc: e.transpose(ptv[:, kc * 128:kc * 128 + rows],
                                                     xbf[0:rows, kc * 128:(kc + 1) * 128], ident_bf[0:rows, 0:rows]),
                  r=[xbf, ident_bf], w=[pt])
        pv3 = ptv.rearrange("p (k t) -> p k t", k=8)[:, :, 0:rows]
        kb.op("dve", lambda e: e.tensor_tensor(out=dst_ap, in0=pv3, in1=bcast(lns[:, ln_idx, :].unsqueeze(2), [128, 8, rows]),
                                               op=ALU.mult), r=[pt, lns], w=[dst_res])

    def rr_a(blk):
        return [mix_r[4 * blk + j] for j in range(4)]

    def load_w(dram_ap, ncols, stack, name):
        wt = kb.sb([128, 8, ncols], BF16, stack=stack, name=name)
        src = dram_ap.rearrange("(k p) n -> p k n", p=128)
        kb.dma("pool", [(wt[:, k, :], src[:, k, :]) for k in range(8)], w=[wt])
        return wt

    TOK = NOWN * 128 + NS
    tiles = [(i * 128, 128) for i in range(NOWN)] + [(NOWN * 128, NS)]
    mixT = kb.sb([128, 8, TOK], BF16, name="mixT")
    mix_r = [Res(f"mix{i}") for i in range(17)]
    front = contextlib.ExitStack()
    scr_sproj = nc.dram_tensor("scr_sproj", [NS, IN_COLS], F32).ap()
    r_sps = Res("scr_sproj")
    xnT = kb.sb([128, 8, NT * 128], BF16, name="xnT", stack=front)
    xnT_res = [Res(f"xnT{t}") for t in range(NT)]
    xsT = kb.sb([128, 8, NS], BF16, name="xsT", stack=front)

    with contextlib.ExitStack() as st:
        stages = []
        for i in range(3):
            stages.append((kb.sb([128, D], F32, stack=st), kb.sb([128, 1], F32, stack=st),
                           kb.sb([128, 1], F32, stack=st), kb.sb([128, D], BF16, stack=st)))
        for t in range(NT):
            norm_transpose(xp[t * 128:(t + 1) * 128, :], 128, stages[t % 3],
                           xnT[:, :, t * 128:(t + 1) * 128], xnT_res[t], 0)
        norm_transpose(xs[:, :], NS, stages[NT % 3], xsT[:, :, :], xsT.res, 0)
        kb.barrier()

    with contextlib.ExitStack() as st:
        sproj = kb.sb([NS, IN_COLS], F32, name="sproj", stack=st)
        co = kb.sb([128, 1536], F32, stack=st)
        wsl = [kb.sb([128, 8, 512], BF16, stack=st) for _ in range(2)]
        wsrc = w_in.rearrange("(k p) n -> p k n", p=128)
        for cb in range(7):
            c0 = cb * 512
            cw = min(512, IN_COLS - c0)
            wt = wsl[cb % 2]
            kb.dma("pool", [(wt[:, k, 0:cw], wsrc[:, k, c0:c0 + cw]) for k in range(8)], w=[wt])
            pa = kb.ps()
            for kc in range(8):
                kb.op("pe", lambda e, kc=kc: e.matmul(pa[0:NS, 0:cw], xsT[:, kc, :], wt[:, kc, 0:cw],
                                                      start=(kc == 0), stop=(kc == 7)), r=[xsT, wt], w=[pa])
            kb.op("dve", lambda e: e.tensor_copy(sproj[:, c0:c0 + cw], pa[0:NS, 0:cw]), r=[pa], w=[sproj])
            if cb < 3:
                pb = kb.ps()
                for kc in range(8):
                    kb.op("pe", lambda e, kc=kc: e.matmul(pb[:, :], xnT[:, kc, (NT - 1) * 128:NT * 128], wt[:, kc, :],
                                                          start=(kc == 0), stop=(kc == 7)), r=[xnT_res[NT - 1], wt], w=[pb])
                kb.op("act", lambda e: e.activation(out=co[:, c0:c0 + 512], in_=pb[:, :], func=AF.Copy), r=[pb], w=[co])
        kb.dma("sp", [(o_conv[:, :], co[125:128, :])], r=[co], final=True)
        kb.dma("sp", [(o_sconv[:, 2, :], sproj[:, 0:1536])], r=[sproj], final=True)
        kb.dma("sp", [(scr_sproj[:, :], sproj[:, :])], r=[sproj], w=[r_sps])
        kb.dma("pool", [(o_sconv[:, 0:2, :], sconv[:, 1:3, :])], final=True)
        kb.barrier()

    if "d" in DBG:
      with contextlib.ExitStack() as st:
        wdn = load_w(w_in[:, 0:1536], 1536, st, "wdn")
        wz = load_w(w_in[:, OFF_Z:OFF_Z + 512], 512, st, "wz")
        wba = load_w(w_in[:, OFF_B:OFF_B + 16], 16, st, "wba")
        NDC = 5 * 128 + 7 * 128 + 48 + 16 + 2
        dnc = kb.sb([128, NDC], F32, stack=st, name="dnc")
        kb.dma("sp", [(dnc[:], dncin[:, :])], w=[dnc])
        UT, NMI, NMT, ST01, BLK = (dnc[:, i * 128:(i + 1) * 128] for i in range(5))
        MO = dnc[:, 640:640 + 896].rearrange("p (l q) -> p l q", l=7)
        CW = dnc[:, 1536:1584].rearrange("p (f j) -> p f j", j=4)
        ALOG, DTB = dnc[:, 1584:1592], dnc[:, 1592:1600]
        EOM = (dnc[:, 1600:1601], dnc[:, 1601:1602])
        gnd = kb.sb([128, 64], F32, stack=st, name="gnd")
        kb.dma("sp", [(gnd[:], dnnorm[:, :])], w=[gnd])
        onesf = kb.sb([128, 128], F32, stack=st, name="onesf_d")
        kb.op("pool", lambda e: e.memset(onesf[:], 1.0), w=[onesf])
        negea = kb.sb([128, 8], F32, stack=st, name="negea")
        kb.op("act", lambda e: e.activation(out=negea[:], in_=ALOG, func=AF.Exp), r=[dnc], w=[negea])
        kb.op("dve", lambda e: e.tensor_scalar(out=negea[:], in0=negea[:], scalar1=-1.0, scalar2=None, op0=ALU.mult), r=[negea], w=[negea])
        eps64 = kb.sb([128, 1], F32, stack=st, name="eps64")
        kb.op("pool", lambda e: e.memset(eps64[:], 64e-6), w=[eps64])
        pre = kb.sb([128, 12, 131], BF16, stack=st, name="pre")
        kb.op("pool", lambda e: e.memset(pre[:], 0.0), w=[pre])
        S = kb.sb([128, 4, 64], F32, stack=st, name="S")
        kb.op("pool", lambda e: e.memset(S[:], 0.0), w=[S])
        F = lambda shape, name: kb.sb(shape, F32, stack=st, name=name)
        cT, tmpc = F([128, 12, 128], "cT"), F([128, 12, 128], "tmpc")
        rs, qkT = F([128, 8, 128], "rs"), F([128, 8, 128], "qkT")
        k_tok, v_tok = F([128, 8, 64], "k_tok"), F([128, 8, 64], "v_tok")
        vkb, kd = kb.sb([128, 8, 128], BF16 if "h" in DBG else F32, stack=st, name="vkb"), F([128, 8, 64], "kd")
        km = (F([128, 4, 128], "km_e"), F([128, 4, 128], "km_o"))
        Sm = (F([128, 2, 64], "Sm_e"), F([128, 2, 64], "Sm_o"))
        w_sb = F([128, 4, 64], "w_sb")
        sm = F([128, 80], "smd")
        BETA, G, GC, EG, EGL, EKD, BG, XX = (sm[:, i * 8:(i + 1) * 8] for i in range(8))
        eglS = sm[:, 64:68]
        G4 = lambda nm: F([128, 4, 128], nm)
        B4 = lambda nm: kb.sb([128, 4, 128], BF16, stack=st, name=nm)
        Dm, DTm, AqkT = B4("Dm"), B4("DTm"), G4("AqkT")
        if "h" not in DBG:
            IX, IX2 = G4("IX"), G4("IX2")
        A, AT = T(rs[:, 0:4, :], rs.res), T(rs[:, 4:8, :], rs.res)
        if "h" in DBG:
            Ta, Tb, Ua, Ub, Nn, NTn = (kb.sb([128, 4, 128], BF16, stack=st, name=n) for n in ("Ta", "Tb", "Ua", "Ub", "Nn", "NTn"))
            IX, IX2 = (kb.sb([128, 4, 128], BF16, stack=st, name=n) for n in ("IXb", "IX2b"))
            diag, t1 = T(cT[:, 0:4, :], cT.res), T(cT[:, 4:8, :], cT.res)
        else:
            diag, t1 = IX, IX2
            Ta, Tb, Ua = (T(cT[:, 4 * i:4 * i + 4, :], cT.res) for i in range(3))
            Ub, Nn, NTn = (T(tmpc[:, 4 * i:4 * i + 4, :], tmpc.res) for i in range(3))
        u_sb, wT, vnew = F([128, 4, 64], "u_sb"), F([128, 2, 128], "wT"), F([128, 4, 64], "vnew")
        o_t = F([128, 8, 64], "o_t")
        obf = T(Dm[:].rearrange("p a b -> p (a b)"), Dm.res)
        HB = [(diag, t1, Dm, DTm, A, AT, AqkT, Ta, Tb, Ua, Ub, Nn, NTn, IX, IX2, u_sb, w_sb, wT, vnew, Sm)]
        HB.append(tuple([T(tmpc[:, 0:4, :], tmpc.res), T(tmpc[:, 4:8, :], tmpc.res), B4("Dm_hb"), B4("DTm_hb"), T(tmpc[:, 8:12, :], tmpc.res), T(cT[:, 8:12, :], cT.res), G4("AqkT_hb")]
                        + [kb.sb([128, 4, 128], BF16, stack=st, name=n + "_hb") for n in ("Ta", "Tb", "Ua", "Ub", "Nn", "NTn", "IX", "IX2")]
                        + [T(k_tok[:, 0:4, :], k_tok.res), T(k_tok[:, 4:8, :], k_tok.res), T(v_tok[:, 4:8, :].rearrange("p (a b) d -> p a (b d)", a=2), v_tok.res),
                           T(v_tok[:, 0:4, :], v_tok.res),
                           (F([128, 2, 64], "Sm_e1"), F([128, 2, 64], "Sm_o1"))]))
        o2 = T(vkb[:, :, 0:64], vkb.res) if "h" not in DBG else T(tmpc[:, 0:4, :].rearrange("p a (b c) -> p (a b) c", c=64), tmpc.res)
        zs = T(kd[:].rearrange("p h d -> p (h d)"), kd.res)
        ro = F([128, 8], "ro")
        idb4 = bcast(ident_f[:, :].unsqueeze(1), [128, 4, 128])
        b4 = lambda ap2: bcast(ap2.unsqueeze(1), [128, 4, 128])
        c4 = lambda ap2: bcast(ap2.unsqueeze(2), [128, 4, 128])
        c64 = lambda ap2, n: bcast(ap2.unsqueeze(2), [128, n, 64])
        V = lambda e: ("dve", e)
        def tt(eng, out, in0, in1, op, r, w):
            kb.op(eng, lambda e: e.tensor_tensor(out=out, in0=in0, in1=in1, op=op), r=r, w=w)
        def cp(out, in_, r, w, func=AF.Copy, **kw):
            kb.op("act", lambda e: e.activation(out=out, in_=in_, func=func, **kw), r=r, w=w)
        for tau in range(int(os.environ.get('KNT', NT))):
            own = tau % 2 == 1
            xt = lambda kc: xnT[:, kc, tau * 128:(tau + 1) * 128]
            xr_ = xnT_res[tau]
            for b3 in range(3):
                pa = kb.ps()
                for f4 in range(4):
                    ft = b3 * 4 + f4
                    for kc in range(8):
                        kb.op("pe", lambda e, kc=kc: e.matmul(pa[:, f4 * 128:(f4 + 1) * 128], wdn[:, kc, ft * 128:(ft + 1) * 128], xt(kc),
                                                              start=(kc == 0), stop=(kc == 7)), r=[xr_, wdn], w=[pa])
                cp(pre[:, b3 * 4:(b3 + 1) * 4, 3:131], pa[:, :].rearrange("p (f t) -> p f t", f=4), [pa], [pre])
            cb = lambda j: bcast(CW[:, :, j].unsqueeze(2), [128, 12, 128])
            tt("dve", cT[:], pre[:, :, 0:128], cb(0), ALU.mult, [pre, dnc], [cT])
            for j in range(1, 4):
                tt("pool", tmpc[:], pre[:, :, j:j + 128], cb(j), ALU.mult, [pre, dnc], [tmpc])
                tt("dve", cT[:], cT[:], tmpc[:], ALU.add, [cT, tmpc], [cT])
            cp(cT[:], cT[:], [cT], [cT], func=AF.Silu)
            kb.op("dve", lambda e: e.tensor_copy(pre[:, :, 0:3], pre[:, :, 128:131]), r=[pre], w=[pre])
            tt("pool", tmpc[:, 0:8, :], cT[:, 0:8, :], cT[:, 0:8, :], ALU.mult, [cT], [tmpc])
            for half in range(2):
                pa = kb.ps()
                kb.op("pe", lambda e: e.matmul(pa[:, :], BLK, tmpc[:, 4 * half:4 * half + 4, :].rearrange("p f t -> p (f t)"), start=True, stop=True),
                      r=[dnc, tmpc], w=[pa])
                cp(rs[:, 4 * half:4 * half + 4, :].rearrange("p f t -> p (f t)"), pa[:, :], [pa, eps64, epsc], [rs], func=AF.Sqrt,
                   scale=(64.0 if half == 0 else 1.0), bias=(eps64[:, 0:1] if half == 0 else epsc[:, 0:1]))
            kb.op("dve", lambda e: e.reciprocal(rs[:], rs[:]), r=[rs], w=[rs])
            tt("dve", qkT[:], cT[:, 0:8, :], rs[:], ALU.mult, [cT, rs], [qkT])
            for (src, f0, dst) in ((qkT, 4, k_tok), (cT, 8, v_tok)):
                pa = kb.ps()
                for f in range(4):
                    kb.op("pe", lambda e, f=f: e.transpose(pa[:, f * 128:(f + 1) * 128], src[:, f0 + f, :], ident_f[:, :]), r=[src, ident_f], w=[pa])
                cp(dst[:].rearrange("p h d -> p (h d)"), pa[:, :], [pa], [dst])
            pb = kb.ps()
            for kc in range(8):
                kb.op("pe", lambda e, kc=kc: e.matmul(pb[:, 0:16], xt(kc), wba[:, kc, :], start=(kc == 0), stop=(kc == 7)), r=[xr_, wba], w=[pb])
            cp(BETA, pb[:, 0:8], [pb], [sm], func=AF.Sigmoid)
            tt("dve", XX, pb[:, 8:16], DTB, ALU.add, [pb, dnc], [sm])
            cp(XX, XX, [sm], [sm], func=AF.Exp)
            cp(XX, XX, [sm, onec], [sm], func=AF.Ln, bias=onec[:, 0:1])
            tt("dve", G, XX, negea[:], ALU.mult, [sm, negea], [sm])
            pg = kb.ps()
            kb.op("pe", lambda e: e.matmul(pg[:, 0:8], UT, G, start=True, stop=True), r=[dnc, sm], w=[pg])
            kb.op("pe", lambda e: e.matmul(pg[:, 8:16], onesf[:, :], G, start=True, stop=True), r=[onesf, sm], w=[pg])
            kb.op("dve", lambda e: e.tensor_copy(GC, pg[:, 0:8]), r=[pg], w=[sm])
            cp(EG, pg[:, 0:8], [pg], [sm], func=AF.Exp)
            cp(EGL, pg[:, 8:16], [pg], [sm], func=AF.Exp)
            tt("dve", XX, pg[:, 8:16], GC, ALU.subtract, [pg, sm], [sm])
            cp(EKD, XX, [sm], [sm], func=AF.Exp)
            tt("dve", BG, BETA, EG, ALU.mult, [sm], [sm])
            kb.op("dve", lambda e: e.tensor_copy(eglS[0:64, :], sm[0:64, 32:40].rearrange("p (a b) -> p a b", b=2)[:, :, 0]), r=[sm], w=[sm])
            kb.op("dve", lambda e: e.tensor_copy(eglS[64:128, :], sm[64:128, 32:40].rearrange("p (a b) -> p a b", b=2)[:, :, 1]), r=[sm], w=[sm])
            tt("dve", vkb[:, :, 0:64], v_tok[:], c64(BETA, 8), ALU.mult, [v_tok, sm], [vkb])
            tt("dve", vkb[:, :, 64:128], k_tok[:], c64(BG, 8), ALU.mult, [k_tok, sm], [vkb])
            for p_ in range(2):
                kb.op("dve", lambda e: e.tensor_scalar(out=km[p_][:], in0=qkT[:, 4:8, :], scalar1=EOM[p_], scalar2=None, op0=ALU.mult), r=[qkT, dnc], w=[km[p_]])
            tt("pool", kd[:], k_tok[:], c64(EKD, 8), ALU.mult, [k_tok, sm], [kd])
            def hg_body(hg, B):
                (diag, t1, Dm, DTm, A, AT, AqkT, Ta, Tb, Ua, Ub, Nn, NTn, IX, IX2, u_sb, w_sb, wT, vnew, Sm) = B
                hs = list(range(4 * hg, 4 * hg + 4))
                gch = sm[:, 16 + 4 * hg:16 + 4 * hg + 4]
                KST = int(os.environ.get('KST', 9))
                tt("dve", diag[:], idb4, c4(gch), ALU.mult, [ident_f, sm], [diag])
                pG = kb.ps()
                kb.op("pe", lambda e: e.matmul(pG[:, :], onesf[:, :], diag[:].rearrange("p h t -> p (h t)"), start=True, stop=True), r=[onesf, diag], w=[pG])
                pG3 = pG[:, :].rearrange("p (h t) -> p h t", h=4)
                tt("dve", t1[:], b4(NMI), pG3, ALU.subtract, [dnc, pG], [t1])
                tt("dve", t1[:], t1[:], c4(gch), ALU.add, [t1, sm], [t1])
                cp(Dm[:], t1[:], [t1], [Dm], func=AF.Exp)
                tt("dve", t1[:], pG3, b4(NMT), ALU.add, [dnc, pG], [t1])
                tt("dve", t1[:], t1[:], c4(gch), ALU.subtract, [t1, sm], [t1])
                cp(DTm[:], t1[:], [t1], [DTm], func=AF.Exp)
                tt("pool", Dm[:], Dm[:], b4(ST01), ALU.mult, [Dm, dnc], [Dm])
                yield
                if KST < 2:
                    return
                pGr, pQK = kb.ps(), kb.ps()
                for i, h in enumerate(hs):
                    kmh = km[h % 2]
                    kb.op("pe", lambda e: e.matmul(pGr[:, i * 128:(i + 1) * 128], kmh[:, h // 2, :], qkT[:, 4 + h // 2, :], start=True, stop=True), r=[kmh, qkT], w=[pGr])
                    kb.op("pe", lambda e: e.matmul(pQK[:, i * 128:(i + 1) * 128], kmh[:, h // 2, :], qkT[:, h // 2, :], start=True, stop=True), r=[kmh, qkT], w=[pQK])
                tt("dve", A[:], pGr[:, :].rearrange("p (h t) -> p h t", h=4), Dm[:], ALU.mult, [pGr, Dm], [A])
                tt("dve", A[:], A[:], c4(sm[:, 4 * hg:4 * hg + 4]), ALU.mult, [A, sm], [A])
                tt("dve", AqkT[:], pQK[:, :].rearrange("p (h t) -> p h t", h=4), DTm[:], ALU.mult, [pQK, DTm], [AqkT])
                yield
                if KST < 3:
                    return
                pa = kb.ps()
                for i in range(4):
                    kb.op("pe", lambda e, i=i: e.transpose(pa[:, i * 128:(i + 1) * 128], A[:, i, :], ident_f[:, :]), r=[A, ident_f], w=[pa])
                cp(AT[:].rearrange("p h t -> p (h t)"), pa[:, :], [pa], [AT])
                T_, U_, Tn, Un = Ta, Ua, Tb, Ub
                tt("pool", Nn[:], A[:], b4(MO[:, 0, :]), ALU.mult, [A, dnc], [Nn])
                tt("dve", T_[:], idb4, Nn[:], ALU.subtract, [ident_f, Nn], [T_])
                tt("pool", NTn[:], AT[:], b4(MO[:, 0, :]), ALU.mult, [AT, dnc], [NTn])
                tt("dve", U_[:], idb4, NTn[:], ALU.subtract, [ident_f, NTn], [U_])
                yield
                for lv in range(1, int(os.environ.get('KLV', 7))):
                    last = lv == 6
                    tt("pool", Nn[:], A[:], b4(MO[:, lv, :]), ALU.mult, [A, dnc], [Nn])
                    pX2 = kb.ps()
                    for i in range(4):
                        kb.op("pe", lambda e, i=i: e.matmul(pX2[:, i * 128:(i + 1) * 128], Nn[:, i, :], U_[:, i, :], start=True, stop=True), r=[Nn, U_], w=[pX2])
                    tt("dve", IX2[:], idb4, pX2[:, :].rearrange("p (h t) -> p h t", h=4), ALU.subtract, [ident_f, pX2], [IX2])
                    pU = kb.ps()
                    for i in range(4):
                        kb.op("pe", lambda e, i=i: e.matmul(pU[:, i * 128:(i + 1) * 128], T_[:, i, :], IX2[:, i, :], start=True, stop=True), r=[T_, IX2], w=[pU])
                    if not last:
                        tt("pool", NTn[:], AT[:], b4(MO[:, lv, :]), ALU.mult, [AT, dnc], [NTn])
                        pX = kb.ps()
                        for i in range(4):
                            kb.op("pe", lambda e, i=i: e.matmul(pX[:, i * 128:(i + 1) * 128], NTn[:, i, :], T_[:, i, :], start=True, stop=True), r=[NTn, T_], w=[pX])
                        tt("dve", IX[:], idb4, pX[:, :].rearrange("p (h t) -> p h t", h=4), ALU.subtract, [ident_f, pX], [IX])
                        pT_ = kb.ps()
                        for i in range(4):
                            kb.op("pe", lambda e, i=i: e.matmul(pT_[:, i * 128:(i + 1) * 128], U_[:, i, :], IX[:, i, :], start=True, stop=True), r=[U_, IX], w=[pT_])
                        cp(Tn[:].rearrange("p h t -> p (h t)"), pT_[:, :], [pT_], [Tn])
                    cp(Un[:].rearrange("p h t -> p (h t)"), pU[:, :], [pU], [Un])
                    T_, Tn = Tn, T_
                    U_, Un = Un, U_
                    yield
                if KST < 4:
                    return
                pu = kb.ps()
                for i, h in enumerate(hs):
                    kb.op("pe", lambda e: e.matmul(pu[:, i * 128:(i + 1) * 128], U_[:, i, :], vkb[:, h, :], start=True, stop=True), r=[U_, vkb], w=[pu])
                pu3 = pu[:, :].rearrange("p (h c) -> p h c", h=4)
                cp(u_sb[:], pu3[:, :, 0:64], [pu], [u_sb])
                cp(w_sb[:], pu3[:, :, 64:128], [pu], [w_sb])
                pwt = kb.ps()
                for a in range(2):
                    kb.op("pe", lambda e, a=a: e.transpose(pwt[:, a * 128:(a + 1) * 128], w_sb[:, 2 * a:2 * a + 2, :].rearrange("p h d -> p (h d)"), ident_f[:, :]),
                          r=[w_sb, ident_f], w=[pwt])
                cp(wT[:].rearrange("p a t -> p (a t)"), pwt[:, 0:256], [pwt], [wT])
                yield
                for p_ in range(2):
                    kb.op("dve", lambda e: e.tensor_scalar(out=Sm[p_][:], in0=S[:, 2 * hg:2 * hg + 2, :], scalar1=EOM[p_], scalar2=None, op0=ALU.mult), r=[S, dnc], w=[Sm[p_]])
                if KST < 5:
                    return
                pws = kb.ps()
                for i, h in enumerate(hs):
                    kb.op("pe", lambda e: e.matmul(pws[:, i * 64:(i + 1) * 64], wT[:, i // 2, :], Sm[h % 2][:, i // 2, :], start=True, stop=True), r=[wT, Sm[h % 2]], w=[pws])
                tt("dve", vnew[:], u_sb[:], pws[:, 0:256].rearrange("p (h d) -> p h d", h=4), ALU.subtract, [u_sb, pws], [vnew])
                yield
                if own:
                    pqs, pav = kb.ps(), kb.ps()
                    for i, h in enumerate(hs):
                        kb.op("pe", lambda e: e.matmul(pqs[:, i * 64:(i + 1) * 64], qkT[:, h // 2, :], Sm[h % 2][:, i // 2, :], start=True, stop=True), r=[qkT, Sm[h % 2]], w=[pqs])
                        kb.op("pe", lambda e: e.matmul(pav[:, i * 64:(i + 1) * 64], AqkT[:, i, :], vnew[:, i, :], start=True, stop=True), r=[AqkT, vnew], w=[pav])
                    osl = o_t[:, 4 * hg:4 * hg + 4, :]
                    tt("dve", osl, pqs[:, 0:256].rearrange("p (h d) -> p h d", h=4), c64(sm[:, 24 + 4 * hg:24 + 4 * hg + 4], 4), ALU.mult, [pqs, sm], [o_t])
                    tt("dve", osl, osl, pav[:, 0:256].rearrange("p (h d) -> p h d", h=4), ALU.add, [o_t, pav], [o_t])
                if KST < 6:
                    return
                pS = kb.ps()
                for a in range(2):
                    kb.op("pe", lambda e, a=a: e.matmul(pS[:, a * 128:(a + 1) * 128], kd[:, 4 * hg + 2 * a:4 * hg + 2 * a + 2, :].rearrange("p h d -> p (h d)"),
                                                        vnew[:, 2 * a:2 * a + 2, :].rearrange("p h d -> p (h d)"), start=True, stop=True), r=[kd, vnew], w=[pS])
                Ssl = S[:, 2 * hg:2 * hg + 2, :]
                tt("dve", Ssl, Ssl, bcast(eglS[:, 2 * hg:2 * hg + 2].unsqueeze(2), [128, 2, 64]), ALU.mult, [S, sm], [S])
                for a in range(2):
                    tt("dve", S[0:64, 2 * hg + a, :], S[0:64, 2 * hg + a, :], pS[0:64, a * 128:a * 128 + 64], ALU.add, [S, pS], [S])
                    tt("dve", S[64:128, 2 * hg + a, :], S[64:128, 2 * hg + a, :], pS[64:128, a * 128 + 64:a * 128 + 128], ALU.add, [S, pS], [S])
                yield
            gens = [hg_body(hg_, HB[hg_]) for hg_ in range(2)]
            while gens:
                for g_ in list(gens):
                    try:
                        next(g_)
                    except StopIteration:
                        gens.remove(g_)
            if own:
                pz = kb.ps()
                for kc in range(8):
                    kb.op("pe", lambda e, kc=kc: e.matmul(pz[:, :], xt(kc), wz[:, kc, :], start=(kc == 0), stop=(kc == 7)), r=[xr_, wz], w=[pz])
                cp(zs[:], pz[:, :], [pz], [zs], func=AF.Silu)
                tt("pool", o2[:], o_t[:], o_t[:], ALU.mult, [o_t], [o2])
                kb.op("dve", lambda e: e.tensor_reduce(out=ro[:], in_=o2[:], axis=AX.X, op=ALU.add), r=[o2], w=[ro])
                cp(ro[:], ro[:], [ro, epsc], [ro], func=AF.Sqrt, scale=1.0 / 64, bias=epsc[:, 0:1])
                kb.op("dve", lambda e: e.reciprocal(ro[:], ro[:]), r=[ro], w=[ro])
                tt("dve", o_t[:], o_t[:], c64(ro[:, :], 8), ALU.mult, [o_t, ro], [o_t])
                tt("dve", o_t[:], o_t[:], bcast(gnd[:, :].unsqueeze(1), [128, 8, 64]), ALU.mult, [o_t, gnd], [o_t])
                tt("dve", obf[:], o_t[:].rearrange("p h d -> p (h d)"), zs[:], ALU.mult, [o_t, zs], [obf])
                pt = kb.ps()
                ptv = pt[:].bitcast(BF16)
                for f in range(4):
                    kb.op("pe", lambda e, f=f: e.transpose(ptv[:, f * 128:(f + 1) * 128], obf[:, f * 128:(f + 1) * 128], ident_bf[:, :]), r=[obf, ident_bf], w=[pt])
                i_own = tau // 2
                kb.op("dve", lambda e: e.tensor_copy(mixT[:, 0:4, i_own * 128:(i_own + 1) * 128], ptv[:, 0:512].rearrange("p (f t) -> p f t", f=4)), r=[pt], w=[mix_r[i_own]])
        for par in range(2):
            kb.dma("sp", [(o_prec.rearrange("(a two) k v -> two k a v", two=2)[par], S[par * 64:(par + 1) * 64, :, :])], r=[S], final=True)
        kb.barrier()

    if "p" in DBG:
        kTs = kb.sb([128, NT * 128], BF16, name="kTs", stack=front)
        kTw = kb.sb([128, NT * 128], BF16, name="kTw", stack=front)
        Vs = kb.sb([128, NT, 128], BF16, name="Vs", stack=front)
        Vw = kb.sb([128, NT, 128], BF16, name="Vw", stack=front)
        kcT = kb.sb([128, 66], BF16, name="kcT", stack=front)
        vc = kb.sb([66, 128], BF16, name="vc", stack=front)
        eom = kb.sb([128, 2], F32, stack=front, name="eom")
    with contextlib.ExitStack() as st:
        if "p" in DBG:
            CK = (kb.sb([128, NT, 256], BF16, stack=st, name="CKe"), kb.sb([128, NT, 256], BF16, stack=st, name="CKo"))
            b1t = kb.sb([128, 2], F32, stack=st, name="b1t")
            kb.dma("sp", [(eom[:], eomin[:, :]), (b1t[:], b1in[:, :])], w=[eom, b1t])
            w2pad = kb.sb([128, 2, 2, 128], BF16, stack=st, name="w2pad")
            kb.op("pool", lambda e: e.memset(w2pad[:], 0.0), w=[w2pad])
        sti = contextlib.ExitStack()
        wkv = load_w(w_in[:, OFF_KV:OFF_KV + 768], 768, sti, "wkv")
        ost = [kb.sb([128, 768], F32, stack=sti) for _ in range(2)]
        if "p" in DBG:
            pe_sb = kb.sb([128, 256], F32, stack=sti, name="pe_sb")
            ckt = kb.sb([128, 256], F32, stack=sti, name="ckt")
            kb.dma("sp", [(pe_sb[:], perep[:, :])], w=[pe_sb])
            w2sb = kb.sb([128, 2, 64], F32, stack=sti, name="w2sb")
            kb.dma("sp", [(w2sb[:], w2in[:, :, :])], w=[w2sb])
            for k in range(2):
                for g in range(2):
                    kb.op("dve", lambda e: e.tensor_copy(w2pad[:, k, g, g * 64:(g + 1) * 64], w2sb[:, k, :]), r=[w2sb], w=[w2pad])
        for t in range(NT):
            pa, pb = kb.ps(), kb.ps()
            for kc in range(8):
                kb.op("pe", lambda e, kc=kc: e.matmul(pa[:, 0:384], xnT[:, kc, t * 128:(t + 1) * 128], wkv[:, kc, 0:384],
                                                      start=(kc == 0), stop=(kc == 7)), r=[xnT_res[t], wkv], w=[pa])
            for kc in range(8):
                kb.op("pe", lambda e, kc=kc: e.matmul(pb[:, 0:384], xnT[:, kc, t * 128:(t + 1) * 128], wkv[:, kc, 384:768],
                                                      start=(kc == 0), stop=(kc == 7)), r=[xnT_res[t], wkv], w=[pb])
            o = ost[t % 2]
            kb.op("act", lambda e: e.activation(out=o[:, 0:384], in_=pa[:, 0:384], func=AF.Copy), r=[pa], w=[o])
            kb.op("dve", lambda e: e.tensor_copy(o[:, 384:768], pb[:, 0:384]), r=[pb], w=[o])
            if t % 2 == 1:
                kb.dma("sp", [(o_nsa[t // 2, :, :], o[:, 0:512])], r=[o], final=True)
            if t >= NT - 5:
                kb.dma("sp", [(o_win[t - (NT - 5), :, :], o[:, 512:768])], r=[o], final=True)
            if "p" in DBG:
                kb.op("pool", lambda e: e.tensor_copy(Vs[:, t, :], o[:, 384:512]), r=[o], w=[Vs])
                kb.op("pool", lambda e: e.tensor_copy(Vw[:, t, :], o[:, 640:768]), r=[o], w=[Vw])
                kb.op("dve", lambda e: e.tensor_tensor(out=ckt[:], in0=o[:, 0:256], in1=pe_sb[:], op=ALU.add), r=[o, pe_sb], w=[ckt])
                for p_ in range(2):
                    kb.op("dve", lambda e: e.tensor_scalar(out=CK[p_][:, t, :], in0=ckt[:], scalar1=eom[:, p_:p_ + 1], scalar2=None, op0=ALU.mult), r=[ckt, eom], w=[CK[p_]])
                for (c0, dst) in ((256, kTs), (512, kTw)):
                    pc = kb.ps()
                    for kc in range(8):
                        kb.op("pe", lambda e, kc=kc: e.matmul(pc[:, 0:128], wkv[:, kc, c0:c0 + 128], xnT[:, kc, t * 128:(t + 1) * 128],
                                                              start=(kc == 0), stop=(kc == 7)), r=[xnT_res[t], wkv], w=[pc])
                    kb.op("act", lambda e: e.activation(out=dst[:, t * 128:(t + 1) * 128], in_=pc[:, 0:128], func=AF.Copy), r=[pc], w=[dst])
        pa, pb = kb.ps(), kb.ps()
        for kc in range(8):
            kb.op("pe", lambda e, kc=kc: e.matmul(pa[0:NS, 0:384], xsT[:, kc, :], wkv[:, kc, 0:384],
                                                  start=(kc == 0), stop=(kc == 7)), r=[xsT, wkv], w=[pa])
        for kc in range(8):
            kb.op("pe", lambda e, kc=kc: e.matmul(pb[0:NS, 0:384], xsT[:, kc, :], wkv[:, kc, 384:768],
                                                  start=(kc == 0), stop=(kc == 7)), r=[xsT, wkv], w=[pb])
        so = kb.sb([NS, 768], F32, stack=sti)
        kb.op("act", lambda e: e.activation(out=so[:, 0:384], in_=pa[0:NS, 0:384], func=AF.Copy), r=[pa], w=[so])
        kb.op("dve", lambda e: e.tensor_copy(so[:, 384:768], pb[0:NS, 0:384]), r=[pb], w=[so])
        kb.dma("sp", [(o_snsa[:, :], so[:, 0:512])], r=[so], final=True)
        kb.dma("sp", [(o_swin[:, 511, :], so[:, 512:768])], r=[so], final=True)
        if "a" in DBG:
            kb.dma("pool", [(o_swin[s, 0:511, :].rearrange("(a b) d -> a (b d)", a=73),
                             cwin[s, 1:512, :].rearrange("(a b) d -> a (b d)", a=73)) for s in range(NS)], final=True)
        kb.barrier()
        sti.close()
        if "p" in DBG:
            w1sb = kb.sb([128, 2, 64, 128], BF16, stack=st, name="w1sb")
            kb.dma("pool", [(w1sb[:, k, :, :], w1rep[:, k, :, :]) for k in range(2)], w=[w1sb])
            HTs = [[kb.sb([128, 66], BF16, stack=st) for g in range(2)] for k in range(2)]
            for k in range(2):
                for g in range(2):
                    pa = kb.ps()
                    for half in range(2):
                        for d in range(64):
                            kb.op("pe", lambda e: e.matmul(pa[:, half * NT:(half + 1) * NT], w1sb[:, k, d, :],
                                                           CK[half][:, :, k * 128 + g * 64 + d], start=(d == 0), stop=(d == 63)), r=[w1sb, CK[half]], w=[pa])
                    kb.op("act", lambda e: e.activation(out=HTs[k][g][:], in_=pa[:, 0:66], func=AF.Relu, bias=b1t[:, k:k + 1]), r=[pa, b1t], w=[HTs[k][g]])
            pk = kb.ps()
            for g in range(2):
                kb.op("pe", lambda e: e.matmul(pk[:, 0:66], w2pad[:, 0, g, :], HTs[0][g][:], start=(g == 0), stop=(g == 1)), r=[w2pad, HTs[0][g]], w=[pk])
            kb.op("act", lambda e: e.activation(out=kcT[:], in_=pk[:, 0:66], func=AF.Copy), r=[pk], w=[kcT])
            pv = kb.ps()
            for g in range(2):
                kb.op("pe", lambda e: e.matmul(pv[0:66, 0:128], HTs[1][g][:], w2pad[:, 1, g, :], start=(g == 0), stop=(g == 1)), r=[w2pad, HTs[1][g]], w=[pv])
            kb.op("act", lambda e: e.activation(out=vc[:], in_=pv[0:66, 0:128], func=AF.Copy), r=[pv], w=[vc])
        kb.barrier()

    if "p" in DBG:
      with contextlib.ExitStack() as st:
        kb.barrier()
        kb.psn = 6
        accP, accD = kb.psum[6], kb.psum[7]
        Esb = kb.sb([66, NT * 128], BF16, stack=st, name="Esb")
        kb.dma("pool", [(Esb[:], Ein[:, :])], w=[Esb])
        mk3 = kb.sb([128, 3, 512], BF16, stack=st, name="mk3")
        kb.dma("pool", [(mk3[:, m, :], mk3in[:, m, :]) for m in range(3)], w=[mk3])
        onesb = kb.sb([128, 128], BF16, stack=st, name="onesb_n")
        kb.op("pool", lambda e: e.memset(onesb[:], 1.0), w=[onesb])
        onesf = kb.sb([128, 128], F32, stack=st, name="onesf_n")
        kb.op("pool", lambda e: e.memset(onesf[:], 1.0), w=[onesf])
        cmk = [kb.sb([66, 512], BF16, stack=st) for _ in range(2)]
        addc = [kb.sb([128, 2, 66], F32, stack=st) for _ in range(2)]
        PT = [kb.sb([128, 512], BF16, stack=st) for _ in range(2)]
        MBT4 = kb.sb([66, 4, 128], BF16, stack=st, name="MBT4")
        rd = kb.sb([128, 512], F32, stack=st, name="rd")
        tb = kb.sb([128, 512], F32, stack=st, name="tb")
        acc = kb.sb([128, 512], F32, stack=st, name="acc")
        grep = kb.sb([128, 3, 512], BF16, stack=st, name="grep")
        dg = kb.sb([128, 4, 128], F32, stack=st, name="dg")
        pn = kb.sb([66, 4, 128], F32, stack=st, name="pn")
        impT = kb.sb([66, 128], F32, stack=st, name="impT")
        sc = kb.sb([128, 66], F32, stack=st, name="sc_n")
        sc2 = kb.sb([128, 66], F32, stack=st, name="sc2")
        m8 = kb.sb([128, 16], F32, stack=st, name="m8")

        qm = (kb.sb([128, 4, 128], BF16, name="qm_e", stack=st), kb.sb([128, 4, 128], BF16, name="qm_o", stack=st))
        gtok = kb.sb([128, 1, 24], F32, name="gtok", stack=st)
        wqn = load_w(wq_p[:, :], 512, st, "wqn")
        wg = load_w(w_in[:, OFF_G:OFF_G + 24], 24, st, "wg")
        def combine(g, br, first):
            hr = slice(g * 64, (g + 1) * 64)
            kb.op("dve", lambda e: e.tensor_scalar(out=rd[hr, :], in0=accD[hr, :], scalar1=1e-30, scalar2=None, op0=ALU.max), r=[accD], w=[rd])
            kb.op("dve", lambda e: e.reciprocal(rd[hr, :], rd[hr, :]), r=[rd], w=[rd])
            kb.op("dve", lambda e: e.tensor_tensor(out=tb[hr, :], in0=accP[hr, :], in1=rd[hr, :], op=ALU.mult), r=[accP, rd], w=[tb])
            if first:
                kb.op("dve", lambda e: e.tensor_tensor(out=acc[hr, :], in0=tb[hr, :], in1=grep[hr, br, :], op=ALU.mult), r=[tb, grep], w=[acc])
            else:
                kb.op("dve", lambda e: e.tensor_tensor(out=tb[hr, :], in0=tb[hr, :], in1=grep[hr, br, :], op=ALU.mult), r=[tb, grep], w=[tb])
                kb.op("dve", lambda e: e.tensor_tensor(out=acc[hr, :], in0=acc[hr, :], in1=tb[hr, :], op=ALU.add), r=[acc, tb], w=[acc])

        for i in range(NOWN):
            tq = 2 * i + 1
            kb.dma("pool", [(cmk[i % 2][:], cmkin[i, :, :])], w=[cmk[i % 2]])
            kb.dma("sp", [(addc[i % 2][:], addcin[i, :, :, :])], w=[addc[i % 2]])
            t = 2 * i + 1
            pa = kb.ps()
            for j in range(4):
                for kc in range(8):
                    kb.op("pe", lambda e, kc=kc: e.matmul(pa[:, j * 128:(j + 1) * 128], wqn[:, kc, j * 128:(j + 1) * 128], xnT[:, kc, t * 128:(t + 1) * 128],
                                                          start=(kc == 0), stop=(kc == 7)), r=[xnT_res[t], wqn], w=[pa])
            for p_ in range(2):
                kb.op("dve", lambda e: e.tensor_scalar(out=qm[p_][:, :, :], in0=pa[:, :].rearrange("p (j t) -> p j t", j=4),
                                                       scalar1=eom[:, p_:p_ + 1], scalar2=0.125, op0=ALU.mult, op1=ALU.mult), r=[pa, eom], w=[qm[p_]])
            pg = kb.ps()
            for kc in range(8):
                kb.op("pe", lambda e, kc=kc: e.matmul(pg[:, 0:24], xnT[:, kc, t * 128:(t + 1) * 128], wg[:, kc, :], start=(kc == 0), stop=(kc == 7)),
                      r=[xnT_res[t], wg], w=[pg])
            kb.op("act", lambda e: e.activation(out=gtok[:, 0, :], in_=pg[:, 0:24], func=AF.Sigmoid), r=[pg], w=[gtok])

            for g in range(2):
                qr = qm[g][:, :, :]
                for br in range(3):
                    gcols = gtok[:, 0, :].rearrange("p (h b) -> p h b", b=3)[:, 4 * g:4 * g + 4, br]
                    kb.op("dve", lambda e: e.tensor_tensor(out=dg[:], in0=bcast(ident_f[:, :].unsqueeze(1), [128, 4, 128]),
                                                           in1=bcast(gcols.unsqueeze(2), [128, 4, 128]), op=ALU.mult), r=[ident_f, gtok], w=[dg])
                    pgp = kb.ps()
                    kb.op("pe", lambda e: e.matmul(pgp[:, :], onesf[:, :], dg[:].rearrange("p j t -> p (j t)"), start=True, stop=True), r=[onesf, dg], w=[pgp])
                    kb.op("act", lambda e: e.activation(out=grep[:, br, :], in_=pgp[:, :], func=AF.Copy), r=[pgp], w=[grep])
                pa = kb.ps()
                kb.op("pe", lambda e: e.matmul(pa[0:66, :], kcT[:, :], qr, start=True, stop=False), r=[kcT, qm[g]], w=[pa])
                kb.op("pe", lambda e: e.matmul(pa[0:66, :], ident_bf[0:66, 0:66], cmk[i % 2][:, :], start=False, stop=True), r=[ident_bf, cmk[i % 2]], w=[pa])
                pt_ = PT[0]
                kb.op("act", lambda e: e.activation(out=pt_[0:66, :], in_=pa[0:66, :], func=AF.Exp), r=[pa], w=[pt_])
                kb.op("pe", lambda e: e.matmul(accP[:, :], vc[:, :], pt_[0:66, :], start=True, stop=True), r=[vc, pt_], w=[accP])
                kb.op("pe", lambda e: e.matmul(accD[:, :], onesb[0:66, :], pt_[0:66, :], start=True, stop=True), r=[onesb, pt_], w=[accD])
                combine(g, 0, True)
                kb.op("dve", lambda e: e.tensor_scalar(out=rd[0:66, :], in0=accD[0:66, :], scalar1=1e-30, scalar2=None, op0=ALU.max), r=[accD], w=[rd])
                kb.op("dve", lambda e: e.reciprocal(rd[0:66, :], rd[0:66, :]), r=[rd], w=[rd])
                kb.op("dve", lambda e: e.tensor_tensor(out=pn[:].rearrange("n j q -> n (j q)"), in0=pt_[0:66, :], in1=rd[0:66, :], op=ALU.mult), r=[pt_, rd], w=[pn])
                kb.op("dve", lambda e: e.tensor_reduce(out=impT[:], in_=pn[:].rearrange("n j q -> n q j"), axis=AX.X, op=ALU.add), r=[pn], w=[impT])
                pi = kb.ps()
                kb.op("pe", lambda e: e.transpose(pi[:, 0:66], impT[:, :], ident_f[0:66, 0:66]), r=[impT, ident_f], w=[pi])
                kb.op("dve", lambda e: e.tensor_tensor(out=sc[:], in0=pi[:, 0:66], in1=addc[i % 2][:, 0, :], op=ALU.add), r=[pi, addc[i % 2]], w=[sc])
                kb.op("dve", lambda e: e.max(m8[:, 0:8], sc[:]), r=[sc], w=[m8])
                kb.op("dve", lambda e: e.match_replace(sc2[:], m8[:, 0:8], sc[:], -1e30), r=[sc, m8], w=[sc2])
                kb.op("dve", lambda e: e.max(m8[:, 8:16], sc2[:]), r=[sc2], w=[m8])
                kb.op("dve", lambda e: e.tensor_scalar(out=sc2[:], in0=sc[:], scalar1=m8[:, 15:16], scalar2=None, op0=ALU.is_ge), r=[sc, m8], w=[sc2])
                kb.op("dve", lambda e: e.tensor_scalar(out=sc2[:], in0=sc2[:], scalar1=-1.0, scalar2=30000.0, op0=ALU.add, op1=ALU.mult), r=[sc2], w=[sc2])
                kb.op("dve", lambda e: e.tensor_tensor(out=sc2[:], in0=sc2[:], in1=addc[i % 2][:, 1, :], op=ALU.min), r=[sc2, addc[i % 2]], w=[sc2])
                tks = [tk for tk in range(tq - 4, tq + 1) if tk >= 0]
                for n_, tk in enumerate(tks):
                    extra = []
                    if tk == tq:
                        extra.append(0)
                    if tk == tq - 4:
                        extra.append(1)
                    if tk == 0:
                        extra.append(2)
                    pa = kb.ps()
                    kb.op("pe", lambda e: e.matmul(pa[:, :], kTw[:, tk * 128:(tk + 1) * 128], qr, start=True, stop=(len(extra) == 0)), r=[kTw, qm[g]], w=[pa])
                    for x_, m in enumerate(extra):
                        kb.op("pe", lambda e: e.matmul(pa[:, :], ident_bf[:, :], mk3[:, m, :], start=False, stop=(x_ == len(extra) - 1)), r=[ident_bf, mk3], w=[pa])
                    pt_ = PT[n_ % 2]
                    kb.op("act", lambda e: e.activation(out=pt_[:, :], in_=pa[:, :], func=AF.Exp), r=[pa], w=[pt_])
                    kb.op("pe", lambda e: e.matmul(accP[:, :], Vw[:, tk, :], pt_[:, :], start=(n_ == 0), stop=(n_ == len(tks) - 1)), r=[Vw, pt_], w=[accP])
                    kb.op("pe", lambda e: e.matmul(accD[:, :], onesb[:, :], pt_[:, :], start=(n_ == 0), stop=(n_ == len(tks) - 1)), r=[onesb, pt_], w=[accD])
                combine(g, 2, False)
                pm = kb.ps()
                kb.op("pe", lambda e: e.transpose(pm[0:66, 0:128], sc2[:, :], ident_f[:, :]), r=[sc2, ident_f], w=[pm])
                kb.op("dve", lambda e: e.tensor_copy(MBT4[:], bcast(pm[0:66, 0:128].unsqueeze(1), [66, 4, 128])), r=[pm], w=[MBT4])
                for tk in range(tq + 1):
                    pa = kb.ps()
                    kb.op("pe", lambda e: e.matmul(pa[:, :], kTs[:, tk * 128:(tk + 1) * 128], qr, start=True, stop=False), r=[kTs, qm[g]], w=[pa])
                    kb.op("pe", lambda e: e.matmul(pa[:, :], Esb[:, tk * 128:(tk + 1) * 128], MBT4[:].rearrange("n j q -> n (j q)"), start=False, stop=(tk != tq)),
                          r=[Esb, MBT4], w=[pa])
                    if tk == tq:
                        kb.op("pe", lambda e: e.matmul(pa[:, :], ident_bf[:, :], mk3[:, 0, :], start=False, stop=True), r=[ident_bf, mk3], w=[pa])
                    pt_ = PT[tk % 2]
                    kb.op("act", lambda e: e.activation(out=pt_[:, :], in_=pa[:, :], func=AF.Exp), r=[pa], w=[pt_])
                    kb.op("pe", lambda e: e.matmul(accP[:, :], Vs[:, tk, :], pt_[:, :], start=(tk == 0), stop=(tk == tq)), r=[Vs, pt_], w=[accP])
                    kb.op("pe", lambda e: e.matmul(accD[:, :], onesb[:, :], pt_[:, :], start=(tk == 0), stop=(tk == tq)), r=[onesb, pt_], w=[accD])
                combine(g, 1, False)
                hr = slice(g * 64, (g + 1) * 64)
                kb.op("act", lambda e: e.activation(out=mixT[hr, 4:8, i * 128:(i + 1) * 128], in_=acc[hr, :].rearrange("p (j q) -> p j q", j=4), func=AF.Copy),
                      r=[acc], w=[mix_r[i]])
        kb.psn = 8
        kb.barrier()
    kb.barrier()
    front.close()
    memKT = kb.sb([128, 8, 256], BF16, name="memKT")
    memV = kb.sb([128, 2, 1024], BF16, name="memV")

    with contextlib.ExitStack() as st:
        stages = []
        for i in range(2):
            stages.append((kb.sb([128, D], F32, stack=st), kb.sb([128, 1], F32, stack=st),
                           kb.sb([128, 1], F32, stack=st), kb.sb([128, D], BF16, stack=st)))
        mT = kb.sb([128, 8, 256], BF16, stack=st)
        mres = [Res(), Res()]
        for t in range(2):
            norm_transpose(memp[t * 128:(t + 1) * 128, :], 128, stages[t], mT[:, :, t * 128:(t + 1) * 128], mres[t], 2)
        wm = load_w(w_mem_kv[:, :], 2048, st, "wmkv")
        mo = [kb.sb([128, 2048], F32, stack=st) for _ in range(2)]
        for t in range(2):
            for cb in range(4):
                pa = kb.ps()
                for kc in range(8):
                    kb.op("pe", lambda e, kc=kc: e.matmul(pa[:, :], mT[:, kc, t * 128:(t + 1) * 128], wm[:, kc, cb * 512:(cb + 1) * 512],
                                                          start=(kc == 0), stop=(kc == 7)), r=[mres[t], wm], w=[pa])
                if cb % 2 == 0:
                    kb.op("dve", lambda e: e.tensor_copy(mo[t][:, cb * 512:(cb + 1) * 512], pa[:, :]), r=[pa], w=[mo[t]])
                else:
                    kb.op("act", lambda e: e.activation(out=mo[t][:, cb * 512:(cb + 1) * 512], in_=pa[:, :], func=AF.Copy), r=[pa], w=[mo[t]])
            kb.dma("sp", [(o_mem[t * 128:(t + 1) * 128, :], mo[t][:, :])], r=[mo[t]], final=True)
            kb.op("pool", lambda e: e.tensor_copy(memV[:, t, :], mo[t][:, 1024:2048]), r=[mo[t]], w=[memV])
        for c in range(8):
            pa = kb.ps()
            for kc in range(8):
                kb.op("pe", lambda e, kc=kc: e.matmul(pa[:, 0:256], wm[:, kc, c * 128:(c + 1) * 128], mT[:, kc, :],
                                                      start=(kc == 0), stop=(kc == 7)), r=[mres[0], mres[1], wm], w=[pa])
            kb.op("act", lambda e: e.activation(out=memKT[:, c, :], in_=pa[:, 0:256], func=AF.Copy), r=[pa], w=[memKT])
        kb.barrier()


    scr_qkv = nc.dram_tensor("scr_qkv", [NS, 8, 3, 64], F32).ap()
    scr_z = nc.dram_tensor("scr_z", [NS, 8, 64], F32).ap()
    scr_ba = nc.dram_tensor("scr_ba", [NS, 8, 2], F32).ap()
    scr_dno = nc.dram_tensor("scr_dno", [NS, 8, 64], F32).ap()
    r_scr = Res("scr")
    with contextlib.ExitStack() as st:
        cst = kb.sb([NS, 3, 1536], F32, stack=st)
        wc = kb.sb([NS, 4, 1536], F32, stack=st)
        cv = kb.sb([NS, 1536], F32, stack=st)
        tmpc = kb.sb([NS, 1536], F32, stack=st)
        kb.dma("sp", [(cst[:], sconv[:, :, :])], w=[cst])
        spq = kb.sb([NS, 1536], F32, stack=st)
        kb.dma("sp", [(spq[:], scr_sproj[:, 0:1536])], r=[r_sps], w=[spq])
        kb.dma("sp", [(wc[:], bass.AP(conv_w.tensor, 0, [[0, NS], [1, 4 * 1536]]).rearrange("p (j c) -> p j c", j=4))], w=[wc])
        kb.op("dve", lambda e: e.tensor_tensor(out=cv[:], in0=spq[:, :], in1=wc[:, 3, :], op=ALU.mult), r=[spq, wc], w=[cv])
        for j in range(3):
            kb.op("dve", lambda e: e.tensor_tensor(out=tmpc[:], in0=cst[:, j, :], in1=wc[:, j, :], op=ALU.mult), r=[cst, wc], w=[tmpc])
            kb.op("dve", lambda e: e.tensor_tensor(out=cv[:], in0=cv[:], in1=tmpc[:], op=ALU.add), r=[cv, tmpc], w=[cv])
        kb.op("act", lambda e: e.activation(out=cv[:], in_=cv[:], func=AF.Silu), r=[cv], w=[cv])
        kb.dma("sp", [(scr_qkv[:, :, a, :], cv[:, a * 512:(a + 1) * 512].rearrange("s (h d) -> s h d", h=8)) for a in range(3)] + [
                      (scr_z.rearrange("s h d -> s (h d)"), scr_sproj[:, OFF_Z:OFF_Z + 512]),
                      (scr_ba[:, :, 0], scr_sproj[:, OFF_B:OFF_B + 8]), (scr_ba[:, :, 1], scr_sproj[:, OFF_A:OFF_A + 8])],
               r=[cv, r_sps], w=[r_scr], allow_slow_non_contiguous=True)
        q3 = kb.sb([128, 3, 64], F32, stack=st)
        z1 = kb.sb([128, 64], F32, stack=st)
        ba = kb.sb([128, 2], F32, stack=st)
        abr = kb.sb([128, 2], F32, stack=st)
        gn = kb.sb([128, 64], F32, stack=st)
        S = kb.sb([128, 64, 64], F32, stack=st)
        big = kb.sb([128, 64, 64], F32, stack=st)
        sm = kb.sb([128, 16], F32, stack=st)
        v64 = [kb.sb([128, 64], F32, stack=st) for _ in range(6)]
        kb.dma("sp", [(q3[:], scr_qkv.rearrange("s h a d -> (s h) a d")),
                      (z1[:], scr_z.rearrange("s h d -> (s h) d")),
                      (ba[:], scr_ba.rearrange("s h a -> (s h) a"))], r=[r_scr], w=[q3, z1, ba])
        kb.dma("sp", [(abr[:], abrep[:, :]), (gn[:], dnnorm[:, :])], w=[abr, gn])
        kb.dma("sp", [(S[:, 0:32, :], srec[:, 0:32, :]), (S[:, 32:64, :], srec[:, 32:64, :])], w=[S])
        BETA, X, EG, SSQ, SSK, RQ, RK, SSO, RO, EA = range(10)
        c1 = lambda i: sm[:, i:i + 1]
        kb.op("act", lambda e: e.activation(out=c1(BETA), in_=ba[:, 0:1], func=AF.Sigmoid), r=[ba], w=[sm])
        kb.op("act", lambda e: e.activation(out=c1(X), in_=ba[:, 1:2], func=AF.Exp, bias=abr[:, 1:2]), r=[ba, abr], w=[sm])
        kb.op("act", lambda e: e.activation(out=c1(X), in_=c1(X), func=AF.Ln, bias=onec[:, 0:1]), r=[sm, onec], w=[sm])
        kb.op("act", lambda e: e.activation(out=c1(EA), in_=abr[:, 0:1], func=AF.Exp), r=[abr], w=[sm])
        kb.op("dve", lambda e: e.tensor_scalar(out=c1(X), in0=c1(X), scalar1=c1(EA), scalar2=-1.0, op0=ALU.mult, op1=ALU.mult), r=[sm], w=[sm])
        kb.op("act", lambda e: e.activation(out=c1(EG), in_=c1(X), func=AF.Exp), r=[sm], w=[sm])
        kb.op("act", lambda e: e.activation(out=v64[0][:], in_=q3[:, 0, :], func=AF.Square, accum_out=c1(SSQ)), r=[q3], w=[v64[0], sm])
        kb.op("act", lambda e: e.activation(out=v64[0][:], in_=q3[:, 1, :], func=AF.Square, accum_out=c1(SSK)), r=[q3], w=[v64[0], sm])
        kb.op("act", lambda e: e.activation(out=sm[:, RQ:RQ + 2], in_=sm[:, SSQ:SSQ + 2], func=AF.Sqrt, bias=epsc[:, 0:1]), r=[sm, epsc], w=[sm])
        kb.op("dve", lambda e: e.reciprocal(sm[:, RQ:RQ + 2], sm[:, RQ:RQ + 2]), r=[sm], w=[sm])
        qn, kn, t1, vn, o1 = v64[1], v64[2], v64[3], v64[4], v64[5]
        kb.op("dve", lambda e: e.tensor_scalar(out=qn[:], in0=q3[:, 0, :], scalar1=c1(RQ), scalar2=0.125, op0=ALU.mult, op1=ALU.mult), r=[q3, sm], w=[qn])
        kb.op("dve", lambda e: e.tensor_scalar(out=kn[:], in0=q3[:, 1, :], scalar1=c1(RK), scalar2=None, op0=ALU.mult), r=[q3, sm], w=[kn])
        kb.op("dve", lambda e: e.tensor_tensor(out=big[:], in0=S[:], in1=bcast(kn[:, :].unsqueeze(2), [128, 64, 64]), op=ALU.mult), r=[S, kn], w=[big])
        kb.op("dve", lambda e: e.tensor_reduce(out=t1[:], in_=big[:].rearrange("p k v -> p v k"), axis=AX.X, op=ALU.add), r=[big], w=[t1])
        kb.op("dve", lambda e: e.tensor_scalar(out=t1[:], in0=t1[:], scalar1=c1(EG), scalar2=None, op0=ALU.mult), r=[t1, sm], w=[t1])
        kb.op("dve", lambda e: e.tensor_tensor(out=vn[:], in0=q3[:, 2, :], in1=t1[:], op=ALU.subtract), r=[q3, t1], w=[vn])
        kb.op("dve", lambda e: e.tensor_scalar(out=vn[:], in0=vn[:], scalar1=c1(BETA), scalar2=None, op0=ALU.mult), r=[vn, sm], w=[vn])
        kb.op("dve", lambda e: e.tensor_tensor(out=big[:], in0=bcast(kn[:, :].unsqueeze(2), [128, 64, 64]),
                                               in1=bcast(vn[:, :].unsqueeze(1), [128, 64, 64]), op=ALU.mult), r=[kn, vn], w=[big])
        kb.op("dve", lambda e: e.scalar_tensor_tensor(out=S[:], in0=S[:], scalar=c1(EG), in1=big[:], op0=ALU.mult, op1=ALU.add), r=[S, big, sm], w=[S])
        kb.dma("sp", [(o_srec[:, 0:32, :], S[:, 0:32, :]), (o_srec[:, 32:64, :], S[:, 32:64, :])], r=[S], final=True)
        kb.op("dve", lambda e: e.tensor_tensor(out=big[:], in0=S[:], in1=bcast(qn[:, :].unsqueeze(2), [128, 64, 64]), op=ALU.mult), r=[S, qn], w=[big])
        kb.op("dve", lambda e: e.tensor_reduce(out=o1[:], in_=big[:].rearrange("p k v -> p v k"), axis=AX.X, op=ALU.add), r=[big], w=[o1])
        kb.op("act", lambda e: e.activation(out=t1[:], in_=o1[:], func=AF.Square, accum_out=c1(SSO)), r=[o1], w=[t1, sm])
        kb.op("act", lambda e: e.activation(out=c1(RO), in_=c1(SSO), func=AF.Sqrt, scale=1.0 / 64, bias=epsc[:, 0:1]), r=[sm, epsc], w=[sm])
        kb.op("dve", lambda e: e.reciprocal(c1(RO), c1(RO)), r=[sm], w=[sm])
        kb.op("dve", lambda e: e.scalar_tensor_tensor(out=o1[:], in0=o1[:], scalar=c1(RO), in1=gn[:], op0=ALU.mult, op1=ALU.mult), r=[o1, sm, gn], w=[o1])
        kb.op("act", lambda e: e.activation(out=z1[:], in_=z1[:], func=AF.Silu), r=[z1], w=[z1])
        kb.op("dve", lambda e: e.tensor_tensor(out=o1[:], in0=o1[:], in1=z1[:], op=ALU.mult), r=[o1, z1], w=[o1])
        kb.dma("sp", [(scr_dno.rearrange("s h d -> (s h) d"), o1[:, :])], r=[o1], w=[r_scr])
        dtok = kb.sb([NS, 512], F32, stack=st)
        dtb = kb.sb([NS, 512], BF16, stack=st)
        kb.dma("sp", [(dtok[:], scr_dno.rearrange("s h d -> s (h d)"))], r=[r_scr], w=[dtok])
        kb.op("dve", lambda e: e.tensor_copy(dtb[:], dtok[:]), r=[dtok], w=[dtb])
        pt = kb.ps()
        ptv = pt[:].bitcast(BF16)
        for f in range(4):
            kb.op("pe", lambda e, f=f: e.transpose(ptv[:, f * NS:(f + 1) * NS], dtb[:, f * 128:(f + 1) * 128], ident_bf[0:NS, 0:NS]), r=[dtb, ident_bf], w=[pt])
        kb.op("dve", lambda e: e.tensor_copy(mixT[:, 0:4, NOWN * 128:TOK], ptv[:, 0:4 * NS].rearrange("p (f t) -> p f t", f=4)), r=[pt], w=[mix_r[16]])
        kb.barrier()


    if "s" in DBG:
      with contextlib.ExitStack() as st:
        kb.barrier()
        F = lambda shape, name, dt=F32: kb.sb(shape, dt, stack=st, name=name)
        w1sb = F([128, 2, 64, 128], "w1sb_s", BF16)
        kb.dma("pool", [(w1sb[:, k, :, :], w1rep[:, k, :, :]) for k in range(2)], w=[w1sb])
        pe_sb, eom, b1t, w2sb = F([128, 256], "pe_s"), F([128, 2], "eom_s"), F([128, 2], "b1t_s"), F([128, 2, 64], "w2sb_s")
        kb.dma("sp", [(pe_sb[:], perep[:, :]), (eom[:], eomin[:, :]), (b1t[:], b1in[:, :]), (w2sb[:], w2in[:, :, :])], w=[pe_sb, eom, b1t, w2sb])
        w2pad = F([128, 2, 2, 128], "w2pad_s", BF16)
        kb.op("pool", lambda e: e.memset(w2pad[:], 0.0), w=[w2pad])
        for k in range(2):
            for g in range(2):
                kb.op("dve", lambda e: e.tensor_copy(w2pad[:, k, g, g * 64:(g + 1) * 64], w2sb[:, k, :]), r=[w2sb], w=[w2pad])
        sc1 = F([1, 512], "sc1")
        kb.dma("sp", [(sc1[:], sconst[:, :])], w=[sc1])
        Ehalf = sc1[0:1, 0:256].rearrange("o (h k) -> o h k", h=2)
        addc1 = sc1[0:1, 256:289]
        onesf = F([128, 128], "onesf_s")
        kb.op("pool", lambda e: e.memset(onesf[:], 1.0), w=[onesf])
        onesb = F([128, 128], "onesb_s", BF16)
        kb.op("pool", lambda e: e.memset(onesb[:], 1.0), w=[onesb])
        selb = F([NS, NS, 128], "selb_s")
        kb.op("dve", lambda e: e.tensor_copy(selb[:], bcast(ident_f[0:NS, 0:NS].unsqueeze(2), [NS, NS, 128])), r=[ident_f], w=[selb])
        sq, skv, sg = F([NS, 512], "sq"), F([NS, 768], "skv"), F([NS, 24], "sg")
        kb.dma("sp", [(sq[:], scr_sproj[:, OFF_Q:OFF_Q + 512]), (skv[:], scr_sproj[:, OFF_KV:OFF_KV + 768]), (sg[:], scr_sproj[:, OFF_G:OFF_G + 24])],
               r=[r_sps], w=[sq, skv, sg])
        kb.op("act", lambda e: e.activation(out=sg[:], in_=sg[:], func=AF.Sigmoid), r=[sg], w=[sg])
        sqp = F([NS, 4, 2, 64], "sqp")
        kb.op("act", lambda e: e.activation(out=sqp[:], in_=sq[:].rearrange("s (g j d) -> s j g d", g=2, j=4), func=AF.Copy, scale=0.125), r=[sq], w=[sqp])
        Q8b, Q8f = F([128, NS, 2, 4], "Q8b", BF16), F([128, NS, 2, 4], "Q8f")
        pq = kb.ps()
        for j in range(4):
            kb.op("pe", lambda e, j=j: e.transpose(pq[:, j * NS:(j + 1) * NS], sqp[:, j, :, :].rearrange("s g d -> s (g d)"), ident_f[0:NS, 0:NS]), r=[sqp, ident_f], w=[pq])
        for g in range(2):
            kb.op("dve", lambda e: e.tensor_scalar(out=Q8f[:, :, g, :], in0=pq[:, 0:4 * NS].rearrange("p (j s) -> p s j", j=4), scalar1=eom[:, g:g + 1], scalar2=None, op0=ALU.mult),
                  r=[pq, eom], w=[Q8f])
        kb.op("dve", lambda e: e.tensor_copy(Q8b[:], Q8f[:]), r=[Q8f], w=[Q8b])
        kTn = F([128, 2, NS], "kTn")
        pk = kb.ps()
        for a, c0 in enumerate((256, 512)):
            kb.op("pe", lambda e, a=a, c0=c0: e.transpose(pk[:, a * NS:(a + 1) * NS], skv[:, c0:c0 + 128], ident_f[0:NS, 0:NS]), r=[skv, ident_f], w=[pk])
        kb.op("act", lambda e: e.activation(out=kTn[:].rearrange("p a s -> p (a s)"), in_=pk[:, 0:2 * NS], func=AF.Copy), r=[pk], w=[kTn])
        pti = F([128, NS * 16], "pti", I32)
        kb.dma("sp", [(pti[:], bass.AP(ptab.tensor, 0, [[0, 128], [1, NS * 16]]))], w=[pti])
        ptf = F([128, NS * 16], "ptf")
        iop = F([128, 1], "iop")
        kb.dma("sp", [(iop[:], iotap[:, :])], w=[iop])
        kb.op("dve", lambda e: e.tensor_copy(ptf[:], pti[:]), r=[pti], w=[ptf])
        kb.op("dve", lambda e: e.tensor_scalar(out=ptf[:], in0=ptf[:], scalar1=128.0, scalar2=iop[:, 0:1], op0=ALU.mult, op1=ALU.add), r=[ptf, iop], w=[ptf])
        kb.op("dve", lambda e: e.tensor_copy(pti[:], ptf[:]), r=[ptf], w=[pti])
        PGs = [F([128, 16, 512], "PG0"), F([128, 16, 512], "PG1")]
        WT = F([128, 4, 256], "WT")
        big = F([128, 16, 4, 64], "big_s")
        PGm = (F([128, 16, 256], "PGe", BF16), F([128, 16, 256], "PGo", BF16))
        qb = F([128, 2, 4, 64], "qb")
        ssl, Pw = F([128, 16, 2, 4], "ssl"), F([128, 4, 2, 4], "Pw")
        Psum = F([128, 8], "Psum")
        HTs = [[F([128, 32], f"HTs{k}{g}", BF16) for g in range(2)] for k in range(2)]
        kcs, vcs = F([128, 32], "kcs", BF16), F([32, 128], "vcs", BF16)
        Pc, Pcb, pnc = F([32, 8], "Pc"), F([32, 8], "Pcb", BF16), F([32, 8], "pnc")
        impT = F([32, 2], "impT_s")
        row = F([1, 2, 34], "row")
        row2 = F([1, 34], "row2")
        m8 = F([1, 16], "m8_s")
        MBrow = F([1, 2, 16, 2], "MBrow")
        Pn = F([NS, 8], "Pn")
        grep = F([128, 24], "grep_s")
        rdn, tbn, accn = F([128, 8], "rdn"), F([128, 8], "tbn"), F([128, 8], "accn")

        def comb(pP, pD, br, first, guard=False):
            kb.op("dve", lambda e: e.tensor_scalar(out=rdn[:], in0=pD[:, 0:8], scalar1=1e-30, scalar2=None, op0=ALU.max), r=[pD], w=[rdn])
            kb.op("dve", lambda e: e.reciprocal(rdn[:], rdn[:]), r=[rdn], w=[rdn])
            kb.op("dve", lambda e: e.tensor_tensor(out=tbn[:], in0=pP[:, 0:8], in1=rdn[:], op=ALU.mult), r=[pP, rdn], w=[tbn])
            gv = grep[:, :].rearrange("p (h b) -> p h b", b=3)[:, :, br]
            if first:
                kb.op("dve", lambda e: e.tensor_tensor(out=accn[:], in0=tbn[:], in1=gv, op=ALU.mult), r=[tbn, grep], w=[accn])
            else:
                kb.op("dve", lambda e: e.tensor_tensor(out=tbn[:], in0=tbn[:], in1=gv, op=ALU.mult), r=[tbn, grep], w=[tbn])
                kb.op("dve", lambda e: e.tensor_tensor(out=accn[:], in0=accn[:], in1=tbn[:], op=ALU.add), r=[accn, tbn], w=[accn])

        for si in range(NS):
            if si == 0:
                kb_gather(kb, PGs[0], pti, cnsa, 0)
            if si + 1 < NS:
                kb_gather(kb, PGs[(si + 1) % 2], pti, cnsa, si + 1)
            PG = PGs[si % 2]
            kb.dma("sp", [(WT[:, t4, :], cwin[si, t4 * 128:(t4 + 1) * 128, :]) for t4 in range(4)], w=[WT])
            pqb = kb.ps()
            kb.op("pe", lambda e: e.matmul(pqb[:, :], selb[:, si, :], sqp[:].rearrange("s j g d -> s (j g d)"), start=True, stop=True), r=[selb, sqp], w=[pqb])
            kb.op("act", lambda e: e.activation(out=qb[:], in_=pqb[:, :].rearrange("p (j g d) -> p g j d", j=4, g=2), func=AF.Copy), r=[pqb], w=[qb])
            pgr = kb.ps()
            kb.op("pe", lambda e: e.matmul(pgr[:, 0:24], selb[:, si, :], sg[:, :], start=True, stop=True), r=[selb, sg], w=[pgr])
            kb.op("act", lambda e: e.activation(out=grep[:], in_=pgr[:, 0:24], func=AF.Copy), r=[pgr], w=[grep])
            kb.op("dve", lambda e: e.tensor_tensor(out=big[:].rearrange("p a b c -> p a (b c)"), in0=PG[:, :, 0:256], in1=bcast(pe_sb[:, :].unsqueeze(1), [128, 16, 256]), op=ALU.add),
                  r=[PG, pe_sb], w=[big])
            for p_ in range(2):
                kb.op("dve", lambda e: e.tensor_scalar(out=PGm[p_][:], in0=big[:].rearrange("p a b c -> p a (b c)"), scalar1=eom[:, p_:p_ + 1], scalar2=None, op0=ALU.mult),
                      r=[big, eom], w=[PGm[p_]])
            for k in range(2):
                for g in range(2):
                    pa = kb.ps()
                    for half in range(2):
                        for d in range(64):
                            kb.op("pe", lambda e: e.matmul(pa[:, half * 16:(half + 1) * 16], w1sb[:, k, d, :], PGm[half][:, :, k * 128 + g * 64 + d],
                                                           start=(d == 0), stop=(d == 63)), r=[w1sb, PGm[half]], w=[pa])
                    kb.op("act", lambda e: e.activation(out=HTs[k][g][:], in_=pa[:, 0:32], func=AF.Relu, bias=b1t[:, k:k + 1]), r=[pa, b1t], w=[HTs[k][g]])
            pk = kb.ps()
            for g in range(2):
                kb.op("pe", lambda e: e.matmul(pk[:, 0:32], w2pad[:, 0, g, :], HTs[0][g][:], start=(g == 0), stop=(g == 1)), r=[w2pad, HTs[0][g]], w=[pk])
            kb.op("act", lambda e: e.activation(out=kcs[:], in_=pk[:, 0:32], func=AF.Copy), r=[pk], w=[kcs])
            pv = kb.ps()
            for g in range(2):
                kb.op("pe", lambda e: e.matmul(pv[0:32, 0:128], HTs[1][g][:], w2pad[:, 1, g, :], start=(g == 0), stop=(g == 1)), r=[w2pad, HTs[1][g]], w=[pv])
            kb.op("act", lambda e: e.activation(out=vcs[:], in_=pv[0:32, 0:128], func=AF.Copy), r=[pv], w=[vcs])
            q8b = Q8b[:, si, :, :].rearrange("p g j -> p (g j)")
            q8f = Q8f[:, si, :, :].rearrange("p g j -> p (g j)")
            pa = kb.ps()
            kb.op("pe", lambda e: e.matmul(pa[0:32, 0:8], kcs[:, :], q8b, start=True, stop=True), r=[kcs, Q8b], w=[pa])
            kb.op("act", lambda e: e.activation(out=Pc[:], in_=pa[0:32, 0:8], func=AF.Exp), r=[pa], w=[Pc])
            kb.op("dve", lambda e: e.tensor_copy(Pcb[:], Pc[:]), r=[Pc], w=[Pcb])
            pP, pD = kb.ps(), kb.ps()
            kb.op("pe", lambda e: e.matmul(pP[:, 0:8], vcs[:, :], Pcb[:, :], start=True, stop=True), r=[vcs, Pcb], w=[pP])
            kb.op("pe", lambda e: e.matmul(pD[:, 0:8], onesb[0:32, :], Pcb[:, :], start=True, stop=True), r=[onesb, Pcb], w=[pD])
            comb(pP, pD, 0, True)
            kb.op("dve", lambda e: e.tensor_tensor(out=pnc[:], in0=Pc[:], in1=rdn[0:32, :], op=ALU.mult), r=[Pc, rdn], w=[pnc])
            kb.op("dve", lambda e: e.tensor_reduce(out=impT[:], in_=pnc[:].rearrange("n (g j) -> n g j", g=2), axis=AX.X, op=ALU.add), r=[pnc], w=[impT])
            pi = kb.ps()
            for g in range(2):
                kb.op("pe", lambda e, g=g: e.transpose(pi[0:1, g * 32:(g + 1) * 32], impT[:, g:g + 1], ident_f[0:32, 0:32]), r=[impT, ident_f], w=[pi])
            kb.op("pool", lambda e: e.memset(row[:], 0.0), w=[row])
            for g in range(2):
                kb.op("dve", lambda e: e.tensor_copy(row[0:1, g, 0:32].rearrange("o (p h) -> o p h", h=2), pi[0:1, g * 32:(g + 1) * 32].rearrange("o (h p) -> o p h", h=2)),
                      r=[pi], w=[row])
                kb.op("dve", lambda e: e.tensor_tensor(out=row[0:1, g, 0:33], in0=row[0:1, g, 0:33], in1=addc1, op=ALU.add), r=[row, sc1], w=[row])
                kb.op("dve", lambda e: e.max(m8[0:1, 0:8], row[0:1, g, 0:33]), r=[row], w=[m8])
                kb.op("dve", lambda e: e.match_replace(row2[0:1, 0:33], m8[0:1, 0:8], row[0:1, g, 0:33], -1e30), r=[row, m8], w=[row2])
                kb.op("dve", lambda e: e.max(m8[0:1, 8:16], row2[0:1, 0:33]), r=[row2], w=[m8])
                kb.op("dve", lambda e: e.tensor_scalar(out=row[0:1, g, 0:33], in0=row[0:1, g, 0:33], scalar1=m8[0:1, 15:16], scalar2=None, op0=ALU.is_ge), r=[row, m8], w=[row])
                kb.op("dve", lambda e: e.tensor_scalar(out=row[0:1, g, 0:33], in0=row[0:1, g, 0:33], scalar1=-1.0, scalar2=30000.0, op0=ALU.add, op1=ALU.mult), r=[row], w=[row])
                kb.op("dve", lambda e: e.tensor_copy(MBrow[0:1, :, :, g], row[0:1, g, 0:32].rearrange("o (p h) -> o h p", h=2)), r=[row], w=[MBrow])
            pmk = kb.ps()
            for hf in range(2):
                kb.op("pe", lambda e, hf=hf: e.matmul(pmk[:, 0:32], Ehalf[0:1, hf, :], MBrow[0:1, hf, :, :].rearrange("o p g -> o (p g)"), start=(hf == 0), stop=(hf == 1)),
                      r=[sc1, MBrow], w=[pmk])
            for g in range(2):
                kb.op("dve", lambda e: e.tensor_tensor(out=big[:], in0=bcast(PG[:, :, 256 + 64 * g:320 + 64 * g].unsqueeze(2), [128, 16, 4, 64]),
                                                       in1=bcast(qb[:, g, :, :].unsqueeze(1), [128, 16, 4, 64]), op=ALU.mult), r=[PG, qb], w=[big])
                kb.op("dve", lambda e: e.tensor_reduce(out=ssl[:, :, g, :], in_=big[:], axis=AX.X, op=ALU.add), r=[big], w=[ssl])
            kb.op("dve", lambda e: e.tensor_tensor(out=ssl[:], in0=ssl[:], in1=bcast(pmk[:, 0:32].rearrange("p (a g) -> p a g", g=2).unsqueeze(3), [128, 16, 2, 4]), op=ALU.add),
                  r=[ssl, pmk], w=[ssl])
            kb.op("act", lambda e: e.activation(out=ssl[:], in_=ssl[:], func=AF.Exp), r=[ssl], w=[ssl])
            for a, (Pt, ntile, vsrc, vofs, vnew0) in enumerate(((ssl, 16, PG, 384, 384), (Pw, 4, WT, 128, 640))):
                if a == 1:
                    for g in range(2):
                        kb.op("dve", lambda e: e.tensor_tensor(out=big[:, 0:4, :, :], in0=bcast(WT[:, :, 64 * g:64 + 64 * g].unsqueeze(2), [128, 4, 4, 64]),
                                                               in1=bcast(qb[:, g, :, :].unsqueeze(1), [128, 4, 4, 64]), op=ALU.mult), r=[WT, qb], w=[big])
                        kb.op("dve", lambda e: e.tensor_reduce(out=Pw[:, :, g, :], in_=big[:, 0:4, :, :], axis=AX.X, op=ALU.add), r=[big], w=[Pw])
                    kb.op("dve", lambda e: e.tensor_scalar(out=Pw[0:1, 0, :, :], in0=Pw[0:1, 0, :, :], scalar1=-30000.0, scalar2=None, op0=ALU.add), r=[Pw], w=[Pw])
                    kb.op("act", lambda e: e.activation(out=Pw[:], in_=Pw[:], func=AF.Exp), r=[Pw], w=[Pw])
                pnw = kb.ps()
                kb.op("pe", lambda e: e.matmul(pnw[0:NS, 0:8], kTn[:, a, :], q8f, start=True, stop=True), r=[kTn, Q8f], w=[pnw])
                kb.op("act", lambda e: e.activation(out=Pn[:], in_=pnw[0:NS, 0:8], func=AF.Exp), r=[pnw], w=[Pn])
                kb.op("dve", lambda e: e.tensor_scalar(out=Pn[:], in0=Pn[:], scalar1=ident_f[0:NS, si:si + 1], scalar2=None, op0=ALU.mult), r=[Pn, ident_f], w=[Pn])
                kb.op("dve", lambda e: e.tensor_reduce(out=Psum[:], in_=Pt[:].rearrange("p a g j -> p (g j) a"), axis=AX.X, op=ALU.add), r=[Pt], w=[Psum])
                pP, pD = kb.ps(), kb.ps()
                for t_ in range(ntile):
                    kb.op("pe", lambda e, t_=t_: e.matmul(pP[:, 0:8], vsrc[:, t_, vofs:vofs + 128], Pt[:, t_, :, :].rearrange("p g j -> p (g j)"), start=(t_ == 0), stop=False),
                          r=[vsrc, Pt], w=[pP])
                kb.op("pe", lambda e: e.matmul(pP[:, 0:8], skv[:, vnew0:vnew0 + 128], Pn[:, :], start=False, stop=True), r=[skv, Pn], w=[pP])
                kb.op("pe", lambda e: e.matmul(pD[:, 0:8], onesf[:, :], Psum[:, :], start=True, stop=False), r=[onesf, Psum], w=[pD])
                kb.op("pe", lambda e: e.matmul(pD[:, 0:8], onesf[0:NS, :], Pn[:, :], start=False, stop=True), r=[onesf, Pn], w=[pD])
                comb(pP, pD, 1 + a, False)
            for g in range(2):
                hr = slice(g * 64, (g + 1) * 64)
                kb.op("act", lambda e: e.activation(out=mixT[hr, 4:8, NOWN * 128 + si], in_=accn[hr, 4 * g:4 * g + 4], func=AF.Copy), r=[accn], w=[mix_r[16]])
        kb.barrier()

    with contextlib.ExitStack() as st:
        if "m" in DBG:
            kb.dma("pool", [(mixT[:, k, :], dbg_mixT[:, k, :]) for k in range(8)], w=mix_r)
        if "n" in DBG:
            kb.dma("pool", [(mixT[:, k, :], dbg_mixT[:, k, :]) for k in range(4, 8)], w=mix_r)
        xres = kb.sb([128, 17, D], F32, stack=st, name="xres")
        xr = [Res(f"xr{i}") for i in range(17)]
        for i, (t0, n) in enumerate(tiles):
            src = xp[(2 * i + 1) * 128:(2 * i + 2) * 128, :] if i < NOWN else xs[:, :]
            kb.dma("sp", [(xres[0:n, i, :], src)], w=[xr[i]])
        hT = kb.sb([128, 8, TOK], BF16, stack=st, name="hT")
        h_r = [Res(f"h{i}") for i in range(17)]
        stg = [(None, kb.sb([128, 1], F32, stack=st), kb.sb([128, 1], F32, stack=st), kb.sb([128, D], BF16, stack=st))] * 2
        wA = [kb.sb([128, 8, 1024], BF16, stack=st, name=f"wA{i}") for i in range(2)]

        def loadA(slot, dram):
            src = dram.rearrange("(k p) n -> p k n", p=128)
            kb.dma("pool", [(wA[slot][:, k, :], src[:, k, :]) for k in range(8)], w=[wA[slot]])
            return wA[slot]

        def linear_add(inT, in_res, w):
            for i, (t0, n) in enumerate(tiles):
                for half in range(2):
                    pa = kb.ps()
                    for kc in range(8):
                        kb.op("pe", lambda e, kc=kc: e.matmul(pa[0:n, :], inT[:, kc, t0:t0 + n], w[:, kc, half * 512:(half + 1) * 512],
                                                              start=(kc == 0), stop=(kc == 7)), r=[in_res[i], w], w=[pa])
                    kb.op("dve", lambda e: e.tensor_tensor(out=xres[0:n, i, half * 512:(half + 1) * 512], in0=pa[0:n, :],
                                                           in1=xres[0:n, i, half * 512:(half + 1) * 512], op=ALU.add), r=[pa, xr[i]], w=[xr[i]])

        def norm_all(ln_idx):
            for i, (t0, n) in enumerate(tiles):
                norm_transpose(xres[0:n, i, :], n, stg[i % 2], hT[:, :, t0:t0 + n], h_r[i], ln_idx, sb_src=xr[i])

        w = loadA(0, w_out)
        wq = loadA(1, w_mem_q)
        linear_add(mixT, mix_r, w)
        if "1" in DBG:
            for i, (t0, n) in enumerate(tiles):
                if i < NOWN:
                    kb.dma("sp", [(o_y[i, :, :], xres[:, i, :])], r=[xr[i]], final=True)
                else:
                    kb.dma("sp", [(o_ys[:, :], xres[0:NS, i, :])], r=[xr[i]], final=True)
        norm_all(1)
        wo = loadA(0, w_mem_o)
        aT = mixT
        a_r = mix_r
        with contextlib.ExitStack() as st2:
            onesb = kb.sb([128, 128], BF16, stack=st2, name="onesb")
            kb.op("pool", lambda e: e.memset(onesb[:], 1.0), w=[onesb])
            onesf = kb.sb([128, 128], F32, stack=st2, name="onesf")
            kb.op("pool", lambda e: e.memset(onesf[:], 1.0), w=[onesf])
            qT = kb.sb([128, 4, 512], BF16, stack=st2, name="qT")
            pT = [kb.sb([128, 512], BF16, stack=st2) for _ in range(2)]
            rden = kb.sb([128, 512], F32, stack=st2, name="rden")
            for blk in range(4):
                b0 = blk * 512
                rr = [h_r[4 * blk + j] for j in range(4)]
                for hd in range(4):
                    if hd % 2 == 0:
                        for c in range(4):
                            pa = kb.ps()
                            cc = 2 * hd + c
                            for kc in range(8):
                                kb.op("pe", lambda e, kc=kc: e.matmul(pa[:, :], wq[:, kc, cc * 128:(cc + 1) * 128], hT[:, kc, b0:b0 + 512],
                                                                      start=(kc == 0), stop=(kc == 7)), r=rr + [wq], w=[pa])
                            kb.op("act", lambda e: e.activation(out=qT[:, c, :], in_=pa[:, :], func=AF.Copy, scale=1.0 / 16), r=[pa], w=[qT])
                    for mt in range(2):
                        pa = kb.ps()
                        for j in range(2):
                            kb.op("pe", lambda e, j=j: e.matmul(pa[:, :], memKT[:, 2 * hd + j, mt * 128:(mt + 1) * 128], qT[:, 2 * (hd % 2) + j, :],
                                                                start=(j == 0), stop=(j == 1)), r=[memKT, qT], w=[pa])
                        kb.op("act", lambda e: e.activation(out=pT[mt][:, :], in_=pa[:, :], func=AF.Exp), r=[pa], w=[pT[mt]])
                    pd = kb.ps()
                    for mt in range(2):
                        kb.op("pe", lambda e, mt=mt: e.matmul(pd[:, :], onesb[:, :], pT[mt][:, :], start=(mt == 0), stop=(mt == 1)), r=[onesb, pT[mt]], w=[pd])
                    kb.op("dve", lambda e: e.reciprocal(rden[:, :], pd[:, :]), r=[pd], w=[rden])
                    for j in range(2):
                        po = kb.ps()
                        for mt in range(2):
                            kb.op("pe", lambda e, mt=mt: e.matmul(po[:, :], memV[:, mt, hd * 256 + j * 128: hd * 256 + (j + 1) * 128], pT[mt][:, :],
                                                                  start=(mt == 0), stop=(mt == 1)), r=[memV, pT[mt]], w=[po])
                        kb.op("dve", lambda e: e.tensor_tensor(out=aT[:, 2 * hd + j, b0:b0 + 512], in0=po[:, :], in1=rden[:, :], op=ALU.mult),
                              r=[po, rden], w=rr_a(blk))
            qs = kb.sb([NS, 1024], BF16, stack=st2, name="qs")
            for half in range(2):
                pa = kb.ps()
                for kc in range(8):
                    kb.op("pe", lambda e, kc=kc: e.matmul(pa[0:NS, :], hT[:, kc, NOWN * 128:TOK], wq[:, kc, half * 512:(half + 1) * 512],
                                                          start=(kc == 0), stop=(kc == 7)), r=[h_r[16], wq], w=[pa])
                kb.op("act", lambda e: e.activation(out=qs[:, half * 512:(half + 1) * 512], in_=pa[0:NS, :], func=AF.Copy, scale=1.0 / 16), r=[pa], w=[qs])
            selb = kb.sb([NS, NS, 128], BF16, stack=st2, name="selb")
            kb.op("dve", lambda e: e.tensor_copy(selb[:], bcast(ident_f[0:NS, 0:NS].unsqueeze(2), [NS, NS, 128])), r=[ident_f], w=[selb])
            ckv = [kb.sb([128, 2, 2, 1024], F32, stack=st2, name="ckv0")] * 2
            sc = kb.sb([128, 8], F32, stack=st2, name="sc")
            rd4 = kb.sb([128, 4], F32, stack=st2, name="rd4")
            for si in range(NS):
                kv = ckv[si % 2]
                kb.dma("sp", [(kv[:, mt, :, :], cmem[si, mt * 128:(mt + 1) * 128, :, :]) for mt in range(2)], w=[kv])
                pq = [kb.ps(), kb.ps()]
                for half in range(2):
                    kb.op("pe", lambda e: e.matmul(pq[half][:, :], selb[:, si, :], qs[:, half * 512:(half + 1) * 512], start=True, stop=True),
                          r=[selb, qs], w=[pq[half]])
                for half in range(2):
                    kb.op("dve", lambda e: e.tensor_tensor(out=kv[:, :, 0, half * 512:(half + 1) * 512], in0=kv[:, :, 0, half * 512:(half + 1) * 512],
                                                           in1=bcast(pq[half][:, :].unsqueeze(1), [128, 2, 512]), op=ALU.mult), r=[kv, pq[half]], w=[kv])
                kb.op("dve", lambda e: e.tensor_reduce(out=sc[:, :].rearrange("p (m h) -> p m h", h=4), in_=kv[:, :, 0, :].rearrange("p m (h d) -> p m h d", h=4), axis=AX.X, op=ALU.add), r=[kv], w=[sc])
                kb.op("act", lambda e: e.activation(out=sc[:, :], in_=sc[:, :], func=AF.Exp), r=[sc], w=[sc])
                pd = kb.ps()
                for mt in range(2):
                    kb.op("pe", lambda e, mt=mt: e.matmul(pd[:, 0:4], onesf[:, :], sc[:, mt * 4:(mt + 1) * 4], start=(mt == 0), stop=(mt == 1)), r=[onesf, sc], w=[pd])
                kb.op("dve", lambda e: e.reciprocal(rd4[:, :], pd[:, 0:4]), r=[pd], w=[rd4])
                po = kb.ps()
                for c in range(8):
                    for mt in range(2):
                        kb.op("pe", lambda e, mt=mt: e.matmul(po[:, c:c + 1], kv[:, mt, 1, c * 128:(c + 1) * 128], sc[:, mt * 4 + c // 2: mt * 4 + c // 2 + 1],
                                                              start=(mt == 0), stop=(mt == 1)), r=[kv, sc], w=[po])
                kb.op("dve", lambda e: e.tensor_tensor(out=aT[:, :, NOWN * 128 + si].rearrange("p (h j) -> p h j", j=2),
                                                       in0=po[:, 0:8].rearrange("p (h j) -> p h j", j=2),
                                                       in1=bcast(rd4[:, :].unsqueeze(2), [128, 4, 2]), op=ALU.mult), r=[po, rd4], w=[a_r[16]])
        kb.barrier()
        linear_add(aT, a_r, wo)
        if "2" in DBG:
            for i, (t0, n) in enumerate(tiles):
                if i < NOWN:
                    kb.dma("sp", [(o_y[i, :, :], xres[:, i, :])], r=[xr[i]], final=True)
                else:
                    kb.dma("sp", [(o_ys[:, :], xres[0:NS, i, :])], r=[xr[i]], final=True)
        norm_all(3)
        wupS = w_up.rearrange("(k p) n -> p k n", p=128)
        wdnS = w_down.rearrange("(c p) n -> p c n", p=128)
        aF = kb.sb([128, 8, 512], BF16, stack=st, name="aF")
        sq = [kb.sb([128, 512], F32, stack=st) for _ in range(2)]
        for qtr in range(4):
            wu, wd = wA[qtr % 2], wA[(qtr + 1) % 2]
            kb.dma("pool", [(wu[:, k, :], wupS[:, k, qtr * 1024:(qtr + 1) * 1024]) for k in range(8)], w=[wu])
            kb.dma("pool", [(wd[:, k, :], wdnS[:, qtr * 8 + k, :]) for k in range(8)], w=[wd])
            for blk in range(5):
                b0 = blk * 512
                bw = 512 if blk < 4 else NS
                rr = [h_r[4 * blk + j] for j in range(4)] if blk < 4 else [h_r[16]]
                for fc in range(8):
                    pa = kb.ps()
                    for kc in range(8):
                        kb.op("pe", lambda e, kc=kc: e.matmul(pa[:, 0:bw], wu[:, kc, fc * 128:(fc + 1) * 128], hT[:, kc, b0:b0 + bw],
                                                              start=(kc == 0), stop=(kc == 7)), r=rr + [wu], w=[pa])
                    sq_ = sq[fc % 2]
                    kb.op("act", lambda e: e.activation(out=sq_[:, 0:bw], in_=pa[:, 0:bw], func=AF.Square), r=[pa], w=[sq_])
                    kb.op("dve", lambda e: e.scalar_tensor_tensor(out=aF[:, fc, 0:bw], in0=pa[:, 0:bw], scalar=0.0, in1=sq_[:, 0:bw],
                                                                  op0=ALU.is_gt, op1=ALU.mult), r=[pa, sq_], w=[aF])
                tl = range(4 * blk, 4 * blk + 4) if blk < 4 else [16]
                for i in tl:
                    t0, n = tiles[i]
                    l0 = t0 - b0
                    for half in range(2):
                        pa = kb.ps()
                        for fc in range(8):
                            kb.op("pe", lambda e, fc=fc: e.matmul(pa[0:n, :], aF[:, fc, l0:l0 + n], wd[:, fc, half * 512:(half + 1) * 512],
                                                                  start=(fc == 0), stop=(fc == 7)), r=[aF, wd], w=[pa])
                        kb.op("dve", lambda e: e.tensor_tensor(out=xres[0:n, i, half * 512:(half + 1) * 512], in0=pa[0:n, :],
                                                               in1=xres[0:n, i, half * 512:(half + 1) * 512], op=ALU.add), r=[pa, xr[i]], w=[xr[i]])
        if "3" in DBG:
            for i, (t0, n) in enumerate(tiles):
                if i < NOWN:
                    kb.dma("sp", [(o_y[i, :, :], xres[:, i, :])], r=[xr[i]], final=True)
                else:
                    kb.dma("sp", [(o_ys[:, :], xres[0:NS, i, :])], r=[xr[i]], final=True)
        gfin = kb.sb([128, D], F32, stack=st, name="gfin")
        kb.dma("sp", [(gfin[:], bass.AP(ln_fin.tensor, 0, [[0, 128], [1, D]]))], w=[gfin])
        yo = [kb.sb([128, D], F32, stack=st) for _ in range(2)]
        for i, (t0, n) in enumerate(tiles):
            _, ss, rstd, xbf = stg[i % 2]
            y = yo[i % 2]
            kb.op("act", lambda e: e.activation(out=y[0:n, :], in_=xres[0:n, i, :], func=AF.Square, accum_out=ss[0:n, 0:1]), r=[xr[i]], w=[y, ss])
            kb.op("act", lambda e: e.activation(out=rstd[0:n, :], in_=ss[0:n, :], func=AF.Sqrt, scale=1.0 / D, bias=epsc[0:n, 0:1]), r=[ss, epsc], w=[rstd])
            kb.op("dve", lambda e: e.reciprocal(rstd[0:n, :], rstd[0:n, :]), r=[rstd], w=[rstd])
            kb.op("dve", lambda e: e.scalar_tensor_tensor(out=y[0:n, :], in0=xres[0:n, i, :], scalar=rstd[0:n, 0:1], in1=gfin[0:n, :],
                                                          op0=ALU.mult, op1=ALU.mult), r=[xr[i], rstd, gfin], w=[y])
            if "1" not in DBG and "2" not in DBG and "3" not in DBG:
                if i < NOWN:
                    kb.dma("sp", [(o_y[i, :, :], y[:, :])], r=[y], final=True)
                else:
                    kb.dma("sp", [(o_ys[:, :], y[0:NS, :])], r=[y], final=True)
        kb.barrier()

    kb.finish()
    return nc


_NC = None


def kernel(**inp):
    global _NC
    f = lambda k: np.ascontiguousarray(np.asarray(inp[k]))
    x_prompt, x_sample, mem_prompt = f("x_prompt"), f("x_sample"), f("mem_prompt")
    cache_win, sconv = f("cache_win_kv"), f("state_dn_conv")
    w_in = f("w_in")[0]
    lnv = np.zeros((128, 6, 8), np.float32)
    for i, k in enumerate(("ln_mix", "ln_mem", "ln_memkv", "ln_ffn")):
        lnv[:, i, :] = f(k)[0].reshape(8, 128).T
    lnv[:, 4, :] = f("ln_final").reshape(8, 128).T
    abrep = np.ascontiguousarray(np.tile(np.stack([f("dn_a_log")[0], f("dn_dt_bias")[0]], axis=1), (NS, 1)))
    dnnorm = np.ascontiguousarray(np.tile(f("dn_norm")[0][None, :], (128, 1)))
    perm = list(range(512))
    for j in range(4):
        perm += [512 + j * 64 + d for d in range(64)] + [512 + (j + 4) * 64 + d for d in range(64)]
    perm = np.array(perm)
    w_out_p = np.ascontiguousarray(f("w_out")[0][perm])
    ii = np.arange(128)
    cols = [(ii[:, None] <= ii[None, :]), np.where(ii[None, :] <= ii[:, None], 0.0, -1e30), np.where(ii[None, :] >= ii[:, None], 0.0, -1e30),
            (ii[None, :] < ii[:, None]), (ii[:, None] // 64 == ii[None, :] // 64)]
    for lv in range(7):
        cols.append((ii[:, None] // (2 << lv) == ii[None, :] // (2 << lv)) & (ii[:, None] // (1 << lv) != ii[None, :] // (1 << lv)))
    cw = f("dn_conv_w")[0]
    cols.append(cw.T.reshape(12, 128, 4).transpose(1, 0, 2).reshape(128, 48))
    cols.append(np.tile(f("dn_a_log")[0][None, :], (128, 1)))
    cols.append(np.tile(f("dn_dt_bias")[0][None, :], (128, 1)))
    cols.append((ii < 64)[:, None])
    cols.append((ii >= 64)[:, None])
    dnc = np.ascontiguousarray(np.concatenate([np.asarray(c, np.float32) for c in cols], axis=1))
    NEG = -30000.0
    w1 = f("cmp_w1")[0]; pe = f("cmp_pe")[0]
    w1rep = np.ascontiguousarray(np.tile(w1.reshape(2, 64, 64, 128).transpose(1, 0, 2, 3), (2, 1, 1, 1)))
    perep = np.ascontiguousarray(np.tile(np.repeat(pe.transpose(1, 0, 2)[:, :, None, :], 2, axis=2).reshape(64, 256), (2, 1)))
    eom = np.ascontiguousarray(np.stack([(ii < 64), (ii >= 64)], axis=1).astype(np.float32))
    b1in = np.ascontiguousarray(f("cmp_b1")[0].T)
    w2in = np.ascontiguousarray(f("cmp_w2")[0].transpose(1, 0, 2))
    qperm = []
    for j in range(4):
        qperm += [OFF_Q + j * 64 + d for d in range(64)] + [OFF_Q + (j + 4) * 64 + d for d in range(64)]
    wq_p = np.ascontiguousarray(w_in[:, qperm])
    keys = np.arange(NT * 128)
    nprime = ((keys % 128) // 64) * NT + keys // 128
    Ein = np.ascontiguousarray((np.arange(66)[:, None] == nprime[None, :]).astype(np.float32))
    tri = np.where(ii[:, None] <= ii[None, :], 0.0, NEG)
    wlo = np.where(ii[:, None] > ii[None, :], 0.0, NEG)
    cnsa_h = f("cache_nsa_kv").reshape(2560 * 128, 512)
    iotap_h = np.arange(128, dtype=np.float32).reshape(128, 1)
    sconst_h = np.zeros((1, 512), np.float32)
    sconst_h[0, 0:64] = 1.0
    sconst_h[0, 128 + 64:256] = 1.0
    sconst_h[0, 256] = 1000.0
    sconst_h[0, 256 + 31] = 1000.0
    sconst_h[0, 256 + 32] = 1000.0

    def nsa_consts(h):
        sh = 1 - h
        mk3 = np.stack([np.tile(tri, (1, 4)), np.tile(wlo, (1, 4)), np.full((128, 512), NEG if sh == 1 else 0.0)], axis=1).astype(np.float32)
        npr = np.arange(66)
        nglob = 2 * (npr % NT - sh) + npr // NT
        cmk = np.zeros((NOWN, 66, 512), np.float32)
        addc = np.zeros((NOWN, 128, 2, 66), np.float32)
        for i in range(NOWN):
            qpos = (2 * i + 1 - sh) * 128 + ii
            vis = (nglob[:, None] >= 0) & (64 * (nglob[:, None] + 1) - 1 <= qpos[None, :])
            cmk[i] = np.tile(np.where(vis, 0.0, NEG), (1, 4))
            cur = qpos // 64
            valid = (nglob[None, :] >= 0) & (nglob[None, :] <= cur[:, None]) & (nglob[None, :] < 64)
            forced = valid & ((nglob[None, :] == 0) | (cur[:, None] - nglob[None, :] < 2))
            addc[i, :, 0, :] = np.where(valid, np.where(forced, 1000.0, 0.0), -1.0)
            addc[i, :, 1, :] = np.where(valid, 0.0, NEG)
        return {"w1rep": w1rep, "perep": perep, "eomin": eom, "b1in": b1in, "w2in": w2in, "wq_p": wq_p, "Ein": Ein,
                "mk3in": np.ascontiguousarray(mk3), "cmkin": cmk, "addcin": addc}

    in_maps = []
    for c in range(8):
        b, h = c // 2, c % 2
        sh = 1 - h
        xpc = np.zeros((NT * 128, D), np.float32)
        xpc[sh * 128: sh * 128 + 4096] = x_prompt[b]
        sl = slice(c * NS, (c + 1) * NS)
        in_maps.append({
            "xp": xpc, "xs": x_sample[sl, 0], "memp": mem_prompt[b], "w_in": w_in,
            "w_mem_kv": f("w_mem_kv")[0], "lnv": lnv,
            "cwin": cache_win[0, sl].reshape(NS, 512, 256), "sconv": sconv[0, sl],
            "conv_w": f("dn_conv_w")[0], "abrep": abrep, "dnnorm": dnnorm,
            "srec": f("state_dn_rec")[0, sl].reshape(128, 64, 64),
            "dncin": dnc, "w_out": w_out_p, **nsa_consts(c % 2),
            "cnsa": cnsa_h, "ptab": np.ascontiguousarray(f("page_table")[sl].reshape(1, NS * 16).astype(np.int32)), "iotap": iotap_h, "sconst": sconst_h, "w_mem_q": f("w_mem_q")[0], "w_mem_o": f("w_mem_o")[0], "w_up": f("w_up")[0], "w_down": f("w_down")[0],
            "ln_fin": f("ln_final").reshape(1, D), "cmem": f("cache_mem_kv")[0, sl].reshape(NS, 256, 2, 1024),
        })
    if "m" in DBG or "n" in DBG:
        dm = inp["_dbg_mix"]
        for c in range(8):
            in_maps[c]["dbg_mixT"] = np.ascontiguousarray(dm[c][:, perm].reshape(-1, 8, 128).transpose(2, 1, 0))
    if _NC is None:
        _NC = build()
    res = run_bass_kernel_spmd(_NC, in_maps, core_ids=list(range(8)))
    R = res.results
    B, S = 4, 4096
    y_prompt = np.zeros((B, S, D), np.float32)
    y_sample = np.zeros((128, 1, D), np.float32)
    p_nsa = np.zeros((1, B, S, 512), np.float32)
    p_win = np.zeros((1, B, 512, 256), np.float32)
    p_mem = np.zeros((1, B, 256, 2048), np.float32)
    p_conv = np.zeros((1, B, 3, 1536), np.float32)
    p_rec = np.zeros((1, B, 8, 64, 64), np.float32)
    s_nsa = np.zeros((1, 128, 512), np.float32)
    s_win = np.zeros((1, 128, 512, 256), np.float32)
    s_conv = np.zeros((1, 128, 3, 1536), np.float32)
    s_rec = np.zeros((1, 128, 8, 64, 64), np.float32)
    for c in range(8):
        b, h = c // 2, c % 2
        sh = 1 - h
        r = R[c]
        sl = slice(c * NS, (c + 1) * NS)
        pn = p_nsa[0, b].reshape(32, 128, 512)
        for i in range(NOWN):
            pn[2 * i + 1 - sh] = r["o_nsa"][i]
        if h == 0:
            p_rec[0, b] = r["o_prec"]
            p_win[0, b] = r["o_win"][1:5].reshape(512, 256)
            p_mem[0, b] = r["o_mem"]
            p_conv[0, b] = r["o_conv"]
        s_nsa[0, sl] = r["o_snsa"]
        s_win[0, sl] = r["o_swin"]
        s_conv[0, sl] = r["o_sconv"]
        s_rec[0, sl] = r["o_srec"].reshape(NS, 8, 64, 64)
        y_sample[sl, 0] = r["o_ys"]
        yp = y_prompt[b].reshape(32, 128, D)
        for i in range(NOWN):
            yp[2 * i + 1 - sh] = r["o_y"][i]
    return (y_prompt, y_sample, p_nsa.reshape(1, B, S, 4, 2, 64), p_win.reshape(1, B, 512, 2, 2, 64),
            p_mem.reshape(1, B, 256, 2, 4, 256), p_conv, p_rec, s_nsa.reshape(1, 128, 1, 4, 2, 64),
            s_win.reshape(1, 128, 512, 2, 2, 64), s_conv, s_rec)
```

```python
import contextlib
import numpy as np
import concourse.bass as bass
import concourse.mybir as mybir
from concourse.bass_utils import run_bass_kernel_spmd

F32 = mybir.dt.float32
BF16 = mybir.dt.bfloat16
I32 = mybir.dt.int32
AF = mybir.ActivationFunctionType
ALU = mybir.AluOpType
AX = mybir.AxisListType

D = 1024
NT = 33
NOWN = 16
NS = 16
IN_COLS = 3368
OFF_Z, OFF_B, OFF_A, OFF_Q, OFF_KV, OFF_G = 1536, 2048, 2056, 2064, 2576, 3344
EPS = 1e-6
import os
DBG = os.environ.get('KDBG', 'abcdpsh')


class Res:
    __slots__ = ("w", "r", "dsem", "dcnt", "name")

    def __init__(self, name=""):
        self.w = {}
        self.r = {}
        self.dsem = None
        self.dcnt = 0
        self.name = name


class T:
    def __init__(self, t, res):
        self.t = t
        self.res = res

    def __getitem__(self, k):
        return self.t[k]


class KB:
    ENG = ("pe", "act", "dve", "pool", "sp")

    def __init__(self):
        self.nc = bass.Bass("TRN2", target_bir_lowering=False)
        nc = self.nc
        self.es = contextlib.ExitStack()
        self.eng = {"pe": nc.tensor, "act": nc.scalar, "dve": nc.vector, "pool": nc.gpsimd, "sp": nc.sync}
        self.sem = {}
        self.cnt = {}
        self.known = {e: {} for e in self.ENG}
        self.semobj = {}
        self.nsem = 0
        for e in self.ENG:
            self._newsem(e)
        self.finals = {}
        self.uid = 0
        self.psum = []
        self.psi = 0
        self.psn = 8

    def _alloc_sem(self):
        self.nsem += 1
        s = self.es.enter_context(self.nc.semaphore(f"s{self.nsem}"))
        key = self.nsem
        self.semobj[key] = s
        return key

    def _newsem(self, e):
        self.sem[e] = self._alloc_sem()
        self.cnt[e] = 0

    def sb(self, shape, dtype, stack=None, name=None):
        self.uid += 1
        st = stack if stack is not None else self.es
        t = st.enter_context(self.nc.sbuf_tensor(name or f"t{self.uid}", list(shape), dtype))
        return T(t, Res(name or f"t{self.uid}"))

    def init_psum(self):
        for i in range(8):
            t = self.es.enter_context(self.nc.psum_tensor(f"ps{i}", [128, 512], F32))
            self.psum.append(T(t, Res(f"ps{i}")))

    def ps(self):
        p = self.psum[self.psi % self.psn]
        self.psi += 1
        return p

    def _wait(self, e, deps):
        eng = self.eng[e]
        kn = self.known[e]
        for key, val in deps.items():
            if e == "pe" and key == self.sem["pe"]:
                continue
            if kn.get(key, 0) >= val:
                continue
            eng.wait_ge(self.semobj[key], val)
            kn[key] = val

    @staticmethod
    def _merge(d, key, val):
        if d.get(key, 0) < val:
            d[key] = val

    def _deps(self, r, w):
        deps = {}
        for x in r:
            for k, v in x.w.items():
                self._merge(deps, k, v)
        for x in w:
            for k, v in x.w.items():
                self._merge(deps, k, v)
            for k, v in x.r.items():
                self._merge(deps, k, v)
        return deps

    @staticmethod
    def _res(lst):
        return [x.res if isinstance(x, T) else x for x in lst]

    def op(self, e, fn, r=(), w=()):
        r = self._res(r)
        w = self._res(w)
        self._wait(e, self._deps(r, w))
        if self.cnt[e] >= 30000:
            self._newsem(e)
        ins = fn(self.eng[e])
        self.cnt[e] += 1
        ins.then_inc(self.semobj[self.sem[e]], 1)
        key, val = self.sem[e], self.cnt[e]
        for x in r:
            self._merge(x.r, key, val)
        for x in w:
            x.w = {key: val}
            x.r = {}
        return ins

    def dma(self, q, pairs, r=(), w=(), final=False, **kw):
        r = self._res(r)
        w = self._res(w)
        self._wait(q, self._deps(r, w))
        owner = w[0] if w else (r[0] if r else None)
        if owner is None:
            owner = Res("anon")
        if owner.dsem is None or owner.dcnt >= 30000 * 16:
            owner.dsem = self._alloc_sem()
            owner.dcnt = 0
        for (o, i) in pairs:
            self.eng[q].dma_start(out=o, in_=i, **kw).then_inc(self.semobj[owner.dsem], 16)
            owner.dcnt += 16
        key, val = owner.dsem, owner.dcnt
        for x in r:
            self._merge(x.r, key, val)
        for x in w:
            x.w = {key: val}
            x.r = {}
        if final:
            self._merge(self.finals, key, val)

    def barrier(self):
        allv = {self.sem[e]: self.cnt[e] for e in self.ENG if self.cnt[e] > 0}
        for e in self.ENG:
            self._wait(e, dict(allv))

    def finish(self):
        self._wait("sp", dict(self.finals))
        self.barrier()
        self.es.close()


def kb_gather(kb, PG, pti, cnsa, si):
    deps = kb._deps([pti.res], [PG.res])
    kb._wait("pool", deps)
    if PG.res.dsem is None:
        PG.res.dsem = kb._alloc_sem()
    for pg_ in range(16):
        c = si * 16 + pg_
        kb.nc.gpsimd.indirect_dma_start(out=PG[:, pg_, :], out_offset=None, in_=cnsa[0:128, :],
                                        in_offset=bass.IndirectOffsetOnAxis(ap=pti[:, c:c + 1], axis=0)).then_inc(kb.semobj[PG.res.dsem], 16)
        PG.res.dcnt += 16
    PG.res.w = {PG.res.dsem: PG.res.dcnt}
    PG.res.r = {}
    kb._merge(pti.res.r, PG.res.dsem, PG.res.dcnt)


def bcast(ap, shape):
    return ap.to_broadcast(list(shape))


def build():
    kb = KB()
    nc = kb.nc
    kb.init_psum()

    def din(name, shape, dt=F32):
        return nc.dram_tensor(name, list(shape), dt, kind="ExternalInput").ap()

    def dout(name, shape, dt=F32):
        return nc.dram_tensor(name, list(shape), dt, kind="ExternalOutput").ap()

    xp = din("xp", [NT * 128, D])
    xs = din("xs", [NS, D])
    memp = din("memp", [256, D])
    w_in = din("w_in", [D, IN_COLS])
    w_mem_kv = din("w_mem_kv", [D, 2048])
    lnv = din("lnv", [128, 6, 8])
    cwin = din("cwin", [NS, 512, 256])
    sconv = din("sconv", [NS, 3, 1536])
    conv_w = din("conv_w", [4, 1536])
    abrep = din("abrep", [128, 2])
    dnnorm = din("dnnorm", [128, 64])
    srec = din("srec", [128, 64, 64])

    o_nsa = dout("o_nsa", [NOWN, 128, 512])
    o_win = dout("o_win", [5, 128, 256])
    o_mem = dout("o_mem", [256, 2048])
    o_conv = dout("o_conv", [3, 1536])
    o_snsa = dout("o_snsa", [NS, 512])
    o_swin = dout("o_swin", [NS, 512, 256])
    o_sconv = dout("o_sconv", [NS, 3, 1536])
    o_srec = dout("o_srec", [128, 64, 64])
    o_y = dout("o_y", [NOWN, 128, D])
    o_prec = dout("o_prec", [8, 64, 64])
    dncin = din("dncin", [128, 1602])
    if "s" in DBG:
        cnsa = din("cnsa", [2560 * 128, 512])
        ptab = din("ptab", [1, NS * 16], I32)
        iotap = din("iotap", [128, 1])
        sconst = din("sconst", [1, 512])
    if "p" in DBG:
        w1rep = din("w1rep", [128, 2, 64, 128])
        perep = din("perep", [128, 256])
        eomin = din("eomin", [128, 2])
        b1in = din("b1in", [128, 2])
        w2in = din("w2in", [128, 2, 64])
        wq_p = din("wq_p", [D, 512])
        Ein = din("Ein", [66, NT * 128])
        mk3in = din("mk3in", [128, 3, 512])
        cmkin = din("cmkin", [NOWN, 66, 512])
        addcin = din("addcin", [NOWN, 128, 2, 66])
    o_ys = dout("o_ys", [NS, D])
    w_out = din("w_out", [D, D])
    w_mem_q = din("w_mem_q", [D, D])
    w_mem_o = din("w_mem_o", [D, D])
    w_up = din("w_up", [D, 4096])
    w_down = din("w_down", [4096, D])
    ln_fin = din("ln_fin", [1, D])
    cmem = din("cmem", [NS, 256, 2, 1024])
    if "m" in DBG or "n" in DBG:
        dbg_mixT = din("dbg_mixT", [128, 8, NOWN * 128 + NS])

    ident_bf = kb.sb([128, 128], BF16, name="ident_bf")
    ident_f = kb.sb([128, 128], F32, name="ident_f")
    kb.op("pool", lambda e: e.memset(ident_f[:], 0.0), w=[ident_f])
    kb.op("pool", lambda e: e.affine_select(out=ident_f[:], in_=ident_f[:], pattern=[[1, 128]],
                                            compare_op=ALU.not_equal, fill=1.0, base=0, channel_multiplier=-1),
          r=[ident_f], w=[ident_f])
    kb.op("dve", lambda e: e.tensor_copy(ident_bf[:], ident_f[:]), r=[ident_f], w=[ident_bf])
    epsc = kb.sb([128, 1], F32, name="epsc")
    kb.op("pool", lambda e: e.memset(epsc[:], EPS), w=[epsc])
    onec = kb.sb([128, 1], F32, name="onec")
    kb.op("pool", lambda e: e.memset(onec[:], 1.0), w=[onec])
    lns = kb.sb([128, 6, 8], F32, name="lns")
    kb.dma("sp", [(lns[:], lnv[:, :, :])], w=[lns])

    def norm_transpose(src_ap, rows, stage, dst_ap, dst_res, ln_idx, gq="sp", sb_src=None):
        xst, ss, rstd, xbf = stage
        if sb_src is None:
            kb.dma(gq, [(xst[0:rows, :], src_ap)], w=[xst])
        else:
            xst = sb_src
        xin = src_ap if sb_src is not None else xst[0:rows, :]
        kb.op("act", lambda e: e.activation(out=xbf[0:rows, :], in_=xin, func=AF.Square,
                                            accum_out=ss[0:rows, 0:1]), r=[xst], w=[xbf, ss])
        kb.op("act", lambda e: e.activation(out=rstd[0:rows, :], in_=ss[0:rows, :], func=AF.Sqrt, scale=1.0 / D,
                                            bias=epsc[0:rows, 0:1]), r=[ss, epsc], w=[rstd])
        kb.op("dve", lambda e: e.reciprocal(rstd[0:rows, :], rstd[0:rows, :]), r=[rstd], w=[rstd])
        kb.op("act", lambda e: e.activation(out=xbf[0:rows, :], in_=xin, func=AF.Copy,
                                            scale=rstd[0:rows, 0:1]), r=[xst, rstd], w=[xbf])
        pt = kb.ps()
        ptv = pt[:].bitcast(BF16)
        for kc in range(8):
            kb.op("pe", lambda e, kc=kc: e.transpose(ptv[:, kc * 128:kc * 128 + rows],
                                                     xbf[0:rows, kc * 128:(kc + 1) * 128], ident_bf[0:rows, 0:rows]),
                  r=[xbf, ident_bf], w=[pt])
        pv3 = ptv.rearrange("p (k t) -> p k t", k=8)[:, :, 0:rows]
        kb.op("dve", lambda e: e.tensor_tensor(out=dst_ap, in0=pv3, in1=bcast(lns[:, ln_idx, :].unsqueeze(2), [128, 8, rows]),
                                               op=ALU.mult), r=[pt, lns], w=[dst_res])

    def rr_a(blk):
        return [mix_r[4 * blk + j] for j in range(4)]

    def load_w(dram_ap, ncols, stack, name):
        wt = kb.sb([128, 8, ncols], BF16, stack=stack, name=name)
        src = dram_ap.rearrange("(k p) n -> p k n", p=128)
        kb.dma("pool", [(wt[:, k, :], src[:, k, :]) for k in range(8)], w=[wt])
        return wt

    TOK = NOWN * 128 + NS
    tiles = [(i * 128, 128) for i in range(NOWN)] + [(NOWN * 128, NS)]
    mixT = kb.sb([128, 8, TOK], BF16, name="mixT")
    mix_r = [Res(f"mix{i}") for i in range(17)]
    front = contextlib.ExitStack()
    scr_sproj = nc.dram_tensor("scr_sproj", [NS, IN_COLS], F32).ap()
    r_sps = Res("scr_sproj")
    xnT = kb.sb([128, 8, NT * 128], BF16, name="xnT", stack=front)
    xnT_res = [Res(f"xnT{t}") for t in range(NT)]
    xsT = kb.sb([128, 8, NS], BF16, name="xsT", stack=front)

    with contextlib.ExitStack() as st:
        stages = []
        for i in range(3):
            stages.append((kb.sb([128, D], F32, stack=st), kb.sb([128, 1], F32, stack=st),
                           kb.sb([128, 1], F32, stack=st), kb.sb([128, D], BF16, stack=st)))
        for t in range(NT):
            norm_transpose(xp[t * 128:(t + 1) * 128, :], 128, stages[t % 3],
                           xnT[:, :, t * 128:(t + 1) * 128], xnT_res[t], 0)
        norm_transpose(xs[:, :], NS, stages[NT % 3], xsT[:, :, :], xsT.res, 0)
        kb.barrier()

    with contextlib.ExitStack() as st:
        sproj = kb.sb([NS, IN_COLS], F32, name="sproj", stack=st)
        co = kb.sb([128, 1536], F32, stack=st)
        wsl = [kb.sb([128, 8, 512], BF16, stack=st) for _ in range(2)]
        wsrc = w_in.rearrange("(k p) n -> p k n", p=128)
        for cb in range(7):
            c0 = cb * 512
            cw = min(512, IN_COLS - c0)
            wt = wsl[cb % 2]
            kb.dma("pool", [(wt[:, k, 0:cw], wsrc[:, k, c0:c0 + cw]) for k in range(8)], w=[wt])
            pa = kb.ps()
            for kc in range(8):
                kb.op("pe", lambda e, kc=kc: e.matmul(pa[0:NS, 0:cw], xsT[:, kc, :], wt[:, kc, 0:cw],
                                                      start=(kc == 0), stop=(kc == 7)), r=[xsT, wt], w=[pa])
            kb.op("dve", lambda e: e.tensor_copy(sproj[:, c0:c0 + cw], pa[0:NS, 0:cw]), r=[pa], w=[sproj])
            if cb < 3:
                pb = kb.ps()
                for kc in range(8):
                    kb.op("pe", lambda e, kc=kc: e.matmul(pb[:, :], xnT[:, kc, (NT - 1) * 128:NT * 128], wt[:, kc, :],
                                                          start=(kc == 0), stop=(kc == 7)), r=[xnT_res[NT - 1], wt], w=[pb])
                kb.op("act", lambda e: e.activation(out=co[:, c0:c0 + 512], in_=pb[:, :], func=AF.Copy), r=[pb], w=[co])
        kb.dma("sp", [(o_conv[:, :], co[125:128, :])], r=[co], final=True)
        kb.dma("sp", [(o_sconv[:, 2, :], sproj[:, 0:1536])], r=[sproj], final=True)
        kb.dma("sp", [(scr_sproj[:, :], sproj[:, :])], r=[sproj], w=[r_sps])
        kb.dma("pool", [(o_sconv[:, 0:2, :], sconv[:, 1:3, :])], final=True)
        kb.barrier()

    if "d" in DBG:
      with contextlib.ExitStack() as st:
        wdn = load_w(w_in[:, 0:1536], 1536, st, "wdn")
        wz = load_w(w_in[:, OFF_Z:OFF_Z + 512], 512, st, "wz")
        wba = load_w(w_in[:, OFF_B:OFF_B + 16], 16, st, "wba")
        NDC = 5 * 128 + 7 * 128 + 48 + 16 + 2
        dnc = kb.sb([128, NDC], F32, stack=st, name="dnc")
        kb.dma("sp", [(dnc[:], dncin[:, :])], w=[dnc])
        UT, NMI, NMT, ST01, BLK = (dnc[:, i * 128:(i + 1) * 128] for i in range(5))
        MO = dnc[:, 640:640 + 896].rearrange("p (l q) -> p l q", l=7)
        CW = dnc[:, 1536:1584].rearrange("p (f j) -> p f j", j=4)
        ALOG, DTB = dnc[:, 1584:1592], dnc[:, 1592:1600]
        EOM = (dnc[:, 1600:1601], dnc[:, 1601:1602])
        gnd = kb.sb([128, 64], F32, stack=st, name="gnd")
        kb.dma("sp", [(gnd[:], dnnorm[:, :])], w=[gnd])
        onesf = kb.sb([128, 128], F32, stack=st, name="onesf_d")
        kb.op("pool", lambda e: e.memset(onesf[:], 1.0), w=[onesf])
        negea = kb.sb([128, 8], F32, stack=st, name="negea")
        kb.op("act", lambda e: e.activation(out=negea[:], in_=ALOG, func=AF.Exp), r=[dnc], w=[negea])
        kb.op("dve", lambda e: e.tensor_scalar(out=negea[:], in0=negea[:], scalar1=-1.0, scalar2=None, op0=ALU.mult), r=[negea], w=[negea])
        eps64 = kb.sb([128, 1], F32, stack=st, name="eps64")
        kb.op("pool", lambda e: e.memset(eps64[:], 64e-6), w=[eps64])
        pre = kb.sb([128, 12, 131], BF16, stack=st, name="pre")
        kb.op("pool", lambda e: e.memset(pre[:], 0.0), w=[pre])
        S = kb.sb([128, 4, 64], F32, stack=st, name="S")
        kb.op("pool", lambda e: e.memset(S[:], 0.0), w=[S])
        F = lambda shape, name: kb.sb(shape, F32, stack=st, name=name)
        cT, tmpc = F([128, 12, 128], "cT"), F([128, 12, 128], "tmpc")
        rs, qkT = F([128, 8, 128], "rs"), F([128, 8, 128], "qkT")
        k_tok, v_tok = F([128, 8, 64], "k_tok"), F([128, 8, 64], "v_tok")
        vkb, kd = kb.sb([128, 8, 128], BF16 if "h" in DBG else F32, stack=st, name="vkb"), F([128, 8, 64], "kd")
        km = (F([128, 4, 128], "km_e"), F([128, 4, 128], "km_o"))
        Sm = (F([128, 2, 64], "Sm_e"), F([128, 2, 64], "Sm_o"))
        w_sb = F([128, 4, 64], "w_sb")
        sm = F([128, 80], "smd")
        BETA, G, GC, EG, EGL, EKD, BG, XX = (sm[:, i * 8:(i + 1) * 8] for i in range(8))
        eglS = sm[:, 64:68]
        G4 = lambda nm: F([128, 4, 128], nm)
        B4 = lambda nm: kb.sb([128, 4, 128], BF16, stack=st, name=nm)
        Dm, DTm, AqkT = B4("Dm"), B4("DTm"), G4("AqkT")
        if "h" not in DBG:
            IX, IX2 = G4("IX"), G4("IX2")
        A, AT = T(rs[:, 0:4, :], rs.res), T(rs[:, 4:8, :], rs.res)
        if "h" in DBG:
            Ta, Tb, Ua, Ub, Nn, NTn = (kb.sb([128, 4, 128], BF16, stack=st, name=n) for n in ("Ta", "Tb", "Ua", "Ub", "Nn", "NTn"))
            IX, IX2 = (kb.sb([128, 4, 128], BF16, stack=st, name=n) for n in ("IXb", "IX2b"))
            diag, t1 = T(cT[:, 0:4, :], cT.res), T(cT[:, 4:8, :], cT.res)
        else:
            diag, t1 = IX, IX2
            Ta, Tb, Ua = (T(cT[:, 4 * i:4 * i + 4, :], cT.res) for i in range(3))
            Ub, Nn, NTn = (T(tmpc[:, 4 * i:4 * i + 4, :], tmpc.res) for i in range(3))
        u_sb, wT, vnew = F([128, 4, 64], "u_sb"), F([128, 2, 128], "wT"), F([128, 4, 64], "vnew")
        o_t = F([128, 8, 64], "o_t")
        obf = T(Dm[:].rearrange("p a b -> p (a b)"), Dm.res)
        HB = [(diag, t1, Dm, DTm, A, AT, AqkT, Ta, Tb, Ua, Ub, Nn, NTn, IX, IX2, u_sb, w_sb, wT, vnew, Sm)]
        HB.append(tuple([T(tmpc[:, 0:4, :], tmpc.res), T(tmpc[:, 4:8, :], tmpc.res), B4("Dm_hb"), B4("DTm_hb"), T(tmpc[:, 8:12, :], tmpc.res), T(cT[:, 8:12, :], cT.res), G4("AqkT_hb")]
                        + [kb.sb([128, 4, 128], BF16, stack=st, name=n + "_hb") for n in ("Ta", "Tb", "Ua", "Ub", "Nn", "NTn", "IX", "IX2")]
                        + [T(k_tok[:, 0:4, :], k_tok.res), T(k_tok[:, 4:8, :], k_tok.res), T(v_tok[:, 4:8, :].rearrange("p (a b) d -> p a (b d)", a=2), v_tok.res),
                           T(v_tok[:, 0:4, :], v_tok.res),
                           (F([128, 2, 64], "Sm_e1"), F([128, 2, 64], "Sm_o1"))]))
        o2 = T(vkb[:, :, 0:64], vkb.res) if "h" not in DBG else T(tmpc[:, 0:4, :].rearrange("p a (b c) -> p (a b) c", c=64), tmpc.res)
        zs = T(kd[:].rearrange("p h d -> p (h d)"), kd.res)
        ro = F([128, 8], "ro")
        idb4 = bcast(ident_f[:, :].unsqueeze(1), [128, 4, 128])
        b4 = lambda ap2: bcast(ap2.unsqueeze(1), [128, 4, 128])
        c4 = lambda ap2: bcast(ap2.unsqueeze(2), [128, 4, 128])
        c64 = lambda ap2, n: bcast(ap2.unsqueeze(2), [128, n, 64])
        V = lambda e: ("dve", e)
        def tt(eng, out, in0, in1, op, r, w):
            kb.op(eng, lambda e: e.tensor_tensor(out=out, in0=in0, in1=in1, op=op), r=r, w=w)
        def cp(out, in_, r, w, func=AF.Copy, **kw):
            kb.op("act", lambda e: e.activation(out=out, in_=in_, func=func, **kw), r=r, w=w)
        for tau in range(int(os.environ.get('KNT', NT))):
            own = tau % 2 == 1
            xt = lambda kc: xnT[:, kc, tau * 128:(tau + 1) * 128]
            xr_ = xnT_res[tau]
            for b3 in range(3):
                pa = kb.ps()
                for f4 in range(4):
                    ft = b3 * 4 + f4
                    for kc in range(8):
                        kb.op("pe", lambda e, kc=kc: e.matmul(pa[:, f4 * 128:(f4 + 1) * 128], wdn[:, kc, ft * 128:(ft + 1) * 128], xt(kc),
                                                              start=(kc == 0), stop=(kc == 7)), r=[xr_, wdn], w=[pa])
                cp(pre[:, b3 * 4:(b3 + 1) * 4, 3:131], pa[:, :].rearrange("p (f t) -> p f t", f=4), [pa], [pre])
            cb = lambda j: bcast(CW[:, :, j].unsqueeze(2), [128, 12, 128])
            tt("dve", cT[:], pre[:, :, 0:128], cb(0), ALU.mult, [pre, dnc], [cT])
            for j in range(1, 4):
                tt("pool", tmpc[:], pre[:, :, j:j + 128], cb(j), ALU.mult, [pre, dnc], [tmpc])
                tt("dve", cT[:], cT[:], tmpc[:], ALU.add, [cT, tmpc], [cT])
            cp(cT[:], cT[:], [cT], [cT], func=AF.Silu)
            kb.op("dve", lambda e: e.tensor_copy(pre[:, :, 0:3], pre[:, :, 128:131]), r=[pre], w=[pre])
            tt("pool", tmpc[:, 0:8, :], cT[:, 0:8, :], cT[:, 0:8, :], ALU.mult, [cT], [tmpc])
            for half in range(2):
                pa = kb.ps()
                kb.op("pe", lambda e: e.matmul(pa[:, :], BLK, tmpc[:, 4 * half:4 * half + 4, :].rearrange("p f t -> p (f t)"), start=True, stop=True),
                      r=[dnc, tmpc], w=[pa])
                cp(rs[:, 4 * half:4 * half + 4, :].rearrange("p f t -> p (f t)"), pa[:, :], [pa, eps64, epsc], [rs], func=AF.Sqrt,
                   scale=(64.0 if half == 0 else 1.0), bias=(eps64[:, 0:1] if half == 0 else epsc[:, 0:1]))
            kb.op("dve", lambda e: e.reciprocal(rs[:], rs[:]), r=[rs], w=[rs])
            tt("dve", qkT[:], cT[:, 0:8, :], rs[:], ALU.mult, [cT, rs], [qkT])
            for (src, f0, dst) in ((qkT, 4, k_tok), (cT, 8, v_tok)):
                pa = kb.ps()
                for f in range(4):
                    kb.op("pe", lambda e, f=f: e.transpose(pa[:, f * 128:(f + 1) * 128], src[:, f0 + f, :], ident_f[:, :]), r=[src, ident_f], w=[pa])
                cp(dst[:].rearrange("p h d -> p (h d)"), pa[:, :], [pa], [dst])
            pb = kb.ps()
            for kc in range(8):
                kb.op("pe", lambda e, kc=kc: e.matmul(pb[:, 0:16], xt(kc), wba[:, kc, :], start=(kc == 0), stop=(kc == 7)), r=[xr_, wba], w=[pb])
            cp(BETA, pb[:, 0:8], [pb], [sm], func=AF.Sigmoid)
            tt("dve", XX, pb[:, 8:16], DTB, ALU.add, [pb, dnc], [sm])
            cp(XX, XX, [sm], [sm], func=AF.Exp)
            cp(XX, XX, [sm, onec], [sm], func=AF.Ln, bias=onec[:, 0:1])
            tt("dve", G, XX, negea[:], ALU.mult, [sm, negea], [sm])
            pg = kb.ps()
            kb.op("pe", lambda e: e.matmul(pg[:, 0:8], UT, G, start=True, stop=True), r=[dnc, sm], w=[pg])
            kb.op("pe", lambda e: e.matmul(pg[:, 8:16], onesf[:, :], G, start=True, stop=True), r=[onesf, sm], w=[pg])
            kb.op("dve", lambda e: e.tensor_copy(GC, pg[:, 0:8]), r=[pg], w=[sm])
            cp(EG, pg[:, 0:8], [pg], [sm], func=AF.Exp)
            cp(EGL, pg[:, 8:16], [pg], [sm], func=AF.Exp)
            tt("dve", XX, pg[:, 8:16], GC, ALU.subtract, [pg, sm], [sm])
            cp(EKD, XX, [sm], [sm], func=AF.Exp)
            tt("dve", BG, BETA, EG, ALU.mult, [sm], [sm])
            kb.op("dve", lambda e: e.tensor_copy(eglS[0:64, :], sm[0:64, 32:40].rearrange("p (a b) -> p a b", b=2)[:, :, 0]), r=[sm], w=[sm])
            kb.op("dve", lambda e: e.tensor_copy(eglS[64:128, :], sm[64:128, 32:40].rearrange("p (a b) -> p a b", b=2)[:, :, 1]), r=[sm], w=[sm])
            tt("dve", vkb[:, :, 0:64], v_tok[:], c64(BETA, 8), ALU.mult, [v_tok, sm], [vkb])
            tt("dve", vkb[:, :, 64:128], k_tok[:], c64(BG, 8), ALU.mult, [k_tok, sm], [vkb])
            for p_ in range(2):
                kb.op("dve", lambda e: e.tensor_scalar(out=km[p_][:], in0=qkT[:, 4:8, :], scalar1=EOM[p_], scalar2=None, op0=ALU.mult), r=[qkT, dnc], w=[km[p_]])
            tt("pool", kd[:], k_tok[:], c64(EKD, 8), ALU.mult, [k_tok, sm], [kd])
            def hg_body(hg, B):
                (diag, t1, Dm, DTm, A, AT, AqkT, Ta, Tb, Ua, Ub, Nn, NTn, IX, IX2, u_sb, w_sb, wT, vnew, Sm) = B
                hs = list(range(4 * hg, 4 * hg + 4))
                gch = sm[:, 16 + 4 * hg:16 + 4 * hg + 4]
                KST = int(os.environ.get('KST', 9))
                tt("dve", diag[:], idb4, c4(gch), ALU.mult, [ident_f, sm], [diag])
                pG = kb.ps()
                kb.op("pe", lambda e: e.matmul(pG[:, :], onesf[:, :], diag[:].rearrange("p h t -> p (h t)"), start=True, stop=True), r=[onesf, diag], w=[pG])
                pG3 = pG[:, :].rearrange("p (h t) -> p h t", h=4)
                tt("dve", t1[:], b4(NMI), pG3, ALU.subtract, [dnc, pG], [t1])
                tt("dve", t1[:], t1[:], c4(gch), ALU.add, [t1, sm], [t1])
                cp(Dm[:], t1[:], [t1], [Dm], func=AF.Exp)
                tt("dve", t1[:], pG3, b4(NMT), ALU.add, [dnc, pG], [t1])
                tt("dve", t1[:], t1[:], c4(gch), ALU.subtract, [t1, sm], [t1])
                cp(DTm[:], t1[:], [t1], [DTm], func=AF.Exp)
                tt("pool", Dm[:], Dm[:], b4(ST01), ALU.mult, [Dm, dnc], [Dm])
                yield
                if KST < 2:
                    return
                pGr, pQK = kb.ps(), kb.ps()
                for i, h in enumerate(hs):
                    kmh = km[h % 2]
                    kb.op("pe", lambda e: e.matmul(pGr[:, i * 128:(i + 1) * 128], kmh[:, h // 2, :], qkT[:, 4 + h // 2, :], start=True, stop=True), r=[kmh, qkT], w=[pGr])
                    kb.op("pe", lambda e: e.matmul(pQK[:, i * 128:(i + 1) * 128], kmh[:, h // 2, :], qkT[:, h // 2, :], start=True, stop=True), r=[kmh, qkT], w=[pQK])
                tt("dve", A[:], pGr[:, :].rearrange("p (h t) -> p h t", h=4), Dm[:], ALU.mult, [pGr, Dm], [A])
                tt("dve", A[:], A[:], c4(sm[:, 4 * hg:4 * hg + 4]), ALU.mult, [A, sm], [A])
                tt("dve", AqkT[:], pQK[:, :].rearrange("p (h t) -> p h t", h=4), DTm[:], ALU.mult, [pQK, DTm], [AqkT])
                yield
                if KST < 3:
                    return
                pa = kb.ps()
                for i in range(4):
                    kb.op("pe", lambda e, i=i: e.transpose(pa[:, i * 128:(i + 1) * 128], A[:, i, :], ident_f[:, :]), r=[A, ident_f], w=[pa])
                cp(AT[:].rearrange("p h t -> p (h t)"), pa[:, :], [pa], [AT])
                T_, U_, Tn, Un = Ta, Ua, Tb, Ub
                tt("pool", Nn[:], A[:], b4(MO[:, 0, :]), ALU.mult, [A, dnc], [Nn])
                tt("dve", T_[:], idb4, Nn[:], ALU.subtract, [ident_f, Nn], [T_])
                tt("pool", NTn[:], AT[:], b4(MO[:, 0, :]), ALU.mult, [AT, dnc], [NTn])
                tt("dve", U_[:], idb4, NTn[:], ALU.subtract, [ident_f, NTn], [U_])
                yield
                for lv in range(1, int(os.environ.get('KLV', 7))):
                    last = lv == 6
                    tt("pool", Nn[:], A[:], b4(MO[:, lv, :]), ALU.mult, [A, dnc], [Nn])
                    pX2 = kb.ps()
                    for i in range(4):
                        kb.op("pe", lambda e, i=i: e.matmul(pX2[:, i * 128:(i + 1) * 128], Nn[:, i, :], U_[:, i, :], start=True, stop=True), r=[Nn, U_], w=[pX2])
                    tt("dve", IX2[:], idb4, pX2[:, :].rearrange("p (h t) -> p h t", h=4), ALU.subtract, [ident_f, pX2], [IX2])
                    pU = kb.ps()
                    for i in range(4):
                        kb.op("pe", lambda e, i=i: e.matmul(pU[:, i * 128:(i + 1) * 128], T_[:, i, :], IX2[:, i, :], start=True, stop=True), r=[T_, IX2], w=[pU])
                    if not last:
                        tt("pool", NTn[:], AT[:], b4(MO[:, lv, :]), ALU.mult, [AT, dnc], [NTn])
                        pX = kb.ps()
                        for i in range(4):
                            kb.op("pe", lambda e, i=i: e.matmul(pX[:, i * 128:(i + 1) * 128], NTn[:, i, :], T_[:, i, :], start=True, stop=True), r=[NTn, T_], w=[pX])
                        tt("dve", IX[:], idb4, pX[:, :].rearrange("p (h t) -> p h t", h=4), ALU.subtract, [ident_f, pX], [IX])
                        pT_ = kb.ps()
                        for i in range(4):
                            kb.op("pe", lambda e, i=i: e.matmul(pT_[:, i * 128:(i + 1) * 128], U_[:, i, :], IX[:, i, :], start=True, stop=True), r=[U_, IX], w=[pT_])
                        cp(Tn[:].rearrange("p h t -> p (h t)"), pT_[:, :], [pT_], [Tn])
                    cp(Un[:].rearrange("p h t -> p (h t)"), pU[:, :], [pU], [Un])
                    T_, Tn = Tn, T_
                    U_, Un = Un, U_
                    yield
                if KST < 4:
                    return
                pu = kb.ps()
                for i, h in enumerate(hs):
                    kb.op("pe", lambda e: e.matmul(pu[:, i * 128:(i + 1) * 128], U_[:, i, :], vkb[:, h, :], start=True, stop=True), r=[U_, vkb], w=[pu])
                pu3 = pu[:, :].rearrange("p (h c) -> p h c", h=4)
                cp(u_sb[:], pu3[:, :, 0:64], [pu], [u_sb])
                cp(w_sb[:], pu3[:, :, 64:128], [pu], [w_sb])
                pwt = kb.ps()
                for a in range(2):
                    kb.op("pe", lambda e, a=a: e.transpose(pwt[:, a * 128:(a + 1) * 128], w_sb[:, 2 * a:2 * a + 2, :].rearrange("p h d -> p (h d)"), ident_f[:, :]),
                          r=[w_sb, ident_f], w=[pwt])
                cp(wT[:].rearrange("p a t -> p (a t)"), pwt[:, 0:256], [pwt], [wT])
                yield
                for p_ in range(2):
                    kb.op("dve", lambda e: e.tensor_scalar(out=Sm[p_][:], in0=S[:, 2 * hg:2 * hg + 2, :], scalar1=EOM[p_], scalar2=None, op0=ALU.mult), r=[S, dnc], w=[Sm[p_]])
                if KST < 5:
                    return
                pws = kb.ps()
                for i, h in enumerate(hs):
                    kb.op("pe", lambda e: e.matmul(pws[:, i * 64:(i + 1) * 64], wT[:, i // 2, :], Sm[h % 2][:, i // 2, :], start=True, stop=True), r=[wT, Sm[h % 2]], w=[pws])
                tt("dve", vnew[:], u_sb[:], pws[:, 0:256].rearrange("p (h d) -> p h d", h=4), ALU.subtract, [u_sb, pws], [vnew])
                yield
                if own:
                    pqs, pav = kb.ps(), kb.ps()
                    for i, h in enumerate(hs):
                        kb.op("pe", lambda e: e.matmul(pqs[:, i * 64:(i + 1) * 64], qkT[:, h // 2, :], Sm[h % 2][:, i // 2, :], start=True, stop=True), r=[qkT, Sm[h % 2]], w=[pqs])
                        kb.op("pe", lambda e: e.matmul(pav[:, i * 64:(i + 1) * 64], AqkT[:, i, :], vnew[:, i, :], start=True, stop=True), r=[AqkT, vnew], w=[pav])
                    osl = o_t[:, 4 * hg:4 * hg + 4, :]
                    tt("dve", osl, pqs[:, 0:256].rearrange("p (h d) -> p h d", h=4), c64(sm[:, 24 + 4 * hg:24 + 4 * hg + 4], 4), ALU.mult, [pqs, sm], [o_t])
                    tt("dve", osl, osl, pav[:, 0:256].rearrange("p (h d) -> p h d", h=4), ALU.add, [o_t, pav], [o_t])
                if KST < 6:
                    return
                pS = kb.ps()
                for a in range(2):
                    kb.op("pe", lambda e, a=a: e.matmul(pS[:, a * 128:(a + 1) * 128], kd[:, 4 * hg + 2 * a:4 * hg + 2 * a + 2, :].rearrange("p h d -> p (h d)"),
                                                        vnew[:, 2 * a:2 * a + 2, :].rearrange("p h d -> p (h d)"), start=True, stop=True), r=[kd, vnew], w=[pS])
                Ssl = S[:, 2 * hg:2 * hg + 2, :]
                tt("dve", Ssl, Ssl, bcast(eglS[:, 2 * hg:2 * hg + 2].unsqueeze(2), [128, 2, 64]), ALU.mult, [S, sm], [S])
                for a in range(2):
                    tt("dve", S[0:64, 2 * hg + a, :], S[0:64, 2 * hg + a, :], pS[0:64, a * 128:a * 128 + 64], ALU.add, [S, pS], [S])
                    tt("dve", S[64:128, 2 * hg + a, :], S[64:128, 2 * hg + a, :], pS[64:128, a * 128 + 64:a * 128 + 128], ALU.add, [S, pS], [S])
                yield
            gens = [hg_body(hg_, HB[hg_]) for hg_ in range(2)]
            while gens:
                for g_ in list(gens):
                    try:
                        next(g_)
                    except StopIteration:
                        gens.remove(g_)
            if own:
                pz = kb.ps()
                for kc in range(8):
                    kb.op("pe", lambda e, kc=kc: e.matmul(pz[:, :], xt(kc), wz[:, kc, :], start=(kc == 0), stop=(kc == 7)), r=[xr_, wz], w=[pz])
                cp(zs[:], pz[:, :], [pz], [zs], func=AF.Silu)
                tt("pool", o2[:], o_t[:], o_t[:], ALU.mult, [o_t], [o2])
                kb.op("dve", lambda e: e.tensor_reduce(out=ro[:], in_=o2[:], axis=AX.X, op=ALU.add), r=[o2], w=[ro])
                cp(ro[:], ro[:], [ro, epsc], [ro], func=AF.Sqrt, scale=1.0 / 64, bias=epsc[:, 0:1])
                kb.op("dve", lambda e: e.reciprocal(ro[:], ro[:]), r=[ro], w=[ro])
                tt("dve", o_t[:], o_t[:], c64(ro[:, :], 8), ALU.mult, [o_t, ro], [o_t])
                tt("dve", o_t[:], o_t[:], bcast(gnd[:, :].unsqueeze(1), [128, 8, 64]), ALU.mult, [o_t, gnd], [o_t])
                tt("dve", obf[:], o_t[:].rearrange("p h d -> p (h d)"), zs[:], ALU.mult, [o_t, zs], [obf])
                pt = kb.ps()
                ptv = pt[:].bitcast(BF16)
                for f in range(4):
                    kb.op("pe", lambda e, f=f: e.transpose(ptv[:, f * 128:(f + 1) * 128], obf[:, f * 128:(f + 1) * 128], ident_bf[:, :]), r=[obf, ident_bf], w=[pt])
                i_own = tau // 2
                kb.op("dve", lambda e: e.tensor_copy(mixT[:, 0:4, i_own * 128:(i_own + 1) * 128], ptv[:, 0:512].rearrange("p (f t) -> p f t", f=4)), r=[pt], w=[mix_r[i_own]])
        for par in range(2):
            kb.dma("sp", [(o_prec.rearrange("(a two) k v -> two k a v", two=2)[par], S[par * 64:(par + 1) * 64, :, :])], r=[S], final=True)
        kb.barrier()

    if "p" in DBG:
        kTs = kb.sb([128, NT * 128], BF16, name="kTs", stack=front)
        kTw = kb.sb([128, NT * 128], BF16, name="kTw", stack=front)
        Vs = kb.sb([128, NT, 128], BF16, name="Vs", stack=front)
        Vw = kb.sb([128, NT, 128], BF16, name="Vw", stack=front)
        kcT = kb.sb([128, 66], BF16, name="kcT", stack=front)
        vc = kb.sb([66, 128], BF16, name="vc", stack=front)
        eom = kb.sb([128, 2], F32, stack=front, name="eom")
    with contextlib.ExitStack() as st:
        if "p" in DBG:
            CK = (kb.sb([128, NT, 256], BF16, stack=st, name="CKe"), kb.sb([128, NT, 256], BF16, stack=st, name="CKo"))
            b1t = kb.sb([128, 2], F32, stack=st, name="b1t")
            kb.dma("sp", [(eom[:], eomin[:, :]), (b1t[:], b1in[:, :])], w=[eom, b1t])
            w2pad = kb.sb([128, 2, 2, 128], BF16, stack=st, name="w2pad")
            kb.op("pool", lambda e: e.memset(w2pad[:], 0.0), w=[w2pad])
        sti = contextlib.ExitStack()
        wkv = load_w(w_in[:, OFF_KV:OFF_KV + 768], 768, sti, "wkv")
        ost = [kb.sb([128, 768], F32, stack=sti) for _ in range(2)]
        if "p" in DBG:
            pe_sb = kb.sb([128, 256], F32, stack=sti, name="pe_sb")
            ckt = kb.sb([128, 256], F32, stack=sti, name="ckt")
            kb.dma("sp", [(pe_sb[:], perep[:, :])], w=[pe_sb])
            w2sb = kb.sb([128, 2, 64], F32, stack=sti, name="w2sb")
            kb.dma("sp", [(w2sb[:], w2in[:, :, :])], w=[w2sb])
            for k in range(2):
                for g in range(2):
                    kb.op("dve", lambda e: e.tensor_copy(w2pad[:, k, g, g * 64:(g + 1) * 64], w2sb[:, k, :]), r=[w2sb], w=[w2pad])
        for t in range(NT):
            pa, pb = kb.ps(), kb.ps()
            for kc in range(8):
                kb.op("pe", lambda e, kc=kc: e.matmul(pa[:, 0:384], xnT[:, kc, t * 128:(t + 1) * 128], wkv[:, kc, 0:384],
                                                      start=(kc == 0), stop=(kc == 7)), r=[xnT_res[t], wkv], w=[pa])
            for kc in range(8):
                kb.op("pe", lambda e, kc=kc: e.matmul(pb[:, 0:384], xnT[:, kc, t * 128:(t + 1) * 128], wkv[:, kc, 384:768],
                                                      start=(kc == 0), stop=(kc == 7)), r=[xnT_res[t], wkv], w=[pb])
            o = ost[t % 2]
            kb.op("act", lambda e: e.activation(out=o[:, 0:384], in_=pa[:, 0:384], func=AF.Copy), r=[pa], w=[o])
            kb.op("dve", lambda e: e.tensor_copy(o[:, 384:768], pb[:, 0:384]), r=[pb], w=[o])
            if t % 2 == 1:
                kb.dma("sp", [(o_nsa[t // 2, :, :], o[:, 0:512])], r=[o], final=True)
            if t >= NT - 5:
                kb.dma("sp", [(o_win[t - (NT - 5), :, :], o[:, 512:768])], r=[o], final=True)
            if "p" in DBG:
                kb.op("pool", lambda e: e.tensor_copy(Vs[:, t, :], o[:, 384:512]), r=[o], w=[Vs])
                kb.op("pool", lambda e: e.tensor_copy(Vw[:, t, :], o[:, 640:768]), r=[o], w=[Vw])
                kb.op("dve", lambda e: e.tensor_tensor(out=ckt[:], in0=o[:, 0:256], in1=pe_sb[:], op=ALU.add), r=[o, pe_sb], w=[ckt])
                for p_ in range(2):
                    kb.op("dve", lambda e: e.tensor_scalar(out=CK[p_][:, t, :], in0=ckt[:], scalar1=eom[:, p_:p_ + 1], scalar2=None, op0=ALU.mult), r=[ckt, eom], w=[CK[p_]])
                for (c0, dst) in ((256, kTs), (512, kTw)):
                    pc = kb.ps()
                    for kc in range(8):
                        kb.op("pe", lambda e, kc=kc: e.matmul(pc[:, 0:128], wkv[:, kc, c0:c0 + 128], xnT[:, kc, t * 128:(t + 1) * 128],
                                                              start=(kc == 0), stop=(kc == 7)), r=[xnT_res[t], wkv], w=[pc])
                    kb.op("act", lambda e: e.activation(out=dst[:, t * 128:(t + 1) * 128], in_=pc[:, 0:128], func=AF.Copy), r=[pc], w=[dst])
        pa, pb = kb.ps(), kb.ps()
        for kc in range(8):
            kb.op("pe", lambda e, kc=kc: e.matmul(pa[0:NS, 0:384], xsT[:, kc, :], wkv[:, kc, 0:384],
                                                  start=(kc == 0), stop=(kc == 7)), r=[xsT, wkv], w=[pa])
        for kc in range(8):
            kb.op("pe", lambda e, kc=kc: e.matmul(pb[0:NS, 0:384], xsT[:, kc, :], wkv[:, kc, 384:768],
                                                  start=(kc == 0), stop=(kc == 7)), r=[xsT, wkv], w=[pb])
        so = kb.sb([NS, 768], F32, stack=sti)
        kb.op("act", lambda e: e.activation(out=so[:, 0:384], in_=pa[0:NS, 0:384], func=AF.Copy), r=[pa], w=[so])
        kb.op("dve", lambda e: e.tensor_copy(so[:, 384:768], pb[0:NS, 0:384]), r=[pb], w=[so])
        kb.dma("sp", [(o_snsa[:, :], so[:, 0:512])], r=[so], final=True)
        kb.dma("sp", [(o_swin[:, 511, :], so[:, 512:768])], r=[so], final=True)
        if "a" in DBG:
            kb.dma("pool", [(o_swin[s, 0:511, :].rearrange("(a b) d -> a (b d)", a=73),
                             cwin[s, 1:512, :].rearrange("(a b) d -> a (b d)", a=73)) for s in range(NS)], final=True)
        kb.barrier()
        sti.close()
        if "p" in DBG:
            w1sb = kb.sb([128, 2, 64, 128], BF16, stack=st, name="w1sb")
            kb.dma("pool", [(w1sb[:, k, :, :], w1rep[:, k, :, :]) for k in range(2)], w=[w1sb])
            HTs = [[kb.sb([128, 66], BF16, stack=st) for g in range(2)] for k in range(2)]
            for k in range(2):
                for g in range(2):
                    pa = kb.ps()
                    for half in range(2):
                        for d in range(64):
                            kb.op("pe", lambda e: e.matmul(pa[:, half * NT:(half + 1) * NT], w1sb[:, k, d, :],
                                                           CK[half][:, :, k * 128 + g * 64 + d], start=(d == 0), stop=(d == 63)), r=[w1sb, CK[half]], w=[pa])
                    kb.op("act", lambda e: e.activation(out=HTs[k][g][:], in_=pa[:, 0:66], func=AF.Relu, bias=b1t[:, k:k + 1]), r=[pa, b1t], w=[HTs[k][g]])
            pk = kb.ps()
            for g in range(2):
                kb.op("pe", lambda e: e.matmul(pk[:, 0:66], w2pad[:, 0, g, :], HTs[0][g][:], start=(g == 0), stop=(g == 1)), r=[w2pad, HTs[0][g]], w=[pk])
            kb.op("act", lambda e: e.activation(out=kcT[:], in_=pk[:, 0:66], func=AF.Copy), r=[pk], w=[kcT])
            pv = kb.ps()
            for g in range(2):
                kb.op("pe", lambda e: e.matmul(pv[0:66, 0:128], HTs[1][g][:], w2pad[:, 1, g, :], start=(g == 0), stop=(g == 1)), r=[w2pad, HTs[1][g]], w=[pv])
            kb.op("act", lambda e: e.activation(out=vc[:], in_=pv[0:66, 0:128], func=AF.Copy), r=[pv], w=[vc])
        kb.barrier()

    if "p" in DBG:
      with contextlib.ExitStack() as st:
        kb.barrier()
        kb.psn = 6
        accP, accD = kb.psum[6], kb.psum[7]
        Esb = kb.sb([66, NT * 128], BF16, stack=st, name="Esb")
        kb.dma("pool", [(Esb[:], Ein[:, :])], w=[Esb])
        mk3 = kb.sb([128, 3, 512], BF16, stack=st, name="mk3")
        kb.dma("pool", [(mk3[:, m, :], mk3in[:, m, :]) for m in range(3)], w=[mk3])
        onesb = kb.sb([128, 128], BF16, stack=st, name="onesb_n")
        kb.op("pool", lambda e: e.memset(onesb[:], 1.0), w=[onesb])
        onesf = kb.sb([128, 128], F32, stack=st, name="onesf_n")
        kb.op("pool", lambda e: e.memset(onesf[:], 1.0), w=[onesf])
        cmk = [kb.sb([66, 512], BF16, stack=st) for _ in range(2)]
        addc = [kb.sb([128, 2, 66], F32, stack=st) for _ in range(2)]
        PT = [kb.sb([128, 512], BF16, stack=st) for _ in range(2)]
        MBT4 = kb.sb([66, 4, 128], BF16, stack=st, name="MBT4")
        rd = kb.sb([128, 512], F32, stack=st, name="rd")
        tb = kb.sb([128, 512], F32, stack=st, name="tb")
        acc = kb.sb([128, 512], F32, stack=st, name="acc")
        grep = kb.sb([128, 3, 512], BF16, stack=st, name="grep")
        dg = kb.sb([128, 4, 128], F32, stack=st, name="dg")
        pn = kb.sb([66, 4, 128], F32, stack=st, name="pn")
        impT = kb.sb([66, 128], F32, stack=st, name="impT")
        sc = kb.sb([128, 66], F32, stack=st, name="sc_n")
        sc2 = kb.sb([128, 66], F32, stack=st, name="sc2")
        m8 = kb.sb([128, 16], F32, stack=st, name="m8")

        qm = (kb.sb([128, 4, 128], BF16, name="qm_e", stack=st), kb.sb([128, 4, 128], BF16, name="qm_o", stack=st))
        gtok = kb.sb([128, 1, 24], F32, name="gtok", stack=st)
        wqn = load_w(wq_p[:, :], 512, st, "wqn")
        wg = load_w(w_in[:, OFF_G:OFF_G + 24], 24, st, "wg")
        def combine(g, br, first):
            hr = slice(g * 64, (g + 1) * 64)
            kb.op("dve", lambda e: e.tensor_scalar(out=rd[hr, :], in0=accD[hr, :], scalar1=1e-30, scalar2=None, op0=ALU.max), r=[accD], w=[rd])
            kb.op("dve", lambda e: e.reciprocal(rd[hr, :], rd[hr, :]), r=[rd], w=[rd])
            kb.op("dve", lambda e: e.tensor_tensor(out=tb[hr, :], in0=accP[hr, :], in1=rd[hr, :], op=ALU.mult), r=[accP, rd], w=[tb])
            if first:
                kb.op("dve", lambda e: e.tensor_tensor(out=acc[hr, :], in0=tb[hr, :], in1=grep[hr, br, :], op=ALU.mult), r=[tb, grep], w=[acc])
            else:
                kb.op("dve", lambda e: e.tensor_tensor(out=tb[hr, :], in0=tb[hr, :], in1=grep[hr, br, :], op=ALU.mult), r=[tb, grep], w=[tb])
                kb.op("dve", lambda e: e.tensor_tensor(out=acc[hr, :], in0=acc[hr, :], in1=tb[hr, :], op=ALU.add), r=[acc, tb], w=[acc])

        for i in range(NOWN):
            tq = 2 * i + 1
            kb.dma("pool", [(cmk[i % 2][:], cmkin[i, :, :])], w=[cmk[i % 2]])
            kb.dma("sp", [(addc[i % 2][:], addcin[i, :, :, :])], w=[addc[i % 2]])
            t = 2 * i + 1
            pa = kb.ps()
            for j in range(4):
                for kc in range(8):
                    kb.op("pe", lambda e, kc=kc: e.matmul(pa[:, j * 128:(j + 1) * 128], wqn[:, kc, j * 128:(j + 1) * 128], xnT[:, kc, t * 128:(t + 1) * 128],
                                                          start=(kc == 0), stop=(kc == 7)), r=[xnT_res[t], wqn], w=[pa])
            for p_ in range(2):
                kb.op("dve", lambda e: e.tensor_scalar(out=qm[p_][:, :, :], in0=pa[:, :].rearrange("p (j t) -> p j t", j=4),
                                                       scalar1=eom[:, p_:p_ + 1], scalar2=0.125, op0=ALU.mult, op1=ALU.mult), r=[pa, eom], w=[qm[p_]])
            pg = kb.ps()
            for kc in range(8):
                kb.op("pe", lambda e, kc=kc: e.matmul(pg[:, 0:24], xnT[:, kc, t * 128:(t + 1) * 128], wg[:, kc, :], start=(kc == 0), stop=(kc == 7)),
                      r=[xnT_res[t], wg], w=[pg])
            kb.op("act", lambda e: e.activation(out=gtok[:, 0, :], in_=pg[:, 0:24], func=AF.Sigmoid), r=[pg], w=[gtok])

            for g in range(2):
                qr = qm[g][:, :, :]
                for br in range(3):
                    gcols = gtok[:, 0, :].rearrange("p (h b) -> p h b", b=3)[:, 4 * g:4 * g + 4, br]
                    kb.op("dve", lambda e: e.tensor_tensor(out=dg[:], in0=bcast(ident_f[:, :].unsqueeze(1), [128, 4, 128]),
                                                           in1=bcast(gcols.unsqueeze(2), [128, 4, 128]), op=ALU.mult), r=[ident_f, gtok], w=[dg])
                    pgp = kb.ps()
                    kb.op("pe", lambda e: e.matmul(pgp[:, :], onesf[:, :], dg[:].rearrange("p j t -> p (j t)"), start=True, stop=True), r=[onesf, dg], w=[pgp])
                    kb.op("act", lambda e: e.activation(out=grep[:, br, :], in_=pgp[:, :], func=AF.Copy), r=[pgp], w=[grep])
                pa = kb.ps()
                kb.op("pe", lambda e: e.matmul(pa[0:66, :], kcT[:, :], qr, start=True, stop=False), r=[kcT, qm[g]], w=[pa])
                kb.op("pe", lambda e: e.matmul(pa[0:66, :], ident_bf[0:66, 0:66], cmk[i % 2][:, :], start=False, stop=True), r=[ident_bf, cmk[i % 2]], w=[pa])
                pt_ = PT[0]
                kb.op("act", lambda e: e.activation(out=pt_[0:66, :], in_=pa[0:66, :], func=AF.Exp), r=[pa], w=[pt_])
                kb.op("pe", lambda e: e.matmul(accP[:, :], vc[:, :], pt_[0:66, :], start=True, stop=True), r=[vc, pt_], w=[accP])
                kb.op("pe", lambda e: e.matmul(accD[:, :], onesb[0:66, :], pt_[0:66, :], start=True, stop=True), r=[onesb, pt_], w=[accD])
                combine(g, 0, True)
                kb.op("dve", lambda e: e.tensor_scalar(out=rd[0:66, :], in0=accD[0:66, :], scalar1=1e-30, scalar2=None, op0=ALU.max), r=[accD], w=[rd])
                kb.op("dve", lambda e: e.reciprocal(rd[0:66, :], rd[0:66, :]), r=[rd], w=[rd])
                kb.op("dve", lambda e: e.tensor_tensor(out=pn[:].rearrange("n j q -> n (j q)"), in0=pt_[0:66, :], in1=rd[0:66, :], op=ALU.mult), r=[pt_, rd], w=[pn])
                kb.op("dve", lambda e: e.tensor_reduce(out=impT[:], in_=pn[:].rearrange("n j q -> n q j"), axis=AX.X, op=ALU.add), r=[pn], w=[impT])
                pi = kb.ps()
                kb.op("pe", lambda e: e.transpose(pi[:, 0:66], impT[:, :], ident_f[0:66, 0:66]), r=[impT, ident_f], w=[pi])
                kb.op("dve", lambda e: e.tensor_tensor(out=sc[:], in0=pi[:, 0:66], in1=addc[i % 2][:, 0, :], op=ALU.add), r=[pi, addc[i % 2]], w=[sc])
                kb.op("dve", lambda e: e.max(m8[:, 0:8], sc[:]), r=[sc], w=[m8])
                kb.op("dve", lambda e: e.match_replace(sc2[:], m8[:, 0:8], sc[:], -1e30), r=[sc, m8], w=[sc2])
                kb.op("dve", lambda e: e.max(m8[:, 8:16], sc2[:]), r=[sc2], w=[m8])
                kb.op("dve", lambda e: e.tensor_scalar(out=sc2[:], in0=sc[:], scalar1=m8[:, 15:16], scalar2=None, op0=ALU.is_ge), r=[sc, m8], w=[sc2])
                kb.op("dve", lambda e: e.tensor_scalar(out=sc2[:], in0=sc2[:], scalar1=-1.0, scalar2=30000.0, op0=ALU.add, op1=ALU.mult), r=[sc2], w=[sc2])
                kb.op("dve", lambda e: e.tensor_tensor(out=sc2[:], in0=sc2[:], in1=addc[i % 2][:, 1, :], op=ALU.min), r=[sc2, addc[i % 2]], w=[sc2])
                pm = kb.ps()
                kb.op("pe", lambda e: e.transpose(pm[0:66, 0:128], sc2[:, :], ident_f[:, :]), r=[sc2, ident_f], w=[pm])
                kb.op("dve", lambda e: e.tensor_copy(MBT4[:], bcast(pm[0:66, 0:128].unsqueeze(1), [66, 4, 128])), r=[pm], w=[MBT4])
                def slc_scores(tk):
                    pa = kb.ps()
                    kb.op("pe", lambda e: e.matmul(pa[:, :], kTs[:, tk * 128:(tk + 1) * 128], qr, start=True, stop=False), r=[kTs, qm[g]], w=[pa])
                    kb.op("pe", lambda e: e.matmul(pa[:, :], Esb[:, tk * 128:(tk + 1) * 128], MBT4[:].rearrange("n j q -> n (j q)"), start=False, stop=(tk != tq)),
                          r=[Esb, MBT4], w=[pa])
                    if tk == tq:
                        kb.op("pe", lambda e: e.matmul(pa[:, :], ident_bf[:, :], mk3[:, 0, :], start=False, stop=True), r=[ident_bf, mk3], w=[pa])
                    return pa
                pas = {0: slc_scores(0)}
                for tk in range(tq + 1):
                    if tk + 1 <= tq:
                        pas[tk + 1] = slc_scores(tk + 1)
                    pa = pas.pop(tk)
                    pt_ = PT[tk % 2]
                    kb.op("act", lambda e: e.activation(out=pt_[:, :], in_=pa[:, :], func=AF.Exp), r=[pa], w=[pt_])
                    kb.op("pe", lambda e: e.matmul(accP[:, :], Vs[:, tk, :], pt_[:, :], start=(tk == 0), stop=(tk == tq)), r=[Vs, pt_], w=[accP])
                    kb.op("pe", lambda e: e.matmul(accD[:, :], onesb[:, :], pt_[:, :], start=(tk == 0), stop=(tk == tq)), r=[onesb, pt_], w=[accD])
                combine(g, 1, False)
                tks = [tk for tk in range(tq - 4, tq + 1) if tk >= 0]
                def win_scores(tk):
                    extra = []
                    if tk == tq:
                        extra.append(0)
                    if tk == tq - 4:
                        extra.append(1)
                    if tk == 0:
                        extra.append(2)
                    pa = kb.ps()
                    kb.op("pe", lambda e: e.matmul(pa[:, :], kTw[:, tk * 128:(tk + 1) * 128], qr, start=True, stop=(len(extra) == 0)), r=[kTw, qm[g]], w=[pa])
                    for x_, m in enumerate(extra):
                        kb.op("pe", lambda e: e.matmul(pa[:, :], ident_bf[:, :], mk3[:, m, :], start=False, stop=(x_ == len(extra) - 1)), r=[ident_bf, mk3], w=[pa])
                    return pa
                pas = {0: win_scores(tks[0])}
                for n_, tk in enumerate(tks):
                    if n_ + 1 < len(tks):
                        pas[n_ + 1] = win_scores(tks[n_ + 1])
                    pa = pas.pop(n_)
                    pt_ = PT[n_ % 2]
                    kb.op("act", lambda e: e.activation(out=pt_[:, :], in_=pa[:, :], func=AF.Exp), r=[pa], w=[pt_])
                    kb.op("pe", lambda e: e.matmul(accP[:, :], Vw[:, tk, :], pt_[:, :], start=(n_ == 0), stop=(n_ == len(tks) - 1)), r=[Vw, pt_], w=[accP])
                    kb.op("pe", lambda e: e.matmul(accD[:, :], onesb[:, :], pt_[:, :], start=(n_ == 0), stop=(n_ == len(tks) - 1)), r=[onesb, pt_], w=[accD])
                combine(g, 2, False)
                hr = slice(g * 64, (g + 1) * 64)
                kb.op("act", lambda e: e.activation(out=mixT[hr, 4:8, i * 128:(i + 1) * 128], in_=acc[hr, :].rearrange("p (j q) -> p j q", j=4), func=AF.Copy),
                      r=[acc], w=[mix_r[i]])
        kb.psn = 8
        kb.barrier()
    kb.barrier()
    front.close()
    memKT = kb.sb([128, 8, 256], BF16, name="memKT")
    memV = kb.sb([128, 2, 1024], BF16, name="memV")

    with contextlib.ExitStack() as st:
        stages = []
        for i in range(2):
            stages.append((kb.sb([128, D], F32, stack=st), kb.sb([128, 1], F32, stack=st),
                           kb.sb([128, 1], F32, stack=st), kb.sb([128, D], BF16, stack=st)))
        mT = kb.sb([128, 8, 256], BF16, stack=st)
        mres = [Res(), Res()]
        for t in range(2):
            norm_transpose(memp[t * 128:(t + 1) * 128, :], 128, stages[t], mT[:, :, t * 128:(t + 1) * 128], mres[t], 2)
        wm = load_w(w_mem_kv[:, :], 2048, st, "wmkv")
        mo = [kb.sb([128, 2048], F32, stack=st) for _ in range(2)]
        for t in range(2):
            for cb in range(4):
                pa = kb.ps()
                for kc in range(8):
                    kb.op("pe", lambda e, kc=kc: e.matmul(pa[:, :], mT[:, kc, t * 128:(t + 1) * 128], wm[:, kc, cb * 512:(cb + 1) * 512],
                                                          start=(kc == 0), stop=(kc == 7)), r=[mres[t], wm], w=[pa])
                if cb % 2 == 0:
                    kb.op("dve", lambda e: e.tensor_copy(mo[t][:, cb * 512:(cb + 1) * 512], pa[:, :]), r=[pa], w=[mo[t]])
                else:
                    kb.op("act", lambda e: e.activation(out=mo[t][:, cb * 512:(cb + 1) * 512], in_=pa[:, :], func=AF.Copy), r=[pa], w=[mo[t]])
            kb.dma("sp", [(o_mem[t * 128:(t + 1) * 128, :], mo[t][:, :])], r=[mo[t]], final=True)
            kb.op("pool", lambda e: e.tensor_copy(memV[:, t, :], mo[t][:, 1024:2048]), r=[mo[t]], w=[memV])
        for c in range(8):
            pa = kb.ps()
            for kc in range(8):
                kb.op("pe", lambda e, kc=kc: e.matmul(pa[:, 0:256], wm[:, kc, c * 128:(c + 1) * 128], mT[:, kc, :],
                                                      start=(kc == 0), stop=(kc == 7)), r=[mres[0], mres[1], wm], w=[pa])
            kb.op("act", lambda e: e.activation(out=memKT[:, c, :], in_=pa[:, 0:256], func=AF.Copy), r=[pa], w=[memKT])
        kb.barrier()


    scr_qkv = nc.dram_tensor("scr_qkv", [NS, 8, 3, 64], F32).ap()
    scr_z = nc.dram_tensor("scr_z", [NS, 8, 64], F32).ap()
    scr_ba = nc.dram_tensor("scr_ba", [NS, 8, 2], F32).ap()
    scr_dno = nc.dram_tensor("scr_dno", [NS, 8, 64], F32).ap()
    r_scr = Res("scr")
    with contextlib.ExitStack() as st:
        cst = kb.sb([NS, 3, 1536], F32, stack=st)
        wc = kb.sb([NS, 4, 1536], F32, stack=st)
        cv = kb.sb([NS, 1536], F32, stack=st)
        tmpc = kb.sb([NS, 1536], F32, stack=st)
        kb.dma("sp", [(cst[:], sconv[:, :, :])], w=[cst])
        spq = kb.sb([NS, 1536], F32, stack=st)
        kb.dma("sp", [(spq[:], scr_sproj[:, 0:1536])], r=[r_sps], w=[spq])
        kb.dma("sp", [(wc[:], bass.AP(conv_w.tensor, 0, [[0, NS], [1, 4 * 1536]]).rearrange("p (j c) -> p j c", j=4))], w=[wc])
        kb.op("dve", lambda e: e.tensor_tensor(out=cv[:], in0=spq[:, :], in1=wc[:, 3, :], op=ALU.mult), r=[spq, wc], w=[cv])
        for j in range(3):
            kb.op("dve", lambda e: e.tensor_tensor(out=tmpc[:], in0=cst[:, j, :], in1=wc[:, j, :], op=ALU.mult), r=[cst, wc], w=[tmpc])
            kb.op("dve", lambda e: e.tensor_tensor(out=cv[:], in0=cv[:], in1=tmpc[:], op=ALU.add), r=[cv, tmpc], w=[cv])
        kb.op("act", lambda e: e.activation(out=cv[:], in_=cv[:], func=AF.Silu), r=[cv], w=[cv])
        kb.dma("sp", [(scr_qkv[:, :, a, :], cv[:, a * 512:(a + 1) * 512].rearrange("s (h d) -> s h d", h=8)) for a in range(3)] + [
                      (scr_z.rearrange("s h d -> s (h d)"), scr_sproj[:, OFF_Z:OFF_Z + 512]),
                      (scr_ba[:, :, 0], scr_sproj[:, OFF_B:OFF_B + 8]), (scr_ba[:, :, 1], scr_sproj[:, OFF_A:OFF_A + 8])],
               r=[cv, r_sps], w=[r_scr], allow_slow_non_contiguous=True)
        q3 = kb.sb([128, 3, 64], F32, stack=st)
        z1 = kb.sb([128, 64], F32, stack=st)
        ba = kb.sb([128, 2], F32, stack=st)
        abr = kb.sb([128, 2], F32, stack=st)
        gn = kb.sb([128, 64], F32, stack=st)
        S = kb.sb([128, 64, 64], F32, stack=st)
        big = kb.sb([128, 64, 64], F32, stack=st)
        sm = kb.sb([128, 16], F32, stack=st)
        v64 = [kb.sb([128, 64], F32, stack=st) for _ in range(6)]
        kb.dma("sp", [(q3[:], scr_qkv.rearrange("s h a d -> (s h) a d")),
                      (z1[:], scr_z.rearrange("s h d -> (s h) d")),
                      (ba[:], scr_ba.rearrange("s h a -> (s h) a"))], r=[r_scr], w=[q3, z1, ba])
        kb.dma("sp", [(abr[:], abrep[:, :]), (gn[:], dnnorm[:, :])], w=[abr, gn])
        kb.dma("sp", [(S[:, 0:32, :], srec[:, 0:32, :]), (S[:, 32:64, :], srec[:, 32:64, :])], w=[S])
        BETA, X, EG, SSQ, SSK, RQ, RK, SSO, RO, EA = range(10)
        c1 = lambda i: sm[:, i:i + 1]
        kb.op("act", lambda e: e.activation(out=c1(BETA), in_=ba[:, 0:1], func=AF.Sigmoid), r=[ba], w=[sm])
        kb.op("act", lambda e: e.activation(out=c1(X), in_=ba[:, 1:2], func=AF.Exp, bias=abr[:, 1:2]), r=[ba, abr], w=[sm])
        kb.op("act", lambda e: e.activation(out=c1(X), in_=c1(X), func=AF.Ln, bias=onec[:, 0:1]), r=[sm, onec], w=[sm])
        kb.op("act", lambda e: e.activation(out=c1(EA), in_=abr[:, 0:1], func=AF.Exp), r=[abr], w=[sm])
        kb.op("dve", lambda e: e.tensor_scalar(out=c1(X), in0=c1(X), scalar1=c1(EA), scalar2=-1.0, op0=ALU.mult, op1=ALU.mult), r=[sm], w=[sm])
        kb.op("act", lambda e: e.activation(out=c1(EG), in_=c1(X), func=AF.Exp), r=[sm], w=[sm])
        kb.op("act", lambda e: e.activation(out=v64[0][:], in_=q3[:, 0, :], func=AF.Square, accum_out=c1(SSQ)), r=[q3], w=[v64[0], sm])
        kb.op("act", lambda e: e.activation(out=v64[0][:], in_=q3[:, 1, :], func=AF.Square, accum_out=c1(SSK)), r=[q3], w=[v64[0], sm])
        kb.op("act", lambda e: e.activation(out=sm[:, RQ:RQ + 2], in_=sm[:, SSQ:SSQ + 2], func=AF.Sqrt, bias=epsc[:, 0:1]), r=[sm, epsc], w=[sm])
        kb.op("dve", lambda e: e.reciprocal(sm[:, RQ:RQ + 2], sm[:, RQ:RQ + 2]), r=[sm], w=[sm])
        qn, kn, t1, vn, o1 = v64[1], v64[2], v64[3], v64[4], v64[5]
        kb.op("dve", lambda e: e.tensor_scalar(out=qn[:], in0=q3[:, 0, :], scalar1=c1(RQ), scalar2=0.125, op0=ALU.mult, op1=ALU.mult), r=[q3, sm], w=[qn])
        kb.op("dve", lambda e: e.tensor_scalar(out=kn[:], in0=q3[:, 1, :], scalar1=c1(RK), scalar2=None, op0=ALU.mult), r=[q3, sm], w=[kn])
        kb.op("dve", lambda e: e.tensor_tensor(out=big[:], in0=S[:], in1=bcast(kn[:, :].unsqueeze(2), [128, 64, 64]), op=ALU.mult), r=[S, kn], w=[big])
        kb.op("dve", lambda e: e.tensor_reduce(out=t1[:], in_=big[:].rearrange("p k v -> p v k"), axis=AX.X, op=ALU.add), r=[big], w=[t1])
        kb.op("dve", lambda e: e.tensor_scalar(out=t1[:], in0=t1[:], scalar1=c1(EG), scalar2=None, op0=ALU.mult), r=[t1, sm], w=[t1])
        kb.op("dve", lambda e: e.tensor_tensor(out=vn[:], in0=q3[:, 2, :], in1=t1[:], op=ALU.subtract), r=[q3, t1], w=[vn])
        kb.op("dve", lambda e: e.tensor_scalar(out=vn[:], in0=vn[:], scalar1=c1(BETA), scalar2=None, op0=ALU.mult), r=[vn, sm], w=[vn])
        kb.op("dve", lambda e: e.tensor_tensor(out=big[:], in0=bcast(kn[:, :].unsqueeze(2), [128, 64, 64]),
                                               in1=bcast(vn[:, :].unsqueeze(1), [128, 64, 64]), op=ALU.mult), r=[kn, vn], w=[big])
        kb.op("dve", lambda e: e.scalar_tensor_tensor(out=S[:], in0=S[:], scalar=c1(EG), in1=big[:], op0=ALU.mult, op1=ALU.add), r=[S, big, sm], w=[S])
        kb.dma("sp", [(o_srec[:, 0:32, :], S[:, 0:32, :]), (o_srec[:, 32:64, :], S[:, 32:64, :])], r=[S], final=True)
        kb.op("dve", lambda e: e.tensor_tensor(out=big[:], in0=S[:], in1=bcast(qn[:, :].unsqueeze(2), [128, 64, 64]), op=ALU.mult), r=[S, qn], w=[big])
        kb.op("dve", lambda e: e.tensor_reduce(out=o1[:], in_=big[:].rearrange("p k v -> p v k"), axis=AX.X, op=ALU.add), r=[big], w=[o1])
        kb.op("act", lambda e: e.activation(out=t1[:], in_=o1[:], func=AF.Square, accum_out=c1(SSO)), r=[o1], w=[t1, sm])
        kb.op("act", lambda e: e.activation(out=c1(RO), in_=c1(SSO), func=AF.Sqrt, scale=1.0 / 64, bias=epsc[:, 0:1]), r=[sm, epsc], w=[sm])
        kb.op("dve", lambda e: e.reciprocal(c1(RO), c1(RO)), r=[sm], w=[sm])
        kb.op("dve", lambda e: e.scalar_tensor_tensor(out=o1[:], in0=o1[:], scalar=c1(RO), in1=gn[:], op0=ALU.mult, op1=ALU.mult), r=[o1, sm, gn], w=[o1])
        kb.op("act", lambda e: e.activation(out=z1[:], in_=z1[:], func=AF.Silu), r=[z1], w=[z1])
        kb.op("dve", lambda e: e.tensor_tensor(out=o1[:], in0=o1[:], in1=z1[:], op=ALU.mult), r=[o1, z1], w=[o1])
        kb.dma("sp", [(scr_dno.rearrange("s h d -> (s h) d"), o1[:, :])], r=[o1], w=[r_scr])
        dtok = kb.sb([NS, 512], F32, stack=st)
        dtb = kb.sb([NS, 512], BF16, stack=st)
        kb.dma("sp", [(dtok[:], scr_dno.rearrange("s h d -> s (h d)"))], r=[r_scr], w=[dtok])
        kb.op("dve", lambda e: e.tensor_copy(dtb[:], dtok[:]), r=[dtok], w=[dtb])
        pt = kb.ps()
        ptv = pt[:].bitcast(BF16)
        for f in range(4):
            kb.op("pe", lambda e, f=f: e.transpose(ptv[:, f * NS:(f + 1) * NS], dtb[:, f * 128:(f + 1) * 128], ident_bf[0:NS, 0:NS]), r=[dtb, ident_bf], w=[pt])
        kb.op("dve", lambda e: e.tensor_copy(mixT[:, 0:4, NOWN * 128:TOK], ptv[:, 0:4 * NS].rearrange("p (f t) -> p f t", f=4)), r=[pt], w=[mix_r[16]])
        kb.barrier()


    if "s" in DBG:
      with contextlib.ExitStack() as st:
        kb.barrier()
        F = lambda shape, name, dt=F32: kb.sb(shape, dt, stack=st, name=name)
        w1sb = F([128, 2, 64, 128], "w1sb_s", BF16)
        kb.dma("pool", [(w1sb[:, k, :, :], w1rep[:, k, :, :]) for k in range(2)], w=[w1sb])
        pe_sb, eom, b1t, w2sb = F([128, 256], "pe_s"), F([128, 2], "eom_s"), F([128, 2], "b1t_s"), F([128, 2, 64], "w2sb_s")
        kb.dma("sp", [(pe_sb[:], perep[:, :]), (eom[:], eomin[:, :]), (b1t[:], b1in[:, :]), (w2sb[:], w2in[:, :, :])], w=[pe_sb, eom, b1t, w2sb])
        w2pad = F([128, 2, 2, 128], "w2pad_s", BF16)
        kb.op("pool", lambda e: e.memset(w2pad[:], 0.0), w=[w2pad])
        for k in range(2):
            for g in range(2):
                kb.op("dve", lambda e: e.tensor_copy(w2pad[:, k, g, g * 64:(g + 1) * 64], w2sb[:, k, :]), r=[w2sb], w=[w2pad])
        sc1 = F([1, 512], "sc1")
        kb.dma("sp", [(sc1[:], sconst[:, :])], w=[sc1])
        Ehalf = sc1[0:1, 0:256].rearrange("o (h k) -> o h k", h=2)
        addc1 = sc1[0:1, 256:289]
        onesf = F([128, 128], "onesf_s")
        kb.op("pool", lambda e: e.memset(onesf[:], 1.0), w=[onesf])
        onesb = F([128, 128], "onesb_s", BF16)
        kb.op("pool", lambda e: e.memset(onesb[:], 1.0), w=[onesb])
        selb = F([NS, NS, 128], "selb_s")
        kb.op("dve", lambda e: e.tensor_copy(selb[:], bcast(ident_f[0:NS, 0:NS].unsqueeze(2), [NS, NS, 128])), r=[ident_f], w=[selb])
        sq, skv, sg = F([NS, 512], "sq"), F([NS, 768], "skv"), F([NS, 24], "sg")
        kb.dma("sp", [(sq[:], scr_sproj[:, OFF_Q:OFF_Q + 512]), (skv[:], scr_sproj[:, OFF_KV:OFF_KV + 768]), (sg[:], scr_sproj[:, OFF_G:OFF_G + 24])],
               r=[r_sps], w=[sq, skv, sg])
        kb.op("act", lambda e: e.activation(out=sg[:], in_=sg[:], func=AF.Sigmoid), r=[sg], w=[sg])
        sqp = F([NS, 4, 2, 64], "sqp")
        kb.op("act", lambda e: e.activation(out=sqp[:], in_=sq[:].rearrange("s (g j d) -> s j g d", g=2, j=4), func=AF.Copy, scale=0.125), r=[sq], w=[sqp])
        Q8b, Q8f = F([128, NS, 2, 4], "Q8b", BF16), F([128, NS, 2, 4], "Q8f")
        pq = kb.ps()
        for j in range(4):
            kb.op("pe", lambda e, j=j: e.transpose(pq[:, j * NS:(j + 1) * NS], sqp[:, j, :, :].rearrange("s g d -> s (g d)"), ident_f[0:NS, 0:NS]), r=[sqp, ident_f], w=[pq])
        for g in range(2):
            kb.op("dve", lambda e: e.tensor_scalar(out=Q8f[:, :, g, :], in0=pq[:, 0:4 * NS].rearrange("p (j s) -> p s j", j=4), scalar1=eom[:, g:g + 1], scalar2=None, op0=ALU.mult),
                  r=[pq, eom], w=[Q8f])
        kb.op("dve", lambda e: e.tensor_copy(Q8b[:], Q8f[:]), r=[Q8f], w=[Q8b])
        kTn = F([128, 2, NS], "kTn")
        pk = kb.ps()
        for a, c0 in enumerate((256, 512)):
            kb.op("pe", lambda e, a=a, c0=c0: e.transpose(pk[:, a * NS:(a + 1) * NS], skv[:, c0:c0 + 128], ident_f[0:NS, 0:NS]), r=[skv, ident_f], w=[pk])
        kb.op("act", lambda e: e.activation(out=kTn[:].rearrange("p a s -> p (a s)"), in_=pk[:, 0:2 * NS], func=AF.Copy), r=[pk], w=[kTn])
        pti = F([128, NS * 16], "pti", I32)
        kb.dma("sp", [(pti[:], bass.AP(ptab.tensor, 0, [[0, 128], [1, NS * 16]]))], w=[pti])
        ptf = F([128, NS * 16], "ptf")
        iop = F([128, 1], "iop")
        kb.dma("sp", [(iop[:], iotap[:, :])], w=[iop])
        kb.op("dve", lambda e: e.tensor_copy(ptf[:], pti[:]), r=[pti], w=[ptf])
        kb.op("dve", lambda e: e.tensor_scalar(out=ptf[:], in0=ptf[:], scalar1=128.0, scalar2=iop[:, 0:1], op0=ALU.mult, op1=ALU.add), r=[ptf, iop], w=[ptf])
        kb.op("dve", lambda e: e.tensor_copy(pti[:], ptf[:]), r=[ptf], w=[pti])
        PGs = [F([128, 16, 512], "PG0"), F([128, 16, 512], "PG1")]
        WT = F([128, 4, 256], "WT")
        big = F([128, 16, 4, 64], "big_s")
        PGm = (F([128, 16, 256], "PGe", BF16), F([128, 16, 256], "PGo", BF16))
        qb = F([128, 2, 4, 64], "qb")
        ssl, Pw = F([128, 16, 2, 4], "ssl"), F([128, 4, 2, 4], "Pw")
        Psum = F([128, 8], "Psum")
        HTs = [[F([128, 32], f"HTs{k}{g}", BF16) for g in range(2)] for k in range(2)]
        kcs, vcs = F([128, 32], "kcs", BF16), F([32, 128], "vcs", BF16)
        Pc, Pcb, pnc = F([32, 8], "Pc"), F([32, 8], "Pcb", BF16), F([32, 8], "pnc")
        impT = F([32, 2], "impT_s")
        row = F([1, 2, 34], "row")
        row2 = F([1, 34], "row2")
        m8 = F([1, 16], "m8_s")
        MBrow = F([1, 2, 16, 2], "MBrow")
        Pn = F([NS, 8], "Pn")
        grep = F([128, 24], "grep_s")
        rdn, tbn, accn = F([128, 8], "rdn"), F([128, 8], "tbn"), F([128, 8], "accn")

        def comb(pP, pD, br, first, guard=False):
            kb.op("dve", lambda e: e.tensor_scalar(out=rdn[:], in0=pD[:, 0:8], scalar1=1e-30, scalar2=None, op0=ALU.max), r=[pD], w=[rdn])
            kb.op("dve", lambda e: e.reciprocal(rdn[:], rdn[:]), r=[rdn], w=[rdn])
            kb.op("dve", lambda e: e.tensor_tensor(out=tbn[:], in0=pP[:, 0:8], in1=rdn[:], op=ALU.mult), r=[pP, rdn], w=[tbn])
            gv = grep[:, :].rearrange("p (h b) -> p h b", b=3)[:, :, br]
            if first:
                kb.op("dve", lambda e: e.tensor_tensor(out=accn[:], in0=tbn[:], in1=gv, op=ALU.mult), r=[tbn, grep], w=[accn])
            else:
                kb.op("dve", lambda e: e.tensor_tensor(out=tbn[:], in0=tbn[:], in1=gv, op=ALU.mult), r=[tbn, grep], w=[tbn])
                kb.op("dve", lambda e: e.tensor_tensor(out=accn[:], in0=accn[:], in1=tbn[:], op=ALU.add), r=[accn, tbn], w=[accn])

        for si in range(NS):
            if si == 0:
                kb_gather(kb, PGs[0], pti, cnsa, 0)
            if si + 1 < NS:
                kb_gather(kb, PGs[(si + 1) % 2], pti, cnsa, si + 1)
            PG = PGs[si % 2]
            kb.dma("sp", [(WT[:, t4, :], cwin[si, t4 * 128:(t4 + 1) * 128, :]) for t4 in range(4)], w=[WT])
            pqb = kb.ps()
            kb.op("pe", lambda e: e.matmul(pqb[:, :], selb[:, si, :], sqp[:].rearrange("s j g d -> s (j g d)"), start=True, stop=True), r=[selb, sqp], w=[pqb])
            kb.op("act", lambda e: e.activation(out=qb[:], in_=pqb[:, :].rearrange("p (j g d) -> p g j d", j=4, g=2), func=AF.Copy), r=[pqb], w=[qb])
            pgr = kb.ps()
            kb.op("pe", lambda e: e.matmul(pgr[:, 0:24], selb[:, si, :], sg[:, :], start=True, stop=True), r=[selb, sg], w=[pgr])
            kb.op("act", lambda e: e.activation(out=grep[:], in_=pgr[:, 0:24], func=AF.Copy), r=[pgr], w=[grep])
            kb.op("dve", lambda e: e.tensor_tensor(out=big[:].rearrange("p a b c -> p a (b c)"), in0=PG[:, :, 0:256], in1=bcast(pe_sb[:, :].unsqueeze(1), [128, 16, 256]), op=ALU.add),
                  r=[PG, pe_sb], w=[big])
            for p_ in range(2):
                kb.op("dve", lambda e: e.tensor_scalar(out=PGm[p_][:], in0=big[:].rearrange("p a b c -> p a (b c)"), scalar1=eom[:, p_:p_ + 1], scalar2=None, op0=ALU.mult),
                      r=[big, eom], w=[PGm[p_]])
            for k in range(2):
                for g in range(2):
                    pa = kb.ps()
                    for half in range(2):
                        for d in range(64):
                            kb.op("pe", lambda e: e.matmul(pa[:, half * 16:(half + 1) * 16], w1sb[:, k, d, :], PGm[half][:, :, k * 128 + g * 64 + d],
                                                           start=(d == 0), stop=(d == 63)), r=[w1sb, PGm[half]], w=[pa])
                    kb.op("act", lambda e: e.activation(out=HTs[k][g][:], in_=pa[:, 0:32], func=AF.Relu, bias=b1t[:, k:k + 1]), r=[pa, b1t], w=[HTs[k][g]])
            pk = kb.ps()
            for g in range(2):
                kb.op("pe", lambda e: e.matmul(pk[:, 0:32], w2pad[:, 0, g, :], HTs[0][g][:], start=(g == 0), stop=(g == 1)), r=[w2pad, HTs[0][g]], w=[pk])
            kb.op("act", lambda e: e.activation(out=kcs[:], in_=pk[:, 0:32], func=AF.Copy), r=[pk], w=[kcs])
            pv = kb.ps()
            for g in range(2):
                kb.op("pe", lambda e: e.matmul(pv[0:32, 0:128], HTs[1][g][:], w2pad[:, 1, g, :], start=(g == 0), stop=(g == 1)), r=[w2pad, HTs[1][g]], w=[pv])
            kb.op("act", lambda e: e.activation(out=vcs[:], in_=pv[0:32, 0:128], func=AF.Copy), r=[pv], w=[vcs])
            q8b = Q8b[:, si, :, :].rearrange("p g j -> p (g j)")
            q8f = Q8f[:, si, :, :].rearrange("p g j -> p (g j)")
            pa = kb.ps()
            kb.op("pe", lambda e: e.matmul(pa[0:32, 0:8], kcs[:, :], q8b, start=True, stop=True), r=[kcs, Q8b], w=[pa])
            kb.op("act", lambda e: e.activation(out=Pc[:], in_=pa[0:32, 0:8], func=AF.Exp), r=[pa], w=[Pc])
            kb.op("dve", lambda e: e.tensor_copy(Pcb[:], Pc[:]), r=[Pc], w=[Pcb])
            pP, pD = kb.ps(), kb.ps()
            kb.op("pe", lambda e: e.matmul(pP[:, 0:8], vcs[:, :], Pcb[:, :], start=True, stop=True), r=[vcs, Pcb], w=[pP])
            kb.op("pe", lambda e: e.matmul(pD[:, 0:8], onesb[0:32, :], Pcb[:, :], start=True, stop=True), r=[onesb, Pcb], w=[pD])
            comb(pP, pD, 0, True)
            kb.op("dve", lambda e: e.tensor_tensor(out=pnc[:], in0=Pc[:], in1=rdn[0:32, :], op=ALU.mult), r=[Pc, rdn], w=[pnc])
            kb.op("dve", lambda e: e.tensor_reduce(out=impT[:], in_=pnc[:].rearrange("n (g j) -> n g j", g=2), axis=AX.X, op=ALU.add), r=[pnc], w=[impT])
            pi = kb.ps()
            for g in range(2):
                kb.op("pe", lambda e, g=g: e.transpose(pi[0:1, g * 32:(g + 1) * 32], impT[:, g:g + 1], ident_f[0:32, 0:32]), r=[impT, ident_f], w=[pi])
            kb.op("pool", lambda e: e.memset(row[:], 0.0), w=[row])
            for g in range(2):
                kb.op("dve", lambda e: e.tensor_copy(row[0:1, g, 0:32].rearrange("o (p h) -> o p h", h=2), pi[0:1, g * 32:(g + 1) * 32].rearrange("o (h p) -> o p h", h=2)),
                      r=[pi], w=[row])
                kb.op("dve", lambda e: e.tensor_tensor(out=row[0:1, g, 0:33], in0=row[0:1, g, 0:33], in1=addc1, op=ALU.add), r=[row, sc1], w=[row])
                kb.op("dve", lambda e: e.max(m8[0:1, 0:8], row[0:1, g, 0:33]), r=[row], w=[m8])
                kb.op("dve", lambda e: e.match_replace(row2[0:1, 0:33], m8[0:1, 0:8], row[0:1, g, 0:33], -1e30), r=[row, m8], w=[row2])
                kb.op("dve", lambda e: e.max(m8[0:1, 8:16], row2[0:1, 0:33]), r=[row2], w=[m8])
                kb.op("dve", lambda e: e.tensor_scalar(out=row[0:1, g, 0:33], in0=row[0:1, g, 0:33], scalar1=m8[0:1, 15:16], scalar2=None, op0=ALU.is_ge), r=[row, m8], w=[row])
                kb.op("dve", lambda e: e.tensor_scalar(out=row[0:1, g, 0:33], in0=row[0:1, g, 0:33], scalar1=-1.0, scalar2=30000.0, op0=ALU.add, op1=ALU.mult), r=[row], w=[row])
                kb.op("dve", lambda e: e.tensor_copy(MBrow[0:1, :, :, g], row[0:1, g, 0:32].rearrange("o (p h) -> o h p", h=2)), r=[row], w=[MBrow])
            pmk = kb.ps()
            for hf in range(2):
                kb.op("pe", lambda e, hf=hf: e.matmul(pmk[:, 0:32], Ehalf[0:1, hf, :], MBrow[0:1, hf, :, :].rearrange("o p g -> o (p g)"), start=(hf == 0), stop=(hf == 1)),
                      r=[sc1, MBrow], w=[pmk])
            for g in range(2):
                kb.op("dve", lambda e: e.tensor_tensor(out=big[:], in0=bcast(PG[:, :, 256 + 64 * g:320 + 64 * g].unsqueeze(2), [128, 16, 4, 64]),
                                                       in1=bcast(qb[:, g, :, :].unsqueeze(1), [128, 16, 4, 64]), op=ALU.mult), r=[PG, qb], w=[big])
                kb.op("dve", lambda e: e.tensor_reduce(out=ssl[:, :, g, :], in_=big[:], axis=AX.X, op=ALU.add), r=[big], w=[ssl])
            kb.op("dve", lambda e: e.tensor_tensor(out=ssl[:], in0=ssl[:], in1=bcast(pmk[:, 0:32].rearrange("p (a g) -> p a g", g=2).unsqueeze(3), [128, 16, 2, 4]), op=ALU.add),
                  r=[ssl, pmk], w=[ssl])
            kb.op("act", lambda e: e.activation(out=ssl[:], in_=ssl[:], func=AF.Exp), r=[ssl], w=[ssl])
            for a, (Pt, ntile, vsrc, vofs, vnew0) in enumerate(((ssl, 16, PG, 384, 384), (Pw, 4, WT, 128, 640))):
                if a == 1:
                    for g in range(2):
                        kb.op("dve", lambda e: e.tensor_tensor(out=big[:, 0:4, :, :], in0=bcast(WT[:, :, 64 * g:64 + 64 * g].unsqueeze(2), [128, 4, 4, 64]),
                                                               in1=bcast(qb[:, g, :, :].unsqueeze(1), [128, 4, 4, 64]), op=ALU.mult), r=[WT, qb], w=[big])
                        kb.op("dve", lambda e: e.tensor_reduce(out=Pw[:, :, g, :], in_=big[:, 0:4, :, :], axis=AX.X, op=ALU.add), r=[big], w=[Pw])
                    kb.op("dve", lambda e: e.tensor_scalar(out=Pw[0:1, 0, :, :], in0=Pw[0:1, 0, :, :], scalar1=-30000.0, scalar2=None, op0=ALU.add), r=[Pw], w=[Pw])
                    kb.op("act", lambda e: e.activation(out=Pw[:], in_=Pw[:], func=AF.Exp), r=[Pw], w=[Pw])
                pnw = kb.ps()
                kb.op("pe", lambda e: e.matmul(pnw[0:NS, 0:8], kTn[:, a, :], q8f, start=True, stop=True), r=[kTn, Q8f], w=[pnw])
                kb.op("act", lambda e: e.activation(out=Pn[:], in_=pnw[0:NS, 0:8], func=AF.Exp), r=[pnw], w=[Pn])
                kb.op("dve", lambda e: e.tensor_scalar(out=Pn[:], in0=Pn[:], scalar1=ident_f[0:NS, si:si + 1], scalar2=None, op0=ALU.mult), r=[Pn, ident_f], w=[Pn])
                kb.op("dve", lambda e: e.tensor_reduce(out=Psum[:], in_=Pt[:].rearrange("p a g j -> p (g j) a"), axis=AX.X, op=ALU.add), r=[Pt], w=[Psum])
                pP, pD = kb.ps(), kb.ps()
                for t_ in range(ntile):
                    kb.op("pe", lambda e, t_=t_: e.matmul(pP[:, 0:8], vsrc[:, t_, vofs:vofs + 128], Pt[:, t_, :, :].rearrange("p g j -> p (g j)"), start=(t_ == 0), stop=False),
                          r=[vsrc, Pt], w=[pP])
                kb.op("pe", lambda e: e.matmul(pP[:, 0:8], skv[:, vnew0:vnew0 + 128], Pn[:, :], start=False, stop=True), r=[skv, Pn], w=[pP])
                kb.op("pe", lambda e: e.matmul(pD[:, 0:8], onesf[:, :], Psum[:, :], start=True, stop=False), r=[onesf, Psum], w=[pD])
                kb.op("pe", lambda e: e.matmul(pD[:, 0:8], onesf[0:NS, :], Pn[:, :], start=False, stop=True), r=[onesf, Pn], w=[pD])
                comb(pP, pD, 1 + a, False)
            for g in range(2):
                hr = slice(g * 64, (g + 1) * 64)
                kb.op("act", lambda e: e.activation(out=mixT[hr, 4:8, NOWN * 128 + si], in_=accn[hr, 4 * g:4 * g + 4], func=AF.Copy), r=[accn], w=[mix_r[16]])
        kb.barrier()

    with contextlib.ExitStack() as st:
        if "m" in DBG:
            kb.dma("pool", [(mixT[:, k, :], dbg_mixT[:, k, :]) for k in range(8)], w=mix_r)
        if "n" in DBG:
            kb.dma("pool", [(mixT[:, k, :], dbg_mixT[:, k, :]) for k in range(4, 8)], w=mix_r)
        xres = kb.sb([128, 17, D], F32, stack=st, name="xres")
        xr = [Res(f"xr{i}") for i in range(17)]
        for i, (t0, n) in enumerate(tiles):
            src = xp[(2 * i + 1) * 128:(2 * i + 2) * 128, :] if i < NOWN else xs[:, :]
            kb.dma("sp", [(xres[0:n, i, :], src)], w=[xr[i]])
        hT = kb.sb([128, 8, TOK], BF16, stack=st, name="hT")
        h_r = [Res(f"h{i}") for i in range(17)]
        stg = [(None, kb.sb([128, 1], F32, stack=st), kb.sb([128, 1], F32, stack=st), kb.sb([128, D], BF16, stack=st))] * 2
        wA = [kb.sb([128, 8, 1024], BF16, stack=st, name=f"wA{i}") for i in range(2)]

        def loadA(slot, dram):
            src = dram.rearrange("(k p) n -> p k n", p=128)
            kb.dma("pool", [(wA[slot][:, k, :], src[:, k, :]) for k in range(8)], w=[wA[slot]])
            return wA[slot]

        def linear_add(inT, in_res, w):
            for i, (t0, n) in enumerate(tiles):
                for half in range(2):
                    pa = kb.ps()
                    for kc in range(8):
                        kb.op("pe", lambda e, kc=kc: e.matmul(pa[0:n, :], inT[:, kc, t0:t0 + n], w[:, kc, half * 512:(half + 1) * 512],
                                                              start=(kc == 0), stop=(kc == 7)), r=[in_res[i], w], w=[pa])
                    kb.op("dve", lambda e: e.tensor_tensor(out=xres[0:n, i, half * 512:(half + 1) * 512], in0=pa[0:n, :],
                                                           in1=xres[0:n, i, half * 512:(half + 1) * 512], op=ALU.add), r=[pa, xr[i]], w=[xr[i]])

        def norm_all(ln_idx):
            for i, (t0, n) in enumerate(tiles):
                norm_transpose(xres[0:n, i, :], n, stg[i % 2], hT[:, :, t0:t0 + n], h_r[i], ln_idx, sb_src=xr[i])

        w = loadA(0, w_out)
        wq = loadA(1, w_mem_q)
        linear_add(mixT, mix_r, w)
        if "1" in DBG:
            for i, (t0, n) in enumerate(tiles):
                if i < NOWN:
                    kb.dma("sp", [(o_y[i, :, :], xres[:, i, :])], r=[xr[i]], final=True)
                else:
                    kb.dma("sp", [(o_ys[:, :], xres[0:NS, i, :])], r=[xr[i]], final=True)
        norm_all(1)
        wo = loadA(0, w_mem_o)
        aT = mixT
        a_r = mix_r
        with contextlib.ExitStack() as st2:
            onesb = kb.sb([128, 128], BF16, stack=st2, name="onesb")
            kb.op("pool", lambda e: e.memset(onesb[:], 1.0), w=[onesb])
            onesf = kb.sb([128, 128], F32, stack=st2, name="onesf")
            kb.op("pool", lambda e: e.memset(onesf[:], 1.0), w=[onesf])
            qT = kb.sb([128, 4, 512], BF16, stack=st2, name="qT")
            pT = [kb.sb([128, 512], BF16, stack=st2) for _ in range(2)]
            rden = kb.sb([128, 512], F32, stack=st2, name="rden")
            for blk in range(4):
                b0 = blk * 512
                rr = [h_r[4 * blk + j] for j in range(4)]
                for hd in range(4):
                    if hd % 2 == 0:
                        for c in range(4):
                            pa = kb.ps()
                            cc = 2 * hd + c
                            for kc in range(8):
                                kb.op("pe", lambda e, kc=kc: e.matmul(pa[:, :], wq[:, kc, cc * 128:(cc + 1) * 128], hT[:, kc, b0:b0 + 512],
                                                                      start=(kc == 0), stop=(kc == 7)), r=rr + [wq], w=[pa])
                            kb.op("act", lambda e: e.activation(out=qT[:, c, :], in_=pa[:, :], func=AF.Copy, scale=1.0 / 16), r=[pa], w=[qT])
                    for mt in range(2):
                        pa = kb.ps()
                        for j in range(2):
                            kb.op("pe", lambda e, j=j: e.matmul(pa[:, :], memKT[:, 2 * hd + j, mt * 128:(mt + 1) * 128], qT[:, 2 * (hd % 2) + j, :],
                                                                start=(j == 0), stop=(j == 1)), r=[memKT, qT], w=[pa])
                        kb.op("act", lambda e: e.activation(out=pT[mt][:, :], in_=pa[:, :], func=AF.Exp), r=[pa], w=[pT[mt]])
                    pd = kb.ps()
                    for mt in range(2):
                        kb.op("pe", lambda e, mt=mt: e.matmul(pd[:, :], onesb[:, :], pT[mt][:, :], start=(mt == 0), stop=(mt == 1)), r=[onesb, pT[mt]], w=[pd])
                    kb.op("dve", lambda e: e.reciprocal(rden[:, :], pd[:, :]), r=[pd], w=[rden])
                    for j in range(2):
                        po = kb.ps()
                        for mt in range(2):
                            kb.op("pe", lambda e, mt=mt: e.matmul(po[:, :], memV[:, mt, hd * 256 + j * 128: hd * 256 + (j + 1) * 128], pT[mt][:, :],
                                                                  start=(mt == 0), stop=(mt == 1)), r=[memV, pT[mt]], w=[po])
                        kb.op("dve", lambda e: e.tensor_tensor(out=aT[:, 2 * hd + j, b0:b0 + 512], in0=po[:, :], in1=rden[:, :], op=ALU.mult),
                              r=[po, rden], w=rr_a(blk))
            qs = kb.sb([NS, 1024], BF16, stack=st2, name="qs")
            for half in range(2):
                pa = kb.ps()
                for kc in range(8):
                    kb.op("pe", lambda e, kc=kc: e.matmul(pa[0:NS, :], hT[:, kc, NOWN * 128:TOK], wq[:, kc, half * 512:(half + 1) * 512],
                                                          start=(kc == 0), stop=(kc == 7)), r=[h_r[16], wq], w=[pa])
                kb.op("act", lambda e: e.activation(out=qs[:, half * 512:(half + 1) * 512], in_=pa[0:NS, :], func=AF.Copy, scale=1.0 / 16), r=[pa], w=[qs])
            selb = kb.sb([NS, NS, 128], BF16, stack=st2, name="selb")
            kb.op("dve", lambda e: e.tensor_copy(selb[:], bcast(ident_f[0:NS, 0:NS].unsqueeze(2), [NS, NS, 128])), r=[ident_f], w=[selb])
            ckv = [kb.sb([128, 2, 2, 1024], F32, stack=st2, name="ckv0")] * 2
            sc = kb.sb([128, 8], F32, stack=st2, name="sc")
            rd4 = kb.sb([128, 4], F32, stack=st2, name="rd4")
            for si in range(NS):
                kv = ckv[si % 2]
                kb.dma("sp", [(kv[:, mt, :, :], cmem[si, mt * 128:(mt + 1) * 128, :, :]) for mt in range(2)], w=[kv])
                pq = [kb.ps(), kb.ps()]
                for half in range(2):
                    kb.op("pe", lambda e: e.matmul(pq[half][:, :], selb[:, si, :], qs[:, half * 512:(half + 1) * 512], start=True, stop=True),
                          r=[selb, qs], w=[pq[half]])
                for half in range(2):
                    kb.op("dve", lambda e: e.tensor_tensor(out=kv[:, :, 0, half * 512:(half + 1) * 512], in0=kv[:, :, 0, half * 512:(half + 1) * 512],
                                                           in1=bcast(pq[half][:, :].unsqueeze(1), [128, 2, 512]), op=ALU.mult), r=[kv, pq[half]], w=[kv])
                kb.op("dve", lambda e: e.tensor_reduce(out=sc[:, :].rearrange("p (m h) -> p m h", h=4), in_=kv[:, :, 0, :].rearrange("p m (h d) -> p m h d", h=4), axis=AX.X, op=ALU.add), r=[kv], w=[sc])
                kb.op("act", lambda e: e.activation(out=sc[:, :], in_=sc[:, :], func=AF.Exp), r=[sc], w=[sc])
                pd = kb.ps()
                for mt in range(2):
                    kb.op("pe", lambda e, mt=mt: e.matmul(pd[:, 0:4], onesf[:, :], sc[:, mt * 4:(mt + 1) * 4], start=(mt == 0), stop=(mt == 1)), r=[onesf, sc], w=[pd])
                kb.op("dve", lambda e: e.reciprocal(rd4[:, :], pd[:, 0:4]), r=[pd], w=[rd4])
                po = kb.ps()
                for c in range(8):
                    for mt in range(2):
                        kb.op("pe", lambda e, mt=mt: e.matmul(po[:, c:c + 1], kv[:, mt, 1, c * 128:(c + 1) * 128], sc[:, mt * 4 + c // 2: mt * 4 + c // 2 + 1],
                                                              start=(mt == 0), stop=(mt == 1)), r=[kv, sc], w=[po])
                kb.op("dve", lambda e: e.tensor_tensor(out=aT[:, :, NOWN * 128 + si].rearrange("p (h j) -> p h j", j=2),
                                                       in0=po[:, 0:8].rearrange("p (h j) -> p h j", j=2),
                                                       in1=bcast(rd4[:, :].unsqueeze(2), [128, 4, 2]), op=ALU.mult), r=[po, rd4], w=[a_r[16]])
        kb.barrier()
        linear_add(aT, a_r, wo)
        if "2" in DBG:
            for i, (t0, n) in enumerate(tiles):
                if i < NOWN:
                    kb.dma("sp", [(o_y[i, :, :], xres[:, i, :])], r=[xr[i]], final=True)
                else:
                    kb.dma("sp", [(o_ys[:, :], xres[0:NS, i, :])], r=[xr[i]], final=True)
        norm_all(3)
        wupS = w_up.rearrange("(k p) n -> p k n", p=128)
        wdnS = w_down.rearrange("(c p) n -> p c n", p=128)
        aF = kb.sb([128, 8, 512], BF16, stack=st, name="aF")
        sq = [kb.sb([128, 512], F32, stack=st) for _ in range(2)]
        for qtr in range(4):
            wu, wd = wA[qtr % 2], wA[(qtr + 1) % 2]
            kb.dma("pool", [(wu[:, k, :], wupS[:, k, qtr * 1024:(qtr + 1) * 1024]) for k in range(8)], w=[wu])
            kb.dma("pool", [(wd[:, k, :], wdnS[:, qtr * 8 + k, :]) for k in range(8)], w=[wd])
            for blk in range(5):
                b0 = blk * 512
                bw = 512 if blk < 4 else NS
                rr = [h_r[4 * blk + j] for j in range(4)] if blk < 4 else [h_r[16]]
                for fc in range(8):
                    pa = kb.ps()
                    for kc in range(8):
                        kb.op("pe", lambda e, kc=kc: e.matmul(pa[:, 0:bw], wu[:, kc, fc * 128:(fc + 1) * 128], hT[:, kc, b0:b0 + bw],
                                                              start=(kc == 0), stop=(kc == 7)), r=rr + [wu], w=[pa])
                    sq_ = sq[fc % 2]
                    kb.op("act", lambda e: e.activation(out=sq_[:, 0:bw], in_=pa[:, 0:bw], func=AF.Square), r=[pa], w=[sq_])
                    kb.op("dve", lambda e: e.scalar_tensor_tensor(out=aF[:, fc, 0:bw], in0=pa[:, 0:bw], scalar=0.0, in1=sq_[:, 0:bw],
                                                                  op0=ALU.is_gt, op1=ALU.mult), r=[pa, sq_], w=[aF])
                tl = range(4 * blk, 4 * blk + 4) if blk < 4 else [16]
                for i in tl:
                    t0, n = tiles[i]
                    l0 = t0 - b0
                    for half in range(2):
                        pa = kb.ps()
                        for fc in range(8):
                            kb.op("pe", lambda e, fc=fc: e.matmul(pa[0:n, :], aF[:, fc, l0:l0 + n], wd[:, fc, half * 512:(half + 1) * 512],
                                                                  start=(fc == 0), stop=(fc == 7)), r=[aF, wd], w=[pa])
                        kb.op("dve", lambda e: e.tensor_tensor(out=xres[0:n, i, half * 512:(half + 1) * 512], in0=pa[0:n, :],
                                                               in1=xres[0:n, i, half * 512:(half + 1) * 512], op=ALU.add), r=[pa, xr[i]], w=[xr[i]])
        if "3" in DBG:
            for i, (t0, n) in enumerate(tiles):
                if i < NOWN:
                    kb.dma("sp", [(o_y[i, :, :], xres[:, i, :])], r=[xr[i]], final=True)
                else:
                    kb.dma("sp", [(o_ys[:, :], xres[0:NS, i, :])], r=[xr[i]], final=True)
        gfin = kb.sb([128, D], F32, stack=st, name="gfin")
        kb.dma("sp", [(gfin[:], bass.AP(ln_fin.tensor, 0, [[0, 128], [1, D]]))], w=[gfin])
        yo = [kb.sb([128, D], F32, stack=st) for _ in range(2)]
        for i, (t0, n) in enumerate(tiles):
            _, ss, rstd, xbf = stg[i % 2]
            y = yo[i % 2]
            kb.op("act", lambda e: e.activation(out=y[0:n, :], in_=xres[0:n, i, :], func=AF.Square, accum_out=ss[0:n, 0:1]), r=[xr[i]], w=[y, ss])
            kb.op("act", lambda e: e.activation(out=rstd[0:n, :], in_=ss[0:n, :], func=AF.Sqrt, scale=1.0 / D, bias=epsc[0:n, 0:1]), r=[ss, epsc], w=[rstd])
            kb.op("dve", lambda e: e.reciprocal(rstd[0:n, :], rstd[0:n, :]), r=[rstd], w=[rstd])
            kb.op("dve", lambda e: e.scalar_tensor_tensor(out=y[0:n, :], in0=xres[0:n, i, :], scalar=rstd[0:n, 0:1], in1=gfin[0:n, :],
                                                          op0=ALU.mult, op1=ALU.mult), r=[xr[i], rstd, gfin], w=[y])
            if "1" not in DBG and "2" not in DBG and "3" not in DBG:
                if i < NOWN:
                    kb.dma("sp", [(o_y[i, :, :], y[:, :])], r=[y], final=True)
                else:
                    kb.dma("sp", [(o_ys[:, :], y[0:NS, :])], r=[y], final=True)
        kb.barrier()

    kb.finish()
    return nc


_NC = None


def kernel(**inp):
    global _NC
    f = lambda k: np.ascontiguousarray(np.asarray(inp[k]))
    x_prompt, x_sample, mem_prompt = f("x_prompt"), f("x_sample"), f("mem_prompt")
    cache_win, sconv = f("cache_win_kv"), f("state_dn_conv")
    w_in = f("w_in")[0]
    lnv = np.zeros((128, 6, 8), np.float32)
    for i, k in enumerate(("ln_mix", "ln_mem", "ln_memkv", "ln_ffn")):
        lnv[:, i, :] = f(k)[0].reshape(8, 128).T
    lnv[:, 4, :] = f("ln_final").reshape(8, 128).T
    abrep = np.ascontiguousarray(np.tile(np.stack([f("dn_a_log")[0], f("dn_dt_bias")[0]], axis=1), (NS, 1)))
    dnnorm = np.ascontiguousarray(np.tile(f("dn_norm")[0][None, :], (128, 1)))
    perm = list(range(512))
    for j in range(4):
        perm += [512 + j * 64 + d for d in range(64)] + [512 + (j + 4) * 64 + d for d in range(64)]
    perm = np.array(perm)
    w_out_p = np.ascontiguousarray(f("w_out")[0][perm])
    ii = np.arange(128)
    cols = [(ii[:, None] <= ii[None, :]), np.where(ii[None, :] <= ii[:, None], 0.0, -1e30), np.where(ii[None, :] >= ii[:, None], 0.0, -1e30),
            (ii[None, :] < ii[:, None]), (ii[:, None] // 64 == ii[None, :] // 64)]
    for lv in range(7):
        cols.append((ii[:, None] // (2 << lv) == ii[None, :] // (2 << lv)) & (ii[:, None] // (1 << lv) != ii[None, :] // (1 << lv)))
    cw = f("dn_conv_w")[0]
    cols.append(cw.T.reshape(12, 128, 4).transpose(1, 0, 2).reshape(128, 48))
    cols.append(np.tile(f("dn_a_log")[0][None, :], (128, 1)))
    cols.append(np.tile(f("dn_dt_bias")[0][None, :], (128, 1)))
    cols.append((ii < 64)[:, None])
    cols.append((ii >= 64)[:, None])
    dnc = np.ascontiguousarray(np.concatenate([np.asarray(c, np.float32) for c in cols], axis=1))
    NEG = -30000.0
    w1 = f("cmp_w1")[0]; pe = f("cmp_pe")[0]
    w1rep = np.ascontiguousarray(np.tile(w1.reshape(2, 64, 64, 128).transpose(1, 0, 2, 3), (2, 1, 1, 1)))
    perep = np.ascontiguousarray(np.tile(np.repeat(pe.transpose(1, 0, 2)[:, :, None, :], 2, axis=2).reshape(64, 256), (2, 1)))
    eom = np.ascontiguousarray(np.stack([(ii < 64), (ii >= 64)], axis=1).astype(np.float32))
    b1in = np.ascontiguousarray(f("cmp_b1")[0].T)
    w2in = np.ascontiguousarray(f("cmp_w2")[0].transpose(1, 0, 2))
    qperm = []
    for j in range(4):
        qperm += [OFF_Q + j * 64 + d for d in range(64)] + [OFF_Q + (j + 4) * 64 + d for d in range(64)]
    wq_p = np.ascontiguousarray(w_in[:, qperm])
    keys = np.arange(NT * 128)
    nprime = ((keys % 128) // 64) * NT + keys // 128
    Ein = np.ascontiguousarray((np.arange(66)[:, None] == nprime[None, :]).astype(np.float32))
    tri = np.where(ii[:, None] <= ii[None, :], 0.0, NEG)
    wlo = np.where(ii[:, None] > ii[None, :], 0.0, NEG)
    cnsa_h = f("cache_nsa_kv").reshape(2560 * 128, 512)
    iotap_h = np.arange(128, dtype=np.float32).reshape(128, 1)
    sconst_h = np.zeros((1, 512), np.float32)
    sconst_h[0, 0:64] = 1.0
    sconst_h[0, 128 + 64:256] = 1.0
    sconst_h[0, 256] = 1000.0
    sconst_h[0, 256 + 31] = 1000.0
    sconst_h[0, 256 + 32] = 1000.0

    def nsa_consts(h):
        sh = 1 - h
        mk3 = np.stack([np.tile(tri, (1, 4)), np.tile(wlo, (1, 4)), np.full((128, 512), NEG if sh == 1 else 0.0)], axis=1).astype(np.float32)
        npr = np.arange(66)
        nglob = 2 * (npr % NT - sh) + npr // NT
        cmk = np.zeros((NOWN, 66, 512), np.float32)
        addc = np.zeros((NOWN, 128, 2, 66), np.float32)
        for i in range(NOWN):
            qpos = (2 * i + 1 - sh) * 128 + ii
            vis = (nglob[:, None] >= 0) & (64 * (nglob[:, None] + 1) - 1 <= qpos[None, :])
            cmk[i] = np.tile(np.where(vis, 0.0, NEG), (1, 4))
            cur = qpos // 64
            valid = (nglob[None, :] >= 0) & (nglob[None, :] <= cur[:, None]) & (nglob[None, :] < 64)
            forced = valid & ((nglob[None, :] == 0) | (cur[:, None] - nglob[None, :] < 2))
            addc[i, :, 0, :] = np.where(valid, np.where(forced, 1000.0, 0.0), -1.0)
            addc[i, :, 1, :] = np.where(valid, 0.0, NEG)
        return {"w1rep": w1rep, "perep": perep, "eomin": eom, "b1in": b1in, "w2in": w2in, "wq_p": wq_p, "Ein": Ein,
                "mk3in": np.ascontiguousarray(mk3), "cmkin": cmk, "addcin": addc}

    in_maps = []
    for c in range(8):
        b, h = c // 2, c % 2
        sh = 1 - h
        xpc = np.zeros((NT * 128, D), np.float32)
        xpc[sh * 128: sh * 128 + 4096] = x_prompt[b]
        sl = slice(c * NS, (c + 1) * NS)
        in_maps.append({
            "xp": xpc, "xs": x_sample[sl, 0], "memp": mem_prompt[b], "w_in": w_in,
            "w_mem_kv": f("w_mem_kv")[0], "lnv": lnv,
            "cwin": cache_win[0, sl].reshape(NS, 512, 256), "sconv": sconv[0, sl],
            "conv_w": f("dn_conv_w")[0], "abrep": abrep, "dnnorm": dnnorm,
            "srec": f("state_dn_rec")[0, sl].reshape(128, 64, 64),
            "dncin": dnc, "w_out": w_out_p, **nsa_consts(c % 2),
            "cnsa": cnsa_h, "ptab": np.ascontiguousarray(f("page_table")[sl].reshape(1, NS * 16).astype(np.int32)), "iotap": iotap_h, "sconst": sconst_h, "w_mem_q": f("w_mem_q")[0], "w_mem_o": f("w_mem_o")[0], "w_up": f("w_up")[0], "w_down": f("w_down")[0],
            "ln_fin": f("ln_final").reshape(1, D), "cmem": f("cache_mem_kv")[0, sl].reshape(NS, 256, 2, 1024),
        })
    if "m" in DBG or "n" in DBG:
        dm = inp["_dbg_mix"]
        for c in range(8):
            in_maps[c]["dbg_mixT"] = np.ascontiguousarray(dm[c][:, perm].reshape(-1, 8, 128).transpose(2, 1, 0))
    if _NC is None:
        _NC = build()
    res = run_bass_kernel_spmd(_NC, in_maps, core_ids=list(range(8)))
    R = res.results
    B, S = 4, 4096
    y_prompt = np.zeros((B, S, D), np.float32)
    y_sample = np.zeros((128, 1, D), np.float32)
    p_nsa = np.zeros((1, B, S, 512), np.float32)
    p_win = np.zeros((1, B, 512, 256), np.float32)
    p_mem = np.zeros((1, B, 256, 2048), np.float32)
    p_conv = np.zeros((1, B, 3, 1536), np.float32)
    p_rec = np.zeros((1, B, 8, 64, 64), np.float32)
    s_nsa = np.zeros((1, 128, 512), np.float32)
    s_win = np.zeros((1, 128, 512, 256), np.float32)
    s_conv = np.zeros((1, 128, 3, 1536), np.float32)
    s_rec = np.zeros((1, 128, 8, 64, 64), np.float32)
    for c in range(8):
        b, h = c // 2, c % 2
        sh = 1 - h
        r = R[c]
        sl = slice(c * NS, (c + 1) * NS)
        pn = p_nsa[0, b].reshape(32, 128, 512)
        for i in range(NOWN):
            pn[2 * i + 1 - sh] = r["o_nsa"][i]
        if h == 0:
            p_rec[0, b] = r["o_prec"]
            p_win[0, b] = r["o_win"][1:5].reshape(512, 256)
            p_mem[0, b] = r["o_mem"]
            p_conv[0, b] = r["o_conv"]
        s_nsa[0, sl] = r["o_snsa"]
        s_win[0, sl] = r["o_swin"]
        s_conv[0, sl] = r["o_sconv"]
        s_rec[0, sl] = r["o_srec"].reshape(NS, 8, 64, 64)
        y_sample[sl, 0] = r["o_ys"]
        yp = y_prompt[b].reshape(32, 128, D)
        for i in range(NOWN):
            yp[2 * i + 1 - sh] = r["o_y"][i]
    return (y_prompt, y_sample, p_nsa.reshape(1, B, S, 4, 2, 64), p_win.reshape(1, B, 512, 2, 2, 64),
            p_mem.reshape(1, B, 256, 2, 4, 256), p_conv, p_rec, s_nsa.reshape(1, 128, 1, 4, 2, 64),
            s_win.reshape(1, 128, 512, 2, 2, 64), s_conv, s_rec)
```

```python
import contextlib
import numpy as np
import concourse.bass as bass
import concourse.mybir as mybir
from concourse.bass_utils import run_bass_kernel_spmd

F32 = mybir.dt.float32
BF16 = mybir.dt.bfloat16
I32 = mybir.dt.int32
AF = mybir.ActivationFunctionType
ALU = mybir.AluOpType
AX = mybir.AxisListType

D = 1024
NT = 33
NOWN = 16
NS = 16
IN_COLS = 3368
OFF_Z, OFF_B, OFF_A, OFF_Q, OFF_KV, OFF_G = 1536, 2048, 2056, 2064, 2576, 3344
EPS = 1e-6
import os
DBG = os.environ.get('KDBG', 'abcdpsh')


class Res:
    __slots__ = ("w", "r", "dsem", "dcnt", "name")

    def __init__(self, name=""):
        self.w = {}
        self.r = {}
        self.dsem = None
        self.dcnt = 0
        self.name = name


class T:
    def __init__(self, t, res):
        self.t = t
        self.res = res

    def __getitem__(self, k):
        return self.t[k]


class KB:
    ENG = ("pe", "act", "dve", "pool", "sp")

    def __init__(self):
        self.nc = bass.Bass("TRN2", target_bir_lowering=False)
        nc = self.nc
        self.es = contextlib.ExitStack()
        self.eng = {"pe": nc.tensor, "act": nc.scalar, "dve": nc.vector, "pool": nc.gpsimd, "sp": nc.sync}
        self.sem = {}
        self.cnt = {}
        self.known = {e: {} for e in self.ENG}
        self.semobj = {}
        self.nsem = 0
        for e in self.ENG:
            self._newsem(e)
        self.finals = {}
        self.uid = 0
        self.psum = []
        self.psi = 0
        self.psn = 8

    def _alloc_sem(self):
        self.nsem += 1
        s = self.es.enter_context(self.nc.semaphore(f"s{self.nsem}"))
        key = self.nsem
        self.semobj[key] = s
        return key

    def _newsem(self, e):
        self.sem[e] = self._alloc_sem()
        self.cnt[e] = 0

    def sb(self, shape, dtype, stack=None, name=None):
        self.uid += 1
        st = stack if stack is not None else self.es
        t = st.enter_context(self.nc.sbuf_tensor(name or f"t{self.uid}", list(shape), dtype))
        return T(t, Res(name or f"t{self.uid}"))

    def init_psum(self):
        for i in range(8):
            t = self.es.enter_context(self.nc.psum_tensor(f"ps{i}", [128, 512], F32))
            self.psum.append(T(t, Res(f"ps{i}")))

    def ps(self):
        p = self.psum[self.psi % self.psn]
        self.psi += 1
        return p

    def _wait(self, e, deps):
        eng = self.eng[e]
        kn = self.known[e]
        for key, val in deps.items():
            if e == "pe" and key == self.sem["pe"]:
                continue
            if kn.get(key, 0) >= val:
                continue
            eng.wait_ge(self.semobj[key], val)
            kn[key] = val

    @staticmethod
    def _merge(d, key, val):
        if d.get(key, 0) < val:
            d[key] = val

    def _deps(self, r, w):
        deps = {}
        for x in r:
            for k, v in x.w.items():
                self._merge(deps, k, v)
        for x in w:
            for k, v in x.w.items():
                self._merge(deps, k, v)
            for k, v in x.r.items():
                self._merge(deps, k, v)
        return deps

    @staticmethod
    def _res(lst):
        return [x.res if isinstance(x, T) else x for x in lst]

    def op(self, e, fn, r=(), w=()):
        r = self._res(r)
        w = self._res(w)
        self._wait(e, self._deps(r, w))
        if self.cnt[e] >= 30000:
            self._newsem(e)
        ins = fn(self.eng[e])
        self.cnt[e] += 1
        ins.then_inc(self.semobj[self.sem[e]], 1)
        key, val = self.sem[e], self.cnt[e]
        for x in r:
            self._merge(x.r, key, val)
        for x in w:
            x.w = {key: val}
            x.r = {}
        return ins

    def dma(self, q, pairs, r=(), w=(), final=False, **kw):
        r = self._res(r)
        w = self._res(w)
        self._wait(q, self._deps(r, w))
        owner = w[0] if w else (r[0] if r else None)
        if owner is None:
            owner = Res("anon")
        if owner.dsem is None or owner.dcnt >= 30000 * 16:
            owner.dsem = self._alloc_sem()
            owner.dcnt = 0
        for (o, i) in pairs:
            self.eng[q].dma_start(out=o, in_=i, **kw).then_inc(self.semobj[owner.dsem], 16)
            owner.dcnt += 16
        key, val = owner.dsem, owner.dcnt
        for x in r:
            self._merge(x.r, key, val)
        for x in w:
            x.w = {key: val}
            x.r = {}
        if final:
            self._merge(self.finals, key, val)

    def barrier(self):
        allv = {self.sem[e]: self.cnt[e] for e in self.ENG if self.cnt[e] > 0}
        for e in self.ENG:
            self._wait(e, dict(allv))

    def finish(self):
        self._wait("sp", dict(self.finals))
        self.barrier()
        self.es.close()


def kb_gather(kb, PG, pti, cnsa, si):
    deps = kb._deps([pti.res], [PG.res])
    kb._wait("pool", deps)
    if PG.res.dsem is None:
        PG.res.dsem = kb._alloc_sem()
    for pg_ in range(16):
        c = si * 16 + pg_
        kb.nc.gpsimd.indirect_dma_start(out=PG[:, pg_, :], out_offset=None, in_=cnsa[0:128, :],
                                        in_offset=bass.IndirectOffsetOnAxis(ap=pti[:, c:c + 1], axis=0)).then_inc(kb.semobj[PG.res.dsem], 16)
        PG.res.dcnt += 16
    PG.res.w = {PG.res.dsem: PG.res.dcnt}
    PG.res.r = {}
    kb._merge(pti.res.r, PG.res.dsem, PG.res.dcnt)


def bcast(ap, shape):
    return ap.to_broadcast(list(shape))


def build():
    kb = KB()
    nc = kb.nc
    kb.init_psum()

    def din(name, shape, dt=F32):
        return nc.dram_tensor(name, list(shape), dt, kind="ExternalInput").ap()

    def dout(name, shape, dt=F32):
        return nc.dram_tensor(name, list(shape), dt, kind="ExternalOutput").ap()

    xp = din("xp", [NT * 128, D])
    xs = din("xs", [NS, D])
    memp = din("memp", [256, D])
    w_in = din("w_in", [D, IN_COLS])
    w_mem_kv = din("w_mem_kv", [D, 2048])
    lnv = din("lnv", [128, 6, 8])
    cwin = din("cwin", [NS, 512, 256])
    sconv = din("sconv", [NS, 3, 1536])
    conv_w = din("conv_w", [4, 1536])
    abrep = din("abrep", [128, 2])
    dnnorm = din("dnnorm", [128, 64])
    srec = din("srec", [128, 64, 64])

    o_nsa = dout("o_nsa", [NOWN, 128, 512])
    o_win = dout("o_win", [5, 128, 256])
    o_mem = dout("o_mem", [256, 2048])
    o_conv = dout("o_conv", [3, 1536])
    o_snsa = dout("o_snsa", [NS, 512])
    o_swin = dout("o_swin", [NS, 512, 256])
    o_sconv = dout("o_sconv", [NS, 3, 1536])
    o_srec = dout("o_srec", [128, 64, 64])
    o_y = dout("o_y", [NOWN, 128, D])
    o_prec = dout("o_prec", [8, 64, 64])
    dncin = din("dncin", [128, 1602])
    if "s" in DBG:
        cnsa = din("cnsa", [2560 * 128, 512])
        ptab = din("ptab", [1, NS * 16], I32)
        iotap = din("iotap", [128, 1])
        sconst = din("sconst", [1, 512])
    if "p" in DBG:
        w1rep = din("w1rep", [128, 2, 64, 128])
        perep = din("perep", [128, 256])
        eomin = din("eomin", [128, 2])
        b1in = din("b1in", [128, 2])
        w2in = din("w2in", [128, 2, 64])
        wq_p = din("wq_p", [D, 512])
        Ein = din("Ein", [66, NT * 128])
        mk3in = din("mk3in", [128, 3, 512])
        cmkin = din("cmkin", [NOWN, 66, 512])
        addcin = din("addcin", [NOWN, 128, 2, 66])
    o_ys = dout("o_ys", [NS, D])
    w_out = din("w_out", [D, D])
    w_mem_q = din("w_mem_q", [D, D])
    w_mem_o = din("w_mem_o", [D, D])
    w_up = din("w_up", [D, 4096])
    w_down = din("w_down", [4096, D])
    ln_fin = din("ln_fin", [1, D])
    cmem = din("cmem", [NS, 256, 2, 1024])
    if "m" in DBG or "n" in DBG:
        dbg_mixT = din("dbg_mixT", [128, 8, NOWN * 128 + NS])

    ident_bf = kb.sb([128, 128], BF16, name="ident_bf")
    ident_f = kb.sb([128, 128], F32, name="ident_f")
    kb.op("pool", lambda e: e.memset(ident_f[:], 0.0), w=[ident_f])
    kb.op("pool", lambda e: e.affine_select(out=ident_f[:], in_=ident_f[:], pattern=[[1, 128]],
                                            compare_op=ALU.not_equal, fill=1.0, base=0, channel_multiplier=-1),
          r=[ident_f], w=[ident_f])
    kb.op("dve", lambda e: e.tensor_copy(ident_bf[:], ident_f[:]), r=[ident_f], w=[ident_bf])
    epsc = kb.sb([128, 1], F32, name="epsc")
    kb.op("pool", lambda e: e.memset(epsc[:], EPS), w=[epsc])
    onec = kb.sb([128, 1], F32, name="onec")
    kb.op("pool", lambda e: e.memset(onec[:], 1.0), w=[onec])
    lns = kb.sb([128, 6, 8], F32, name="lns")
    kb.dma("sp", [(lns[:], lnv[:, :, :])], w=[lns])

    def norm_transpose(src_ap, rows, stage, dst_ap, dst_res, ln_idx, gq="sp", sb_src=None):
        xst, ss, rstd, xbf = stage
        if sb_src is None:
            kb.dma(gq, [(xst[0:rows, :], src_ap)], w=[xst])
        else:
            xst = sb_src
        xin = src_ap if sb_src is not None else xst[0:rows, :]
        kb.op("act", lambda e: e.activation(out=xbf[0:rows, :], in_=xin, func=AF.Square,
                                            accum_out=ss[0:rows, 0:1]), r=[xst], w=[xbf, ss])
        kb.op("act", lambda e: e.activation(out=rstd[0:rows, :], in_=ss[0:rows, :], func=AF.Sqrt, scale=1.0 / D,
                                            bias=epsc[0:rows, 0:1]), r=[ss, epsc], w=[rstd])
        kb.op("dve", lambda e: e.reciprocal(rstd[0:rows, :], rstd[0:rows, :]), r=[rstd], w=[rstd])
        kb.op("act", lambda e: e.activation(out=xbf[0:rows, :], in_=xin, func=AF.Copy,
                                            scale=rstd[0:rows, 0:1]), r=[xst, rstd], w=[xbf])
        pt = kb.ps()
        ptv = pt[:].bitcast(BF16)
        for kc in range(8):
            kb.op("pe", lambda e, kc=kc: e.transpose(ptv[:, kc * 128:kc * 128 + rows],
                                                     xbf[0:rows, kc * 128:(kc + 1) * 128], ident_bf[0:rows, 0:rows]),
                  r=[xbf, ident_bf], w=[pt])
        pv3 = ptv.rearrange("p (k t) -> p k t", k=8)[:, :, 0:rows]
        kb.op("dve", lambda e: e.tensor_tensor(out=dst_ap, in0=pv3, in1=bcast(lns[:, ln_idx, :].unsqueeze(2), [128, 8, rows]),
                                               op=ALU.mult), r=[pt, lns], w=[dst_res])

    def rr_a(blk):
        return [mix_r[4 * blk + j] for j in range(4)]

    def load_w(dram_ap, ncols, stack, name):
        wt = kb.sb([128, 8, ncols], BF16, stack=stack, name=name)
        src = dram_ap.rearrange("(k p) n -> p k n", p=128)
        kb.dma("pool", [(wt[:, k, :], src[:, k, :]) for k in range(8)], w=[wt])
        return wt

    TOK = NOWN * 128 + NS
    tiles = [(i * 128, 128) for i in range(NOWN)] + [(NOWN * 128, NS)]
    mixT = kb.sb([128, 8, TOK], BF16, name="mixT")
    mix_r = [Res(f"mix{i}") for i in range(17)]
    front = contextlib.ExitStack()
    scr_sproj = nc.dram_tensor("scr_sproj", [NS, IN_COLS], F32).ap()
    r_sps = Res("scr_sproj")
    xnT = kb.sb([128, 8, NT * 128], BF16, name="xnT", stack=front)
    xnT_res = [Res(f"xnT{t}") for t in range(NT)]
    xsT = kb.sb([128, 8, NS], BF16, name="xsT", stack=front)

    with contextlib.ExitStack() as st:
        stages = []
        for i in range(3):
            stages.append((kb.sb([128, D], F32, stack=st), kb.sb([128, 1], F32, stack=st),
                           kb.sb([128, 1], F32, stack=st), kb.sb([128, D], BF16, stack=st)))
        for t in range(NT):
            norm_transpose(xp[t * 128:(t + 1) * 128, :], 128, stages[t % 3],
                           xnT[:, :, t * 128:(t + 1) * 128], xnT_res[t], 0)
        norm_transpose(xs[:, :], NS, stages[NT % 3], xsT[:, :, :], xsT.res, 0)
        kb.barrier()

    with contextlib.ExitStack() as st:
        sproj = kb.sb([NS, IN_COLS], F32, name="sproj", stack=st)
        co = kb.sb([128, 1536], F32, stack=st)
        wsl = [kb.sb([128, 8, 512], BF16, stack=st) for _ in range(2)]
        wsrc = w_in.rearrange("(k p) n -> p k n", p=128)
        for cb in range(7):
            c0 = cb * 512
            cw = min(512, IN_COLS - c0)
            wt = wsl[cb % 2]
            kb.dma("pool", [(wt[:, k, 0:cw], wsrc[:, k, c0:c0 + cw]) for k in range(8)], w=[wt])
            pa = kb.ps()
            for kc in range(8):
                kb.op("pe", lambda e, kc=kc: e.matmul(pa[0:NS, 0:cw], xsT[:, kc, :], wt[:, kc, 0:cw],
                                                      start=(kc == 0), stop=(kc == 7)), r=[xsT, wt], w=[pa])
            kb.op("dve", lambda e: e.tensor_copy(sproj[:, c0:c0 + cw], pa[0:NS, 0:cw]), r=[pa], w=[sproj])
            if cb < 3:
                pb = kb.ps()
                for kc in range(8):
                    kb.op("pe", lambda e, kc=kc: e.matmul(pb[:, :], xnT[:, kc, (NT - 1) * 128:NT * 128], wt[:, kc, :],
                                                          start=(kc == 0), stop=(kc == 7)), r=[xnT_res[NT - 1], wt], w=[pb])
                kb.op("act", lambda e: e.activation(out=co[:, c0:c0 + 512], in_=pb[:, :], func=AF.Copy), r=[pb], w=[co])
        kb.dma("sp", [(o_conv[:, :], co[125:128, :])], r=[co], final=True)
        kb.dma("sp", [(o_sconv[:, 2, :], sproj[:, 0:1536])], r=[sproj], final=True)
        kb.dma("sp", [(scr_sproj[:, :], sproj[:, :])], r=[sproj], w=[r_sps])
        kb.dma("pool", [(o_sconv[:, 0:2, :], sconv[:, 1:3, :])], final=True)
        kb.barrier()

    if "d" in DBG:
      with contextlib.ExitStack() as st:
        wdn = load_w(w_in[:, 0:1536], 1536, st, "wdn")
        wz = load_w(w_in[:, OFF_Z:OFF_Z + 512], 512, st, "wz")
        wba = load_w(w_in[:, OFF_B:OFF_B + 16], 16, st, "wba")
        NDC = 5 * 128 + 7 * 128 + 48 + 16 + 2
        dnc = kb.sb([128, NDC], F32, stack=st, name="dnc")
        kb.dma("sp", [(dnc[:], dncin[:, :])], w=[dnc])
        UT, NMI, NMT, ST01, BLK = (dnc[:, i * 128:(i + 1) * 128] for i in range(5))
        MO = dnc[:, 640:640 + 896].rearrange("p (l q) -> p l q", l=7)
        CW = dnc[:, 1536:1584].rearrange("p (f j) -> p f j", j=4)
        ALOG, DTB = dnc[:, 1584:1592], dnc[:, 1592:1600]
        EOM = (dnc[:, 1600:1601], dnc[:, 1601:1602])
        gnd = kb.sb([128, 64], F32, stack=st, name="gnd")
        kb.dma("sp", [(gnd[:], dnnorm[:, :])], w=[gnd])
        onesf = kb.sb([128, 128], F32, stack=st, name="onesf_d")
        kb.op("pool", lambda e: e.memset(onesf[:], 1.0), w=[onesf])
        negea = kb.sb([128, 8], F32, stack=st, name="negea")
        kb.op("act", lambda e: e.activation(out=negea[:], in_=ALOG, func=AF.Exp), r=[dnc], w=[negea])
        kb.op("dve", lambda e: e.tensor_scalar(out=negea[:], in0=negea[:], scalar1=-1.0, scalar2=None, op0=ALU.mult), r=[negea], w=[negea])
        eps64 = kb.sb([128, 1], F32, stack=st, name="eps64")
        kb.op("pool", lambda e: e.memset(eps64[:], 64e-6), w=[eps64])
        pre = kb.sb([128, 12, 131], BF16, stack=st, name="pre")
        kb.op("pool", lambda e: e.memset(pre[:], 0.0), w=[pre])
        S = kb.sb([128, 4, 64], F32, stack=st, name="S")
        kb.op("pool", lambda e: e.memset(S[:], 0.0), w=[S])
        F = lambda shape, name: kb.sb(shape, F32, stack=st, name=name)
        cT, tmpc = F([128, 12, 128], "cT"), F([128, 12, 128], "tmpc")
        rs, qkT = F([128, 8, 128], "rs"), F([128, 8, 128], "qkT")
        k_tok, v_tok = F([128, 8, 64], "k_tok"), F([128, 8, 64], "v_tok")
        vkb, kd = kb.sb([128, 8, 128], BF16 if "h" in DBG else F32, stack=st, name="vkb"), F([128, 8, 64], "kd")
        km = (F([128, 4, 128], "km_e"), F([128, 4, 128], "km_o"))
        Sm = (F([128, 2, 64], "Sm_e"), F([128, 2, 64], "Sm_o"))
        w_sb = F([128, 4, 64], "w_sb")
        sm = F([128, 80], "smd")
        BETA, G, GC, EG, EGL, EKD, BG, XX = (sm[:, i * 8:(i + 1) * 8] for i in range(8))
        eglS = sm[:, 64:68]
        G4 = lambda nm: F([128, 4, 128], nm)
        B4 = lambda nm: kb.sb([128, 4, 128], BF16, stack=st, name=nm)
        Dm, DTm, AqkT = B4("Dm"), B4("DTm"), G4("AqkT")
        if "h" not in DBG:
            IX, IX2 = G4("IX"), G4("IX2")
        A, AT = T(rs[:, 0:4, :], rs.res), T(rs[:, 4:8, :], rs.res)
        if "h" in DBG:
            Ta, Tb, Ua, Ub, Nn, NTn = (kb.sb([128, 4, 128], BF16, stack=st, name=n) for n in ("Ta", "Tb", "Ua", "Ub", "Nn", "NTn"))
            IX, IX2 = (kb.sb([128, 4, 128], BF16, stack=st, name=n) for n in ("IXb", "IX2b"))
            diag, t1 = T(cT[:, 0:4, :], cT.res), T(cT[:, 4:8, :], cT.res)
        else:
            diag, t1 = IX, IX2
            Ta, Tb, Ua = (T(cT[:, 4 * i:4 * i + 4, :], cT.res) for i in range(3))
            Ub, Nn, NTn = (T(tmpc[:, 4 * i:4 * i + 4, :], tmpc.res) for i in range(3))
        u_sb, wT, vnew = F([128, 4, 64], "u_sb"), F([128, 2, 128], "wT"), F([128, 4, 64], "vnew")
        o_t = F([128, 8, 64], "o_t")
        obf = T(Dm[:].rearrange("p a b -> p (a b)"), Dm.res)
        HB = [(diag, t1, Dm, DTm, A, AT, AqkT, Ta, Tb, Ua, Ub, Nn, NTn, IX, IX2, u_sb, w_sb, wT, vnew, Sm)]
        HB.append(tuple([T(tmpc[:, 0:4, :], tmpc.res), T(tmpc[:, 4:8, :], tmpc.res), B4("Dm_hb"), B4("DTm_hb"), T(tmpc[:, 8:12, :], tmpc.res), T(cT[:, 8:12, :], cT.res), G4("AqkT_hb")]
                        + [kb.sb([128, 4, 128], BF16, stack=st, name=n + "_hb") for n in ("Ta", "Tb", "Ua", "Ub", "Nn", "NTn", "IX", "IX2")]
                        + [T(k_tok[:, 0:4, :], k_tok.res), T(k_tok[:, 4:8, :], k_tok.res), T(v_tok[:, 4:8, :].rearrange("p (a b) d -> p a (b d)", a=2), v_tok.res),
                           T(v_tok[:, 0:4, :], v_tok.res),
                           (F([128, 2, 64], "Sm_e1"), F([128, 2, 64], "Sm_o1"))]))
        o2 = T(vkb[:, :, 0:64], vkb.res) if "h" not in DBG else T(tmpc[:, 0:4, :].rearrange("p a (b c) -> p (a b) c", c=64), tmpc.res)
        zs = T(kd[:].rearrange("p h d -> p (h d)"), kd.res)
        ro = F([128, 8], "ro")
        idb4 = bcast(ident_f[:, :].unsqueeze(1), [128, 4, 128])
        b4 = lambda ap2: bcast(ap2.unsqueeze(1), [128, 4, 128])
        c4 = lambda ap2: bcast(ap2.unsqueeze(2), [128, 4, 128])
        c64 = lambda ap2, n: bcast(ap2.unsqueeze(2), [128, n, 64])
        V = lambda e: ("dve", e)
        def tt(eng, out, in0, in1, op, r, w):
            kb.op(eng, lambda e: e.tensor_tensor(out=out, in0=in0, in1=in1, op=op), r=r, w=w)
        def cp(out, in_, r, w, func=AF.Copy, **kw):
            kb.op("act", lambda e: e.activation(out=out, in_=in_, func=func, **kw), r=r, w=w)
        for tau in range(int(os.environ.get('KNT', NT))):
            own = tau % 2 == 1
            xt = lambda kc: xnT[:, kc, tau * 128:(tau + 1) * 128]
            xr_ = xnT_res[tau]
            for b3 in range(3):
                pa = kb.ps()
                for f4 in range(4):
                    ft = b3 * 4 + f4
                    for kc in range(8):
                        kb.op("pe", lambda e, kc=kc: e.matmul(pa[:, f4 * 128:(f4 + 1) * 128], wdn[:, kc, ft * 128:(ft + 1) * 128], xt(kc),
                                                              start=(kc == 0), stop=(kc == 7)), r=[xr_, wdn], w=[pa])
                cp(pre[:, b3 * 4:(b3 + 1) * 4, 3:131], pa[:, :].rearrange("p (f t) -> p f t", f=4), [pa], [pre])
            cb = lambda j: bcast(CW[:, :, j].unsqueeze(2), [128, 12, 128])
            tt("dve", cT[:], pre[:, :, 0:128], cb(0), ALU.mult, [pre, dnc], [cT])
            for j in range(1, 4):
                tt("pool", tmpc[:], pre[:, :, j:j + 128], cb(j), ALU.mult, [pre, dnc], [tmpc])
                tt("dve", cT[:], cT[:], tmpc[:], ALU.add, [cT, tmpc], [cT])
            cp(cT[:], cT[:], [cT], [cT], func=AF.Silu)
            kb.op("dve", lambda e: e.tensor_copy(pre[:, :, 0:3], pre[:, :, 128:131]), r=[pre], w=[pre])
            tt("pool", tmpc[:, 0:8, :], cT[:, 0:8, :], cT[:, 0:8, :], ALU.mult, [cT], [tmpc])
            for half in range(2):
                pa = kb.ps()
                kb.op("pe", lambda e: e.matmul(pa[:, :], BLK, tmpc[:, 4 * half:4 * half + 4, :].rearrange("p f t -> p (f t)"), start=True, stop=True),
                      r=[dnc, tmpc], w=[pa])
                cp(rs[:, 4 * half:4 * half + 4, :].rearrange("p f t -> p (f t)"), pa[:, :], [pa, eps64, epsc], [rs], func=AF.Sqrt,
                   scale=(64.0 if half == 0 else 1.0), bias=(eps64[:, 0:1] if half == 0 else epsc[:, 0:1]))
            kb.op("dve", lambda e: e.reciprocal(rs[:], rs[:]), r=[rs], w=[rs])
            tt("dve", qkT[:], cT[:, 0:8, :], rs[:], ALU.mult, [cT, rs], [qkT])
            for (src, f0, dst) in ((qkT, 4, k_tok), (cT, 8, v_tok)):
                pa = kb.ps()
                for f in range(4):
                    kb.op("pe", lambda e, f=f: e.transpose(pa[:, f * 128:(f + 1) * 128], src[:, f0 + f, :], ident_f[:, :]), r=[src, ident_f], w=[pa])
                cp(dst[:].rearrange("p h d -> p (h d)"), pa[:, :], [pa], [dst])
            pb = kb.ps()
            for kc in range(8):
                kb.op("pe", lambda e, kc=kc: e.matmul(pb[:, 0:16], xt(kc), wba[:, kc, :], start=(kc == 0), stop=(kc == 7)), r=[xr_, wba], w=[pb])
            cp(BETA, pb[:, 0:8], [pb], [sm], func=AF.Sigmoid)
            tt("dve", XX, pb[:, 8:16], DTB, ALU.add, [pb, dnc], [sm])
            cp(XX, XX, [sm], [sm], func=AF.Exp)
            cp(XX, XX, [sm, onec], [sm], func=AF.Ln, bias=onec[:, 0:1])
            tt("dve", G, XX, negea[:], ALU.mult, [sm, negea], [sm])
            pg = kb.ps()
            kb.op("pe", lambda e: e.matmul(pg[:, 0:8], UT, G, start=True, stop=True), r=[dnc, sm], w=[pg])
            kb.op("pe", lambda e: e.matmul(pg[:, 8:16], onesf[:, :], G, start=True, stop=True), r=[onesf, sm], w=[pg])
            kb.op("dve", lambda e: e.tensor_copy(GC, pg[:, 0:8]), r=[pg], w=[sm])
            cp(EG, pg[:, 0:8], [pg], [sm], func=AF.Exp)
            cp(EGL, pg[:, 8:16], [pg], [sm], func=AF.Exp)
            tt("dve", XX, pg[:, 8:16], GC, ALU.subtract, [pg, sm], [sm])
            cp(EKD, XX, [sm], [sm], func=AF.Exp)
            tt("dve", BG, BETA, EG, ALU.mult, [sm], [sm])
            kb.op("dve", lambda e: e.tensor_copy(eglS[0:64, :], sm[0:64, 32:40].rearrange("p (a b) -> p a b", b=2)[:, :, 0]), r=[sm], w=[sm])
            kb.op("dve", lambda e: e.tensor_copy(eglS[64:128, :], sm[64:128, 32:40].rearrange("p (a b) -> p a b", b=2)[:, :, 1]), r=[sm], w=[sm])
            tt("dve", vkb[:, :, 0:64], v_tok[:], c64(BETA, 8), ALU.mult, [v_tok, sm], [vkb])
            tt("dve", vkb[:, :, 64:128], k_tok[:], c64(BG, 8), ALU.mult, [k_tok, sm], [vkb])
            for p_ in range(2):
                kb.op("dve", lambda e: e.tensor_scalar(out=km[p_][:], in0=qkT[:, 4:8, :], scalar1=EOM[p_], scalar2=None, op0=ALU.mult), r=[qkT, dnc], w=[km[p_]])
            tt("pool", kd[:], k_tok[:], c64(EKD, 8), ALU.mult, [k_tok, sm], [kd])
            def hg_body(hg, B):
                (diag, t1, Dm, DTm, A, AT, AqkT, Ta, Tb, Ua, Ub, Nn, NTn, IX, IX2, u_sb, w_sb, wT, vnew, Sm) = B
                hs = list(range(4 * hg, 4 * hg + 4))
                gch = sm[:, 16 + 4 * hg:16 + 4 * hg + 4]
                KST = int(os.environ.get('KST', 9))
                tt("dve", diag[:], idb4, c4(gch), ALU.mult, [ident_f, sm], [diag])
                pG = kb.ps()
                kb.op("pe", lambda e: e.matmul(pG[:, :], onesf[:, :], diag[:].rearrange("p h t -> p (h t)"), start=True, stop=True), r=[onesf, diag], w=[pG])
                pG3 = pG[:, :].rearrange("p (h t) -> p h t", h=4)
                tt("dve", t1[:], b4(NMI), pG3, ALU.subtract, [dnc, pG], [t1])
                tt("dve", t1[:], t1[:], c4(gch), ALU.add, [t1, sm], [t1])
                cp(Dm[:], t1[:], [t1], [Dm], func=AF.Exp)
                tt("dve", t1[:], pG3, b4(NMT), ALU.add, [dnc, pG], [t1])
                tt("dve", t1[:], t1[:], c4(gch), ALU.subtract, [t1, sm], [t1])
                cp(DTm[:], t1[:], [t1], [DTm], func=AF.Exp)
                tt("pool", Dm[:], Dm[:], b4(ST01), ALU.mult, [Dm, dnc], [Dm])
                yield
                if KST < 2:
                    return
                pGr, pQK = kb.ps(), kb.ps()
                for i, h in enumerate(hs):
                    kmh = km[h % 2]
                    kb.op("pe", lambda e: e.matmul(pGr[:, i * 128:(i + 1) * 128], kmh[:, h // 2, :], qkT[:, 4 + h // 2, :], start=True, stop=True), r=[kmh, qkT], w=[pGr])
                    kb.op("pe", lambda e: e.matmul(pQK[:, i * 128:(i + 1) * 128], kmh[:, h // 2, :], qkT[:, h // 2, :], start=True, stop=True), r=[kmh, qkT], w=[pQK])
                tt("dve", A[:], pGr[:, :].rearrange("p (h t) -> p h t", h=4), Dm[:], ALU.mult, [pGr, Dm], [A])
                tt("dve", A[:], A[:], c4(sm[:, 4 * hg:4 * hg + 4]), ALU.mult, [A, sm], [A])
                tt("dve", AqkT[:], pQK[:, :].rearrange("p (h t) -> p h t", h=4), DTm[:], ALU.mult, [pQK, DTm], [AqkT])
                yield
                if KST < 3:
                    return
                pa = kb.ps()
                for i in range(4):
                    kb.op("pe", lambda e, i=i: e.transpose(pa[:, i * 128:(i + 1) * 128], A[:, i, :], ident_f[:, :]), r=[A, ident_f], w=[pa])
                cp(AT[:].rearrange("p h t -> p (h t)"), pa[:, :], [pa], [AT])
                T_, U_, Tn, Un = Ta, Ua, Tb, Ub
                tt("pool", Nn[:], A[:], b4(MO[:, 0, :]), ALU.mult, [A, dnc], [Nn])
                tt("dve", T_[:], idb4, Nn[:], ALU.subtract, [ident_f, Nn], [T_])
                tt("pool", NTn[:], AT[:], b4(MO[:, 0, :]), ALU.mult, [AT, dnc], [NTn])
                tt("dve", U_[:], idb4, NTn[:], ALU.subtract, [ident_f, NTn], [U_])
                yield
                for lv in range(1, int(os.environ.get('KLV', 7))):
                    last = lv == 6
                    tt("pool", Nn[:], A[:], b4(MO[:, lv, :]), ALU.mult, [A, dnc], [Nn])
                    pX2 = kb.ps()
                    for i in range(4):
                        kb.op("pe", lambda e, i=i: e.matmul(pX2[:, i * 128:(i + 1) * 128], Nn[:, i, :], U_[:, i, :], start=True, stop=True), r=[Nn, U_], w=[pX2])
                    tt("dve", IX2[:], idb4, pX2[:, :].rearrange("p (h t) -> p h t", h=4), ALU.subtract, [ident_f, pX2], [IX2])
                    yield
                    pU = kb.ps()
                    for i in range(4):
                        kb.op("pe", lambda e, i=i: e.matmul(pU[:, i * 128:(i + 1) * 128], T_[:, i, :], IX2[:, i, :], start=True, stop=True), r=[T_, IX2], w=[pU])
                    if not last:
                        tt("pool", NTn[:], AT[:], b4(MO[:, lv, :]), ALU.mult, [AT, dnc], [NTn])
                        pX = kb.ps()
                        for i in range(4):
                            kb.op("pe", lambda e, i=i: e.matmul(pX[:, i * 128:(i + 1) * 128], NTn[:, i, :], T_[:, i, :], start=True, stop=True), r=[NTn, T_], w=[pX])
                        tt("dve", IX[:], idb4, pX[:, :].rearrange("p (h t) -> p h t", h=4), ALU.subtract, [ident_f, pX], [IX])
                        yield
                        pT_ = kb.ps()
                        for i in range(4):
                            kb.op("pe", lambda e, i=i: e.matmul(pT_[:, i * 128:(i + 1) * 128], U_[:, i, :], IX[:, i, :], start=True, stop=True), r=[U_, IX], w=[pT_])
                        cp(Tn[:].rearrange("p h t -> p (h t)"), pT_[:, :], [pT_], [Tn])
                    cp(Un[:].rearrange("p h t -> p (h t)"), pU[:, :], [pU], [Un])
                    T_, Tn = Tn, T_
                    U_, Un = Un, U_
                    yield
                if KST < 4:
                    return
                pu = kb.ps()
                for i, h in enumerate(hs):
                    kb.op("pe", lambda e: e.matmul(pu[:, i * 128:(i + 1) * 128], U_[:, i, :], vkb[:, h, :], start=True, stop=True), r=[U_, vkb], w=[pu])
                pu3 = pu[:, :].rearrange("p (h c) -> p h c", h=4)
                cp(u_sb[:], pu3[:, :, 0:64], [pu], [u_sb])
                cp(w_sb[:], pu3[:, :, 64:128], [pu], [w_sb])
                pwt = kb.ps()
                for a in range(2):
                    kb.op("pe", lambda e, a=a: e.transpose(pwt[:, a * 128:(a + 1) * 128], w_sb[:, 2 * a:2 * a + 2, :].rearrange("p h d -> p (h d)"), ident_f[:, :]),
                          r=[w_sb, ident_f], w=[pwt])
                cp(wT[:].rearrange("p a t -> p (a t)"), pwt[:, 0:256], [pwt], [wT])
                yield
                for p_ in range(2):
                    kb.op("dve", lambda e: e.tensor_scalar(out=Sm[p_][:], in0=S[:, 2 * hg:2 * hg + 2, :], scalar1=EOM[p_], scalar2=None, op0=ALU.mult), r=[S, dnc], w=[Sm[p_]])
                if KST < 5:
                    return
                pws = kb.ps()
                for i, h in enumerate(hs):
                    kb.op("pe", lambda e: e.matmul(pws[:, i * 64:(i + 1) * 64], wT[:, i // 2, :], Sm[h % 2][:, i // 2, :], start=True, stop=True), r=[wT, Sm[h % 2]], w=[pws])
                tt("dve", vnew[:], u_sb[:], pws[:, 0:256].rearrange("p (h d) -> p h d", h=4), ALU.subtract, [u_sb, pws], [vnew])
                yield
                if own:
                    pqs, pav = kb.ps(), kb.ps()
                    for i, h in enumerate(hs):
                        kb.op("pe", lambda e: e.matmul(pqs[:, i * 64:(i + 1) * 64], qkT[:, h // 2, :], Sm[h % 2][:, i // 2, :], start=True, stop=True), r=[qkT, Sm[h % 2]], w=[pqs])
                        kb.op("pe", lambda e: e.matmul(pav[:, i * 64:(i + 1) * 64], AqkT[:, i, :], vnew[:, i, :], start=True, stop=True), r=[AqkT, vnew], w=[pav])
                    osl = o_t[:, 4 * hg:4 * hg + 4, :]
                    tt("dve", osl, pqs[:, 0:256].rearrange("p (h d) -> p h d", h=4), c64(sm[:, 24 + 4 * hg:24 + 4 * hg + 4], 4), ALU.mult, [pqs, sm], [o_t])
                    tt("dve", osl, osl, pav[:, 0:256].rearrange("p (h d) -> p h d", h=4), ALU.add, [o_t, pav], [o_t])
                if KST < 6:
                    return
                pS = kb.ps()
                for a in range(2):
                    kb.op("pe", lambda e, a=a: e.matmul(pS[:, a * 128:(a + 1) * 128], kd[:, 4 * hg + 2 * a:4 * hg + 2 * a + 2, :].rearrange("p h d -> p (h d)"),
                                                        vnew[:, 2 * a:2 * a + 2, :].rearrange("p h d -> p (h d)"), start=True, stop=True), r=[kd, vnew], w=[pS])
                Ssl = S[:, 2 * hg:2 * hg + 2, :]
                tt("dve", Ssl, Ssl, bcast(eglS[:, 2 * hg:2 * hg + 2].unsqueeze(2), [128, 2, 64]), ALU.mult, [S, sm], [S])
                for a in range(2):
                    tt("dve", S[0:64, 2 * hg + a, :], S[0:64, 2 * hg + a, :], pS[0:64, a * 128:a * 128 + 64], ALU.add, [S, pS], [S])
                    tt("dve", S[64:128, 2 * hg + a, :], S[64:128, 2 * hg + a, :], pS[64:128, a * 128 + 64:a * 128 + 128], ALU.add, [S, pS], [S])
                yield
            gens = [hg_body(hg_, HB[hg_]) for hg_ in range(2)]
            while gens:
                for g_ in list(gens):
                    try:
                        next(g_)
                    except StopIteration:
                        gens.remove(g_)
            if own:
                pz = kb.ps()
                for kc in range(8):
                    kb.op("pe", lambda e, kc=kc: e.matmul(pz[:, :], xt(kc), wz[:, kc, :], start=(kc == 0), stop=(kc == 7)), r=[xr_, wz], w=[pz])
                cp(zs[:], pz[:, :], [pz], [zs], func=AF.Silu)
                tt("pool", o2[:], o_t[:], o_t[:], ALU.mult, [o_t], [o2])
                kb.op("dve", lambda e: e.tensor_reduce(out=ro[:], in_=o2[:], axis=AX.X, op=ALU.add), r=[o2], w=[ro])
                cp(ro[:], ro[:], [ro, epsc], [ro], func=AF.Sqrt, scale=1.0 / 64, bias=epsc[:, 0:1])
                kb.op("dve", lambda e: e.reciprocal(ro[:], ro[:]), r=[ro], w=[ro])
                tt("dve", o_t[:], o_t[:], c64(ro[:, :], 8), ALU.mult, [o_t, ro], [o_t])
                tt("dve", o_t[:], o_t[:], bcast(gnd[:, :].unsqueeze(1), [128, 8, 64]), ALU.mult, [o_t, gnd], [o_t])
                tt("dve", obf[:], o_t[:].rearrange("p h d -> p (h d)"), zs[:], ALU.mult, [o_t, zs], [obf])
                pt = kb.ps()
                ptv = pt[:].bitcast(BF16)
                for f in range(4):
                    kb.op("pe", lambda e, f=f: e.transpose(ptv[:, f * 128:(f + 1) * 128], obf[:, f * 128:(f + 1) * 128], ident_bf[:, :]), r=[obf, ident_bf], w=[pt])
                i_own = tau // 2
                kb.op("dve", lambda e: e.tensor_copy(mixT[:, 0:4, i_own * 128:(i_own + 1) * 128], ptv[:, 0:512].rearrange("p (f t) -> p f t", f=4)), r=[pt], w=[mix_r[i_own]])
        for par in range(2):
            kb.dma("sp", [(o_prec.rearrange("(a two) k v -> two k a v", two=2)[par], S[par * 64:(par + 1) * 64, :, :])], r=[S], final=True)
        kb.barrier()

    if "p" in DBG:
        kTs = kb.sb([128, NT * 128], BF16, name="kTs", stack=front)
        kTw = kb.sb([128, NT * 128], BF16, name="kTw", stack=front)
        Vs = kb.sb([128, NT, 128], BF16, name="Vs", stack=front)
        Vw = kb.sb([128, NT, 128], BF16, name="Vw", stack=front)
        kcT = kb.sb([128, 66], BF16, name="kcT", stack=front)
        vc = kb.sb([66, 128], BF16, name="vc", stack=front)
        eom = kb.sb([128, 2], F32, stack=front, name="eom")
    with contextlib.ExitStack() as st:
        if "p" in DBG:
            CK = (kb.sb([128, NT, 256], BF16, stack=st, name="CKe"), kb.sb([128, NT, 256], BF16, stack=st, name="CKo"))
            b1t = kb.sb([128, 2], F32, stack=st, name="b1t")
            kb.dma("sp", [(eom[:], eomin[:, :]), (b1t[:], b1in[:, :])], w=[eom, b1t])
            w2pad = kb.sb([128, 2, 2, 128], BF16, stack=st, name="w2pad")
            kb.op("pool", lambda e: e.memset(w2pad[:], 0.0), w=[w2pad])
        sti = contextlib.ExitStack()
        wkv = load_w(w_in[:, OFF_KV:OFF_KV + 768], 768, sti, "wkv")
        ost = [kb.sb([128, 768], F32, stack=sti) for _ in range(2)]
        if "p" in DBG:
            pe_sb = kb.sb([128, 256], F32, stack=sti, name="pe_sb")
            ckt = kb.sb([128, 256], F32, stack=sti, name="ckt")
            kb.dma("sp", [(pe_sb[:], perep[:, :])], w=[pe_sb])
            w2sb = kb.sb([128, 2, 64], F32, stack=sti, name="w2sb")
            kb.dma("sp", [(w2sb[:], w2in[:, :, :])], w=[w2sb])
            for k in range(2):
                for g in range(2):
                    kb.op("dve", lambda e: e.tensor_copy(w2pad[:, k, g, g * 64:(g + 1) * 64], w2sb[:, k, :]), r=[w2sb], w=[w2pad])
        for t in range(NT):
            pa, pb = kb.ps(), kb.ps()
            for kc in range(8):
                kb.op("pe", lambda e, kc=kc: e.matmul(pa[:, 0:384], xnT[:, kc, t * 128:(t + 1) * 128], wkv[:, kc, 0:384],
                                                      start=(kc == 0), stop=(kc == 7)), r=[xnT_res[t], wkv], w=[pa])
            for kc in range(8):
                kb.op("pe", lambda e, kc=kc: e.matmul(pb[:, 0:384], xnT[:, kc, t * 128:(t + 1) * 128], wkv[:, kc, 384:768],
                                                      start=(kc == 0), stop=(kc == 7)), r=[xnT_res[t], wkv], w=[pb])
            o = ost[t % 2]
            kb.op("act", lambda e: e.activation(out=o[:, 0:384], in_=pa[:, 0:384], func=AF.Copy), r=[pa], w=[o])
            kb.op("dve", lambda e: e.tensor_copy(o[:, 384:768], pb[:, 0:384]), r=[pb], w=[o])
            if t % 2 == 1:
                kb.dma("sp", [(o_nsa[t // 2, :, :], o[:, 0:512])], r=[o], final=True)
            if t >= NT - 5:
                kb.dma("sp", [(o_win[t - (NT - 5), :, :], o[:, 512:768])], r=[o], final=True)
            if "p" in DBG:
                kb.op("pool", lambda e: e.tensor_copy(Vs[:, t, :], o[:, 384:512]), r=[o], w=[Vs])
                kb.op("pool", lambda e: e.tensor_copy(Vw[:, t, :], o[:, 640:768]), r=[o], w=[Vw])
                kb.op("dve", lambda e: e.tensor_tensor(out=ckt[:], in0=o[:, 0:256], in1=pe_sb[:], op=ALU.add), r=[o, pe_sb], w=[ckt])
                for p_ in range(2):
                    kb.op("dve", lambda e: e.tensor_scalar(out=CK[p_][:, t, :], in0=ckt[:], scalar1=eom[:, p_:p_ + 1], scalar2=None, op0=ALU.mult), r=[ckt, eom], w=[CK[p_]])
                for (c0, dst) in ((256, kTs), (512, kTw)):
                    pc = kb.ps()
                    for kc in range(8):
                        kb.op("pe", lambda e, kc=kc: e.matmul(pc[:, 0:128], wkv[:, kc, c0:c0 + 128], xnT[:, kc, t * 128:(t + 1) * 128],
                                                              start=(kc == 0), stop=(kc == 7)), r=[xnT_res[t], wkv], w=[pc])
                    kb.op("act", lambda e: e.activation(out=dst[:, t * 128:(t + 1) * 128], in_=pc[:, 0:128], func=AF.Copy), r=[pc], w=[dst])
        pa, pb = kb.ps(), kb.ps()
        for kc in range(8):
            kb.op("pe", lambda e, kc=kc: e.matmul(pa[0:NS, 0:384], xsT[:, kc, :], wkv[:, kc, 0:384],
                                                  start=(kc == 0), stop=(kc == 7)), r=[xsT, wkv], w=[pa])
        for kc in range(8):
            kb.op("pe", lambda e, kc=kc: e.matmul(pb[0:NS, 0:384], xsT[:, kc, :], wkv[:, kc, 384:768],
                                                  start=(kc == 0), stop=(kc == 7)), r=[xsT, wkv], w=[pb])
        so = kb.sb([NS, 768], F32, stack=sti)
        kb.op("act", lambda e: e.activation(out=so[:, 0:384], in_=pa[0:NS, 0:384], func=AF.Copy), r=[pa], w=[so])
        kb.op("dve", lambda e: e.tensor_copy(so[:, 384:768], pb[0:NS, 0:384]), r=[pb], w=[so])
        kb.dma("sp", [(o_snsa[:, :], so[:, 0:512])], r=[so], final=True)
        kb.dma("sp", [(o_swin[:, 511, :], so[:, 512:768])], r=[so], final=True)
        if "a" in DBG:
            kb.dma("pool", [(o_swin[s, 0:511, :].rearrange("(a b) d -> a (b d)", a=73),
                             cwin[s, 1:512, :].rearrange("(a b) d -> a (b d)", a=73)) for s in range(NS)], final=True)
        kb.barrier()
        sti.close()
        if "p" in DBG:
            w1sb = kb.sb([128, 2, 64, 128], BF16, stack=st, name="w1sb")
            kb.dma("pool", [(w1sb[:, k, :, :], w1rep[:, k, :, :]) for k in range(2)], w=[w1sb])
            HTs = [[kb.sb([128, 66], BF16, stack=st) for g in range(2)] for k in range(2)]
            for k in range(2):
                for g in range(2):
                    pa = kb.ps()
                    for half in range(2):
                        for d in range(64):
                            kb.op("pe", lambda e: e.matmul(pa[:, half * NT:(half + 1) * NT], w1sb[:, k, d, :],
                                                           CK[half][:, :, k * 128 + g * 64 + d], start=(d == 0), stop=(d == 63)), r=[w1sb, CK[half]], w=[pa])
                    kb.op("act", lambda e: e.activation(out=HTs[k][g][:], in_=pa[:, 0:66], func=AF.Relu, bias=b1t[:, k:k + 1]), r=[pa, b1t], w=[HTs[k][g]])
            pk = kb.ps()
            for g in range(2):
                kb.op("pe", lambda e: e.matmul(pk[:, 0:66], w2pad[:, 0, g, :], HTs[0][g][:], start=(g == 0), stop=(g == 1)), r=[w2pad, HTs[0][g]], w=[pk])
            kb.op("act", lambda e: e.activation(out=kcT[:], in_=pk[:, 0:66], func=AF.Copy), r=[pk], w=[kcT])
            pv = kb.ps()
            for g in range(2):
                kb.op("pe", lambda e: e.matmul(pv[0:66, 0:128], HTs[1][g][:], w2pad[:, 1, g, :], start=(g == 0), stop=(g == 1)), r=[w2pad, HTs[1][g]], w=[pv])
            kb.op("act", lambda e: e.activation(out=vc[:], in_=pv[0:66, 0:128], func=AF.Copy), r=[pv], w=[vc])
        kb.barrier()

    if "p" in DBG:
      with contextlib.ExitStack() as st:
        kb.barrier()
        kb.psn = 6
        accP, accD = kb.psum[6], kb.psum[7]
        Esb = kb.sb([66, NT * 128], BF16, stack=st, name="Esb")
        kb.dma("pool", [(Esb[:], Ein[:, :])], w=[Esb])
        mk3 = kb.sb([128, 3, 512], BF16, stack=st, name="mk3")
        kb.dma("pool", [(mk3[:, m, :], mk3in[:, m, :]) for m in range(3)], w=[mk3])
        onesb = kb.sb([128, 128], BF16, stack=st, name="onesb_n")
        kb.op("pool", lambda e: e.memset(onesb[:], 1.0), w=[onesb])
        onesf = kb.sb([128, 128], F32, stack=st, name="onesf_n")
        kb.op("pool", lambda e: e.memset(onesf[:], 1.0), w=[onesf])
        cmk = [kb.sb([66, 512], BF16, stack=st) for _ in range(2)]
        addc = [kb.sb([128, 2, 66], F32, stack=st) for _ in range(2)]
        PT = [kb.sb([128, 512], BF16, stack=st) for _ in range(2)]
        MBT4 = kb.sb([66, 4, 128], BF16, stack=st, name="MBT4")
        rd = kb.sb([128, 512], F32, stack=st, name="rd")
        tb = kb.sb([128, 512], F32, stack=st, name="tb")
        acc = kb.sb([128, 512], F32, stack=st, name="acc")
        grep = kb.sb([128, 3, 512], BF16, stack=st, name="grep")
        dg = kb.sb([128, 4, 128], F32, stack=st, name="dg")
        pn = kb.sb([66, 4, 128], F32, stack=st, name="pn")
        impT = kb.sb([66, 128], F32, stack=st, name="impT")
        sc = kb.sb([128, 66], F32, stack=st, name="sc_n")
        sc2 = kb.sb([128, 66], F32, stack=st, name="sc2")
        m8 = kb.sb([128, 16], F32, stack=st, name="m8")

        qm = (kb.sb([128, 4, 128], BF16, name="qm_e", stack=st), kb.sb([128, 4, 128], BF16, name="qm_o", stack=st))
        gtok = kb.sb([128, 1, 24], F32, name="gtok", stack=st)
        wqn = load_w(wq_p[:, :], 512, st, "wqn")
        wg = load_w(w_in[:, OFF_G:OFF_G + 24], 24, st, "wg")
        def combine(g, br, first):
            hr = slice(g * 64, (g + 1) * 64)
            kb.op("dve", lambda e: e.tensor_scalar(out=rd[hr, :], in0=accD[hr, :], scalar1=1e-30, scalar2=None, op0=ALU.max), r=[accD], w=[rd])
            kb.op("dve", lambda e: e.reciprocal(rd[hr, :], rd[hr, :]), r=[rd], w=[rd])
            kb.op("dve", lambda e: e.tensor_tensor(out=tb[hr, :], in0=accP[hr, :], in1=rd[hr, :], op=ALU.mult), r=[accP, rd], w=[tb])
            if first:
                kb.op("dve", lambda e: e.tensor_tensor(out=acc[hr, :], in0=tb[hr, :], in1=grep[hr, br, :], op=ALU.mult), r=[tb, grep], w=[acc])
            else:
                kb.op("dve", lambda e: e.tensor_tensor(out=tb[hr, :], in0=tb[hr, :], in1=grep[hr, br, :], op=ALU.mult), r=[tb, grep], w=[tb])
                kb.op("dve", lambda e: e.tensor_tensor(out=acc[hr, :], in0=acc[hr, :], in1=tb[hr, :], op=ALU.add), r=[acc, tb], w=[acc])

        for i in range(NOWN):
            tq = 2 * i + 1
            kb.dma("pool", [(cmk[i % 2][:], cmkin[i, :, :])], w=[cmk[i % 2]])
            kb.dma("sp", [(addc[i % 2][:], addcin[i, :, :, :])], w=[addc[i % 2]])
            t = 2 * i + 1
            pa = kb.ps()
            for j in range(4):
                for kc in range(8):
                    kb.op("pe", lambda e, kc=kc: e.matmul(pa[:, j * 128:(j + 1) * 128], wqn[:, kc, j * 128:(j + 1) * 128], xnT[:, kc, t * 128:(t + 1) * 128],
                                                          start=(kc == 0), stop=(kc == 7)), r=[xnT_res[t], wqn], w=[pa])
            for p_ in range(2):
                kb.op("dve", lambda e: e.tensor_scalar(out=qm[p_][:, :, :], in0=pa[:, :].rearrange("p (j t) -> p j t", j=4),
                                                       scalar1=eom[:, p_:p_ + 1], scalar2=0.125, op0=ALU.mult, op1=ALU.mult), r=[pa, eom], w=[qm[p_]])
            pg = kb.ps()
            for kc in range(8):
                kb.op("pe", lambda e, kc=kc: e.matmul(pg[:, 0:24], xnT[:, kc, t * 128:(t + 1) * 128], wg[:, kc, :], start=(kc == 0), stop=(kc == 7)),
                      r=[xnT_res[t], wg], w=[pg])
            kb.op("act", lambda e: e.activation(out=gtok[:, 0, :], in_=pg[:, 0:24], func=AF.Sigmoid), r=[pg], w=[gtok])

            for g in range(2):
                qr = qm[g][:, :, :]
                for br in range(3):
                    gcols = gtok[:, 0, :].rearrange("p (h b) -> p h b", b=3)[:, 4 * g:4 * g + 4, br]
                    kb.op("dve", lambda e: e.tensor_tensor(out=dg[:], in0=bcast(ident_f[:, :].unsqueeze(1), [128, 4, 128]),
                                                           in1=bcast(gcols.unsqueeze(2), [128, 4, 128]), op=ALU.mult), r=[ident_f, gtok], w=[dg])
                    pgp = kb.ps()
                    kb.op("pe", lambda e: e.matmul(pgp[:, :], onesf[:, :], dg[:].rearrange("p j t -> p (j t)"), start=True, stop=True), r=[onesf, dg], w=[pgp])
                    kb.op("act", lambda e: e.activation(out=grep[:, br, :], in_=pgp[:, :], func=AF.Copy), r=[pgp], w=[grep])
                pa = kb.ps()
                kb.op("pe", lambda e: e.matmul(pa[0:66, :], kcT[:, :], qr, start=True, stop=False), r=[kcT, qm[g]], w=[pa])
                kb.op("pe", lambda e: e.matmul(pa[0:66, :], ident_bf[0:66, 0:66], cmk[i % 2][:, :], start=False, stop=True), r=[ident_bf, cmk[i % 2]], w=[pa])
                pt_ = PT[0]
                kb.op("act", lambda e: e.activation(out=pt_[0:66, :], in_=pa[0:66, :], func=AF.Exp), r=[pa], w=[pt_])
                kb.op("pe", lambda e: e.matmul(accP[:, :], vc[:, :], pt_[0:66, :], start=True, stop=True), r=[vc, pt_], w=[accP])
                kb.op("pe", lambda e: e.matmul(accD[:, :], onesb[0:66, :], pt_[0:66, :], start=True, stop=True), r=[onesb, pt_], w=[accD])
                combine(g, 0, True)
                kb.op("dve", lambda e: e.tensor_scalar(out=rd[0:66, :], in0=accD[0:66, :], scalar1=1e-30, scalar2=None, op0=ALU.max), r=[accD], w=[rd])
                kb.op("dve", lambda e: e.reciprocal(rd[0:66, :], rd[0:66, :]), r=[rd], w=[rd])
                kb.op("dve", lambda e: e.tensor_tensor(out=pn[:].rearrange("n j q -> n (j q)"), in0=pt_[0:66, :], in1=rd[0:66, :], op=ALU.mult), r=[pt_, rd], w=[pn])
                kb.op("dve", lambda e: e.tensor_reduce(out=impT[:], in_=pn[:].rearrange("n j q -> n q j"), axis=AX.X, op=ALU.add), r=[pn], w=[impT])
                pi = kb.ps()
                kb.op("pe", lambda e: e.transpose(pi[:, 0:66], impT[:, :], ident_f[0:66, 0:66]), r=[impT, ident_f], w=[pi])
                kb.op("dve", lambda e: e.tensor_tensor(out=sc[:], in0=pi[:, 0:66], in1=addc[i % 2][:, 0, :], op=ALU.add), r=[pi, addc[i % 2]], w=[sc])
                kb.op("dve", lambda e: e.max(m8[:, 0:8], sc[:]), r=[sc], w=[m8])
                kb.op("dve", lambda e: e.match_replace(sc2[:], m8[:, 0:8], sc[:], -1e30), r=[sc, m8], w=[sc2])
                kb.op("dve", lambda e: e.max(m8[:, 8:16], sc2[:]), r=[sc2], w=[m8])
                kb.op("dve", lambda e: e.tensor_scalar(out=sc2[:], in0=sc[:], scalar1=m8[:, 15:16], scalar2=None, op0=ALU.is_ge), r=[sc, m8], w=[sc2])
                kb.op("dve", lambda e: e.tensor_scalar(out=sc2[:], in0=sc2[:], scalar1=-1.0, scalar2=30000.0, op0=ALU.add, op1=ALU.mult), r=[sc2], w=[sc2])
                kb.op("dve", lambda e: e.tensor_tensor(out=sc2[:], in0=sc2[:], in1=addc[i % 2][:, 1, :], op=ALU.min), r=[sc2, addc[i % 2]], w=[sc2])
                pm = kb.ps()
                kb.op("pe", lambda e: e.transpose(pm[0:66, 0:128], sc2[:, :], ident_f[:, :]), r=[sc2, ident_f], w=[pm])
                kb.op("dve", lambda e: e.tensor_copy(MBT4[:], bcast(pm[0:66, 0:128].unsqueeze(1), [66, 4, 128])), r=[pm], w=[MBT4])
                def slc_scores(tk):
                    pa = kb.ps()
                    kb.op("pe", lambda e: e.matmul(pa[:, :], kTs[:, tk * 128:(tk + 1) * 128], qr, start=True, stop=False), r=[kTs, qm[g]], w=[pa])
                    kb.op("pe", lambda e: e.matmul(pa[:, :], Esb[:, tk * 128:(tk + 1) * 128], MBT4[:].rearrange("n j q -> n (j q)"), start=False, stop=(tk != tq)),
                          r=[Esb, MBT4], w=[pa])
                    if tk == tq:
                        kb.op("pe", lambda e: e.matmul(pa[:, :], ident_bf[:, :], mk3[:, 0, :], start=False, stop=True), r=[ident_bf, mk3], w=[pa])
                    return pa
                pas = {0: slc_scores(0)}
                for tk in range(tq + 1):
                    if tk + 1 <= tq:
                        pas[tk + 1] = slc_scores(tk + 1)
                    pa = pas.pop(tk)
                    pt_ = PT[tk % 2]
                    kb.op("act", lambda e: e.activation(out=pt_[:, :], in_=pa[:, :], func=AF.Exp), r=[pa], w=[pt_])
                    kb.op("pe", lambda e: e.matmul(accP[:, :], Vs[:, tk, :], pt_[:, :], start=(tk == 0), stop=(tk == tq)), r=[Vs, pt_], w=[accP])
                    kb.op("pe", lambda e: e.matmul(accD[:, :], onesb[:, :], pt_[:, :], start=(tk == 0), stop=(tk == tq)), r=[onesb, pt_], w=[accD])
                combine(g, 1, False)
                tks = [tk for tk in range(tq - 4, tq + 1) if tk >= 0]
                def win_scores(tk):
                    extra = []
                    if tk == tq:
                        extra.append(0)
                    if tk == tq - 4:
                        extra.append(1)
                    if tk == 0:
                        extra.append(2)
                    pa = kb.ps()
                    kb.op("pe", lambda e: e.matmul(pa[:, :], kTw[:, tk * 128:(tk + 1) * 128], qr, start=True, stop=(len(extra) == 0)), r=[kTw, qm[g]], w=[pa])
                    for x_, m in enumerate(extra):
                        kb.op("pe", lambda e: e.matmul(pa[:, :], ident_bf[:, :], mk3[:, m, :], start=False, stop=(x_ == len(extra) - 1)), r=[ident_bf, mk3], w=[pa])
                    return pa
                pas = {0: win_scores(tks[0])}
                for n_, tk in enumerate(tks):
                    if n_ + 1 < len(tks):
                        pas[n_ + 1] = win_scores(tks[n_ + 1])
                    pa = pas.pop(n_)
                    pt_ = PT[n_ % 2]
                    kb.op("act", lambda e: e.activation(out=pt_[:, :], in_=pa[:, :], func=AF.Exp), r=[pa], w=[pt_])
                    kb.op("pe", lambda e: e.matmul(accP[:, :], Vw[:, tk, :], pt_[:, :], start=(n_ == 0), stop=(n_ == len(tks) - 1)), r=[Vw, pt_], w=[accP])
                    kb.op("pe", lambda e: e.matmul(accD[:, :], onesb[:, :], pt_[:, :], start=(n_ == 0), stop=(n_ == len(tks) - 1)), r=[onesb, pt_], w=[accD])
                combine(g, 2, False)
                hr = slice(g * 64, (g + 1) * 64)
                kb.op("act", lambda e: e.activation(out=mixT[hr, 4:8, i * 128:(i + 1) * 128], in_=acc[hr, :].rearrange("p (j q) -> p j q", j=4), func=AF.Copy),
                      r=[acc], w=[mix_r[i]])
        kb.psn = 8
        kb.barrier()
    kb.barrier()
    front.close()
    memKT = kb.sb([128, 8, 256], BF16, name="memKT")
    memV = kb.sb([128, 2, 1024], BF16, name="memV")

    with contextlib.ExitStack() as st:
        stages = []
        for i in range(2):
            stages.append((kb.sb([128, D], F32, stack=st), kb.sb([128, 1], F32, stack=st),
                           kb.sb([128, 1], F32, stack=st), kb.sb([128, D], BF16, stack=st)))
        mT = kb.sb([128, 8, 256], BF16, stack=st)
        mres = [Res(), Res()]
        for t in range(2):
            norm_transpose(memp[t * 128:(t + 1) * 128, :], 128, stages[t], mT[:, :, t * 128:(t + 1) * 128], mres[t], 2)
        wm = load_w(w_mem_kv[:, :], 2048, st, "wmkv")
        mo = [kb.sb([128, 2048], F32, stack=st) for _ in range(2)]
        for t in range(2):
            for cb in range(4):
                pa = kb.ps()
                for kc in range(8):
                    kb.op("pe", lambda e, kc=kc: e.matmul(pa[:, :], mT[:, kc, t * 128:(t + 1) * 128], wm[:, kc, cb * 512:(cb + 1) * 512],
                                                          start=(kc == 0), stop=(kc == 7)), r=[mres[t], wm], w=[pa])
                if cb % 2 == 0:
                    kb.op("dve", lambda e: e.tensor_copy(mo[t][:, cb * 512:(cb + 1) * 512], pa[:, :]), r=[pa], w=[mo[t]])
                else:
                    kb.op("act", lambda e: e.activation(out=mo[t][:, cb * 512:(cb + 1) * 512], in_=pa[:, :], func=AF.Copy), r=[pa], w=[mo[t]])
            kb.dma("sp", [(o_mem[t * 128:(t + 1) * 128, :], mo[t][:, :])], r=[mo[t]], final=True)
            kb.op("pool", lambda e: e.tensor_copy(memV[:, t, :], mo[t][:, 1024:2048]), r=[mo[t]], w=[memV])
        for c in range(8):
            pa = kb.ps()
            for kc in range(8):
                kb.op("pe", lambda e, kc=kc: e.matmul(pa[:, 0:256], wm[:, kc, c * 128:(c + 1) * 128], mT[:, kc, :],
                                                      start=(kc == 0), stop=(kc == 7)), r=[mres[0], mres[1], wm], w=[pa])
            kb.op("act", lambda e: e.activation(out=memKT[:, c, :], in_=pa[:, 0:256], func=AF.Copy), r=[pa], w=[memKT])
        kb.barrier()


    scr_qkv = nc.dram_tensor("scr_qkv", [NS, 8, 3, 64], F32).ap()
    scr_z = nc.dram_tensor("scr_z", [NS, 8, 64], F32).ap()
    scr_ba = nc.dram_tensor("scr_ba", [NS, 8, 2], F32).ap()
    scr_dno = nc.dram_tensor("scr_dno", [NS, 8, 64], F32).ap()
    r_scr = Res("scr")
    with contextlib.ExitStack() as st:
        cst = kb.sb([NS, 3, 1536], F32, stack=st)
        wc = kb.sb([NS, 4, 1536], F32, stack=st)
        cv = kb.sb([NS, 1536], F32, stack=st)
        tmpc = kb.sb([NS, 1536], F32, stack=st)
        kb.dma("sp", [(cst[:], sconv[:, :, :])], w=[cst])
        spq = kb.sb([NS, 1536], F32, stack=st)
        kb.dma("sp", [(spq[:], scr_sproj[:, 0:1536])], r=[r_sps], w=[spq])
        kb.dma("sp", [(wc[:], bass.AP(conv_w.tensor, 0, [[0, NS], [1, 4 * 1536]]).rearrange("p (j c) -> p j c", j=4))], w=[wc])
        kb.op("dve", lambda e: e.tensor_tensor(out=cv[:], in0=spq[:, :], in1=wc[:, 3, :], op=ALU.mult), r=[spq, wc], w=[cv])
        for j in range(3):
            kb.op("dve", lambda e: e.tensor_tensor(out=tmpc[:], in0=cst[:, j, :], in1=wc[:, j, :], op=ALU.mult), r=[cst, wc], w=[tmpc])
            kb.op("dve", lambda e: e.tensor_tensor(out=cv[:], in0=cv[:], in1=tmpc[:], op=ALU.add), r=[cv, tmpc], w=[cv])
        kb.op("act", lambda e: e.activation(out=cv[:], in_=cv[:], func=AF.Silu), r=[cv], w=[cv])
        kb.dma("sp", [(scr_qkv[:, :, a, :], cv[:, a * 512:(a + 1) * 512].rearrange("s (h d) -> s h d", h=8)) for a in range(3)] + [
                      (scr_z.rearrange("s h d -> s (h d)"), scr_sproj[:, OFF_Z:OFF_Z + 512]),
                      (scr_ba[:, :, 0], scr_sproj[:, OFF_B:OFF_B + 8]), (scr_ba[:, :, 1], scr_sproj[:, OFF_A:OFF_A + 8])],
               r=[cv, r_sps], w=[r_scr], allow_slow_non_contiguous=True)
        q3 = kb.sb([128, 3, 64], F32, stack=st)
        z1 = kb.sb([128, 64], F32, stack=st)
        ba = kb.sb([128, 2], F32, stack=st)
        abr = kb.sb([128, 2], F32, stack=st)
        gn = kb.sb([128, 64], F32, stack=st)
        S = kb.sb([128, 64, 64], F32, stack=st)
        big = kb.sb([128, 64, 64], F32, stack=st)
        sm = kb.sb([128, 16], F32, stack=st)
        v64 = [kb.sb([128, 64], F32, stack=st) for _ in range(6)]
        kb.dma("sp", [(q3[:], scr_qkv.rearrange("s h a d -> (s h) a d")),
                      (z1[:], scr_z.rearrange("s h d -> (s h) d")),
                      (ba[:], scr_ba.rearrange("s h a -> (s h) a"))], r=[r_scr], w=[q3, z1, ba])
        kb.dma("sp", [(abr[:], abrep[:, :]), (gn[:], dnnorm[:, :])], w=[abr, gn])
        kb.dma("sp", [(S[:, 0:32, :], srec[:, 0:32, :]), (S[:, 32:64, :], srec[:, 32:64, :])], w=[S])
        BETA, X, EG, SSQ, SSK, RQ, RK, SSO, RO, EA = range(10)
        c1 = lambda i: sm[:, i:i + 1]
        kb.op("act", lambda e: e.activation(out=c1(BETA), in_=ba[:, 0:1], func=AF.Sigmoid), r=[ba], w=[sm])
        kb.op("act", lambda e: e.activation(out=c1(X), in_=ba[:, 1:2], func=AF.Exp, bias=abr[:, 1:2]), r=[ba, abr], w=[sm])
        kb.op("act", lambda e: e.activation(out=c1(X), in_=c1(X), func=AF.Ln, bias=onec[:, 0:1]), r=[sm, onec], w=[sm])
        kb.op("act", lambda e: e.activation(out=c1(EA), in_=abr[:, 0:1], func=AF.Exp), r=[abr], w=[sm])
        kb.op("dve", lambda e: e.tensor_scalar(out=c1(X), in0=c1(X), scalar1=c1(EA), scalar2=-1.0, op0=ALU.mult, op1=ALU.mult), r=[sm], w=[sm])
        kb.op("act", lambda e: e.activation(out=c1(EG), in_=c1(X), func=AF.Exp), r=[sm], w=[sm])
        kb.op("act", lambda e: e.activation(out=v64[0][:], in_=q3[:, 0, :], func=AF.Square, accum_out=c1(SSQ)), r=[q3], w=[v64[0], sm])
        kb.op("act", lambda e: e.activation(out=v64[0][:], in_=q3[:, 1, :], func=AF.Square, accum_out=c1(SSK)), r=[q3], w=[v64[0], sm])
        kb.op("act", lambda e: e.activation(out=sm[:, RQ:RQ + 2], in_=sm[:, SSQ:SSQ + 2], func=AF.Sqrt, bias=epsc[:, 0:1]), r=[sm, epsc], w=[sm])
        kb.op("dve", lambda e: e.reciprocal(sm[:, RQ:RQ + 2], sm[:, RQ:RQ + 2]), r=[sm], w=[sm])
        qn, kn, t1, vn, o1 = v64[1], v64[2], v64[3], v64[4], v64[5]
        kb.op("dve", lambda e: e.tensor_scalar(out=qn[:], in0=q3[:, 0, :], scalar1=c1(RQ), scalar2=0.125, op0=ALU.mult, op1=ALU.mult), r=[q3, sm], w=[qn])
        kb.op("dve", lambda e: e.tensor_scalar(out=kn[:], in0=q3[:, 1, :], scalar1=c1(RK), scalar2=None, op0=ALU.mult), r=[q3, sm], w=[kn])
        kb.op("dve", lambda e: e.tensor_tensor(out=big[:], in0=S[:], in1=bcast(kn[:, :].unsqueeze(2), [128, 64, 64]), op=ALU.mult), r=[S, kn], w=[big])
        kb.op("dve", lambda e: e.tensor_reduce(out=t1[:], in_=big[:].rearrange("p k v -> p v k"), axis=AX.X, op=ALU.add), r=[big], w=[t1])
        kb.op("dve", lambda e: e.tensor_scalar(out=t1[:], in0=t1[:], scalar1=c1(EG), scalar2=None, op0=ALU.mult), r=[t1, sm], w=[t1])
        kb.op("dve", lambda e: e.tensor_tensor(out=vn[:], in0=q3[:, 2, :], in1=t1[:], op=ALU.subtract), r=[q3, t1], w=[vn])
        kb.op("dve", lambda e: e.tensor_scalar(out=vn[:], in0=vn[:], scalar1=c1(BETA), scalar2=None, op0=ALU.mult), r=[vn, sm], w=[vn])
        kb.op("dve", lambda e: e.tensor_tensor(out=big[:], in0=bcast(kn[:, :].unsqueeze(2), [128, 64, 64]),
                                               in1=bcast(vn[:, :].unsqueeze(1), [128, 64, 64]), op=ALU.mult), r=[kn, vn], w=[big])
        kb.op("dve", lambda e: e.scalar_tensor_tensor(out=S[:], in0=S[:], scalar=c1(EG), in1=big[:], op0=ALU.mult, op1=ALU.add), r=[S, big, sm], w=[S])
        kb.dma("sp", [(o_srec[:, 0:32, :], S[:, 0:32, :]), (o_srec[:, 32:64, :], S[:, 32:64, :])], r=[S], final=True)
        kb.op("dve", lambda e: e.tensor_tensor(out=big[:], in0=S[:], in1=bcast(qn[:, :].unsqueeze(2), [128, 64, 64]), op=ALU.mult), r=[S, qn], w=[big])
        kb.op("dve", lambda e: e.tensor_reduce(out=o1[:], in_=big[:].rearrange("p k v -> p v k"), axis=AX.X, op=ALU.add), r=[big], w=[o1])
        kb.op("act", lambda e: e.activation(out=t1[:], in_=o1[:], func=AF.Square, accum_out=c1(SSO)), r=[o1], w=[t1, sm])
        kb.op("act", lambda e: e.activation(out=c1(RO), in_=c1(SSO), func=AF.Sqrt, scale=1.0 / 64, bias=epsc[:, 0:1]), r=[sm, epsc], w=[sm])
        kb.op("dve", lambda e: e.reciprocal(c1(RO), c1(RO)), r=[sm], w=[sm])
        kb.op("dve", lambda e: e.scalar_tensor_tensor(out=o1[:], in0=o1[:], scalar=c1(RO), in1=gn[:], op0=ALU.mult, op1=ALU.mult), r=[o1, sm, gn], w=[o1])
        kb.op("act", lambda e: e.activation(out=z1[:], in_=z1[:], func=AF.Silu), r=[z1], w=[z1])
        kb.op("dve", lambda e: e.tensor_tensor(out=o1[:], in0=o1[:], in1=z1[:], op=ALU.mult), r=[o1, z1], w=[o1])
        kb.dma("sp", [(scr_dno.rearrange("s h d -> (s h) d"), o1[:, :])], r=[o1], w=[r_scr])
        dtok = kb.sb([NS, 512], F32, stack=st)
        dtb = kb.sb([NS, 512], BF16, stack=st)
        kb.dma("sp", [(dtok[:], scr_dno.rearrange("s h d -> s (h d)"))], r=[r_scr], w=[dtok])
        kb.op("dve", lambda e: e.tensor_copy(dtb[:], dtok[:]), r=[dtok], w=[dtb])
        pt = kb.ps()
        ptv = pt[:].bitcast(BF16)
        for f in range(4):
            kb.op("pe", lambda e, f=f: e.transpose(ptv[:, f * NS:(f + 1) * NS], dtb[:, f * 128:(f + 1) * 128], ident_bf[0:NS, 0:NS]), r=[dtb, ident_bf], w=[pt])
        kb.op("dve", lambda e: e.tensor_copy(mixT[:, 0:4, NOWN * 128:TOK], ptv[:, 0:4 * NS].rearrange("p (f t) -> p f t", f=4)), r=[pt], w=[mix_r[16]])
        kb.barrier()


    if "s" in DBG:
      with contextlib.ExitStack() as st:
        kb.barrier()
        F = lambda shape, name, dt=F32: kb.sb(shape, dt, stack=st, name=name)
        w1sb = F([128, 2, 64, 128], "w1sb_s", BF16)
        kb.dma("pool", [(w1sb[:, k, :, :], w1rep[:, k, :, :]) for k in range(2)], w=[w1sb])
        pe_sb, eom, b1t, w2sb = F([128, 256], "pe_s"), F([128, 2], "eom_s"), F([128, 2], "b1t_s"), F([128, 2, 64], "w2sb_s")
        kb.dma("sp", [(pe_sb[:], perep[:, :]), (eom[:], eomin[:, :]), (b1t[:], b1in[:, :]), (w2sb[:], w2in[:, :, :])], w=[pe_sb, eom, b1t, w2sb])
        w2pad = F([128, 2, 2, 128], "w2pad_s", BF16)
        kb.op("pool", lambda e: e.memset(w2pad[:], 0.0), w=[w2pad])
        for k in range(2):
            for g in range(2):
                kb.op("dve", lambda e: e.tensor_copy(w2pad[:, k, g, g * 64:(g + 1) * 64], w2sb[:, k, :]), r=[w2sb], w=[w2pad])
        sc1 = F([1, 512], "sc1")
        kb.dma("sp", [(sc1[:], sconst[:, :])], w=[sc1])
        Ehalf = sc1[0:1, 0:256].rearrange("o (h k) -> o h k", h=2)
        addc1 = sc1[0:1, 256:289]
        onesf = F([128, 128], "onesf_s")
        kb.op("pool", lambda e: e.memset(onesf[:], 1.0), w=[onesf])
        onesb = F([128, 128], "onesb_s", BF16)
        kb.op("pool", lambda e: e.memset(onesb[:], 1.0), w=[onesb])
        selb = F([NS, NS, 128], "selb_s")
        kb.op("dve", lambda e: e.tensor_copy(selb[:], bcast(ident_f[0:NS, 0:NS].unsqueeze(2), [NS, NS, 128])), r=[ident_f], w=[selb])
        sq, skv, sg = F([NS, 512], "sq"), F([NS, 768], "skv"), F([NS, 24], "sg")
        kb.dma("sp", [(sq[:], scr_sproj[:, OFF_Q:OFF_Q + 512]), (skv[:], scr_sproj[:, OFF_KV:OFF_KV + 768]), (sg[:], scr_sproj[:, OFF_G:OFF_G + 24])],
               r=[r_sps], w=[sq, skv, sg])
        kb.op("act", lambda e: e.activation(out=sg[:], in_=sg[:], func=AF.Sigmoid), r=[sg], w=[sg])
        sqp = F([NS, 4, 2, 64], "sqp")
        kb.op("act", lambda e: e.activation(out=sqp[:], in_=sq[:].rearrange("s (g j d) -> s j g d", g=2, j=4), func=AF.Copy, scale=0.125), r=[sq], w=[sqp])
        Q8b, Q8f = F([128, NS, 2, 4], "Q8b", BF16), F([128, NS, 2, 4], "Q8f")
        pq = kb.ps()
        for j in range(4):
            kb.op("pe", lambda e, j=j: e.transpose(pq[:, j * NS:(j + 1) * NS], sqp[:, j, :, :].rearrange("s g d -> s (g d)"), ident_f[0:NS, 0:NS]), r=[sqp, ident_f], w=[pq])
        for g in range(2):
            kb.op("dve", lambda e: e.tensor_scalar(out=Q8f[:, :, g, :], in0=pq[:, 0:4 * NS].rearrange("p (j s) -> p s j", j=4), scalar1=eom[:, g:g + 1], scalar2=None, op0=ALU.mult),
                  r=[pq, eom], w=[Q8f])
        kb.op("dve", lambda e: e.tensor_copy(Q8b[:], Q8f[:]), r=[Q8f], w=[Q8b])
        kTn = F([128, 2, NS], "kTn")
        pk = kb.ps()
        for a, c0 in enumerate((256, 512)):
            kb.op("pe", lambda e, a=a, c0=c0: e.transpose(pk[:, a * NS:(a + 1) * NS], skv[:, c0:c0 + 128], ident_f[0:NS, 0:NS]), r=[skv, ident_f], w=[pk])
        kb.op("act", lambda e: e.activation(out=kTn[:].rearrange("p a s -> p (a s)"), in_=pk[:, 0:2 * NS], func=AF.Copy), r=[pk], w=[kTn])
        pti = F([128, NS * 16], "pti", I32)
        kb.dma("sp", [(pti[:], bass.AP(ptab.tensor, 0, [[0, 128], [1, NS * 16]]))], w=[pti])
        ptf = F([128, NS * 16], "ptf")
        iop = F([128, 1], "iop")
        kb.dma("sp", [(iop[:], iotap[:, :])], w=[iop])
        kb.op("dve", lambda e: e.tensor_copy(ptf[:], pti[:]), r=[pti], w=[ptf])
        kb.op("dve", lambda e: e.tensor_scalar(out=ptf[:], in0=ptf[:], scalar1=128.0, scalar2=iop[:, 0:1], op0=ALU.mult, op1=ALU.add), r=[ptf, iop], w=[ptf])
        kb.op("dve", lambda e: e.tensor_copy(pti[:], ptf[:]), r=[ptf], w=[pti])
        PGs = [F([128, 16, 512], "PG0"), F([128, 16, 512], "PG1")]
        WT = F([128, 4, 256], "WT")
        big = F([128, 16, 4, 64], "big_s")
        PGm = (F([128, 16, 256], "PGe", BF16), F([128, 16, 256], "PGo", BF16))
        qb = F([128, 2, 4, 64], "qb")
        ssl, Pw = F([128, 16, 2, 4], "ssl"), F([128, 4, 2, 4], "Pw")
        Psum = F([128, 8], "Psum")
        HTs = [[F([128, 32], f"HTs{k}{g}", BF16) for g in range(2)] for k in range(2)]
        kcs, vcs = F([128, 32], "kcs", BF16), F([32, 128], "vcs", BF16)
        Pc, Pcb, pnc = F([32, 8], "Pc"), F([32, 8], "Pcb", BF16), F([32, 8], "pnc")
        impT = F([32, 2], "impT_s")
        row = F([1, 2, 34], "row")
        row2 = F([1, 34], "row2")
        m8 = F([1, 16], "m8_s")
        MBrow = F([1, 2, 16, 2], "MBrow")
        Pn = F([NS, 8], "Pn")
        grep = F([128, 24], "grep_s")
        rdn, tbn, accn = F([128, 8], "rdn"), F([128, 8], "tbn"), F([128, 8], "accn")

        def comb(pP, pD, br, first, guard=False):
            kb.op("dve", lambda e: e.tensor_scalar(out=rdn[:], in0=pD[:, 0:8], scalar1=1e-30, scalar2=None, op0=ALU.max), r=[pD], w=[rdn])
            kb.op("dve", lambda e: e.reciprocal(rdn[:], rdn[:]), r=[rdn], w=[rdn])
            kb.op("dve", lambda e: e.tensor_tensor(out=tbn[:], in0=pP[:, 0:8], in1=rdn[:], op=ALU.mult), r=[pP, rdn], w=[tbn])
            gv = grep[:, :].rearrange("p (h b) -> p h b", b=3)[:, :, br]
            if first:
                kb.op("dve", lambda e: e.tensor_tensor(out=accn[:], in0=tbn[:], in1=gv, op=ALU.mult), r=[tbn, grep], w=[accn])
            else:
                kb.op("dve", lambda e: e.tensor_tensor(out=tbn[:], in0=tbn[:], in1=gv, op=ALU.mult), r=[tbn, grep], w=[tbn])
                kb.op("dve", lambda e: e.tensor_tensor(out=accn[:], in0=accn[:], in1=tbn[:], op=ALU.add), r=[accn, tbn], w=[accn])

        for si in range(NS):
            if si == 0:
                kb_gather(kb, PGs[0], pti, cnsa, 0)
            if si + 1 < NS:
                kb_gather(kb, PGs[(si + 1) % 2], pti, cnsa, si + 1)
            PG = PGs[si % 2]
            kb.dma("sp", [(WT[:, t4, :], cwin[si, t4 * 128:(t4 + 1) * 128, :]) for t4 in range(4)], w=[WT])
            pqb = kb.ps()
            kb.op("pe", lambda e: e.matmul(pqb[:, :], selb[:, si, :], sqp[:].rearrange("s j g d -> s (j g d)"), start=True, stop=True), r=[selb, sqp], w=[pqb])
            kb.op("act", lambda e: e.activation(out=qb[:], in_=pqb[:, :].rearrange("p (j g d) -> p g j d", j=4, g=2), func=AF.Copy), r=[pqb], w=[qb])
            pgr = kb.ps()
            kb.op("pe", lambda e: e.matmul(pgr[:, 0:24], selb[:, si, :], sg[:, :], start=True, stop=True), r=[selb, sg], w=[pgr])
            kb.op("act", lambda e: e.activation(out=grep[:], in_=pgr[:, 0:24], func=AF.Copy), r=[pgr], w=[grep])
            kb.op("dve", lambda e: e.tensor_tensor(out=big[:].rearrange("p a b c -> p a (b c)"), in0=PG[:, :, 0:256], in1=bcast(pe_sb[:, :].unsqueeze(1), [128, 16, 256]), op=ALU.add),
                  r=[PG, pe_sb], w=[big])
            for p_ in range(2):
                kb.op("dve", lambda e: e.tensor_scalar(out=PGm[p_][:], in0=big[:].rearrange("p a b c -> p a (b c)"), scalar1=eom[:, p_:p_ + 1], scalar2=None, op0=ALU.mult),
                      r=[big, eom], w=[PGm[p_]])
            for k in range(2):
                for g in range(2):
                    pa = kb.ps()
                    for half in range(2):
                        for d in range(64):
                            kb.op("pe", lambda e: e.matmul(pa[:, half * 16:(half + 1) * 16], w1sb[:, k, d, :], PGm[half][:, :, k * 128 + g * 64 + d],
                                                           start=(d == 0), stop=(d == 63)), r=[w1sb, PGm[half]], w=[pa])
                    kb.op("act", lambda e: e.activation(out=HTs[k][g][:], in_=pa[:, 0:32], func=AF.Relu, bias=b1t[:, k:k + 1]), r=[pa, b1t], w=[HTs[k][g]])
            pk = kb.ps()
            for g in range(2):
                kb.op("pe", lambda e: e.matmul(pk[:, 0:32], w2pad[:, 0, g, :], HTs[0][g][:], start=(g == 0), stop=(g == 1)), r=[w2pad, HTs[0][g]], w=[pk])
            kb.op("act", lambda e: e.activation(out=kcs[:], in_=pk[:, 0:32], func=AF.Copy), r=[pk], w=[kcs])
            pv = kb.ps()
            for g in range(2):
                kb.op("pe", lambda e: e.matmul(pv[0:32, 0:128], HTs[1][g][:], w2pad[:, 1, g, :], start=(g == 0), stop=(g == 1)), r=[w2pad, HTs[1][g]], w=[pv])
            kb.op("act", lambda e: e.activation(out=vcs[:], in_=pv[0:32, 0:128], func=AF.Copy), r=[pv], w=[vcs])
            q8b = Q8b[:, si, :, :].rearrange("p g j -> p (g j)")
            q8f = Q8f[:, si, :, :].rearrange("p g j -> p (g j)")
            pa = kb.ps()
            kb.op("pe", lambda e: e.matmul(pa[0:32, 0:8], kcs[:, :], q8b, start=True, stop=True), r=[kcs, Q8b], w=[pa])
            kb.op("act", lambda e: e.activation(out=Pc[:], in_=pa[0:32, 0:8], func=AF.Exp), r=[pa], w=[Pc])
            kb.op("dve", lambda e: e.tensor_copy(Pcb[:], Pc[:]), r=[Pc], w=[Pcb])
            pP, pD = kb.ps(), kb.ps()
            kb.op("pe", lambda e: e.matmul(pP[:, 0:8], vcs[:, :], Pcb[:, :], start=True, stop=True), r=[vcs, Pcb], w=[pP])
            kb.op("pe", lambda e: e.matmul(pD[:, 0:8], onesb[0:32, :], Pcb[:, :], start=True, stop=True), r=[onesb, Pcb], w=[pD])
            comb(pP, pD, 0, True)
            kb.op("dve", lambda e: e.tensor_tensor(out=pnc[:], in0=Pc[:], in1=rdn[0:32, :], op=ALU.mult), r=[Pc, rdn], w=[pnc])
            kb.op("dve", lambda e: e.tensor_reduce(out=impT[:], in_=pnc[:].rearrange("n (g j) -> n g j", g=2), axis=AX.X, op=ALU.add), r=[pnc], w=[impT])
            pi = kb.ps()
            for g in range(2):
                kb.op("pe", lambda e, g=g: e.transpose(pi[0:1, g * 32:(g + 1) * 32], impT[:, g:g + 1], ident_f[0:32, 0:32]), r=[impT, ident_f], w=[pi])
            kb.op("pool", lambda e: e.memset(row[:], 0.0), w=[row])
            for g in range(2):
                kb.op("dve", lambda e: e.tensor_copy(row[0:1, g, 0:32].rearrange("o (p h) -> o p h", h=2), pi[0:1, g * 32:(g + 1) * 32].rearrange("o (h p) -> o p h", h=2)),
                      r=[pi], w=[row])
                kb.op("dve", lambda e: e.tensor_tensor(out=row[0:1, g, 0:33], in0=row[0:1, g, 0:33], in1=addc1, op=ALU.add), r=[row, sc1], w=[row])
                kb.op("dve", lambda e: e.max(m8[0:1, 0:8], row[0:1, g, 0:33]), r=[row], w=[m8])
                kb.op("dve", lambda e: e.match_replace(row2[0:1, 0:33], m8[0:1, 0:8], row[0:1, g, 0:33], -1e30), r=[row, m8], w=[row2])
                kb.op("dve", lambda e: e.max(m8[0:1, 8:16], row2[0:1, 0:33]), r=[row2], w=[m8])
                kb.op("dve", lambda e: e.tensor_scalar(out=row[0:1, g, 0:33], in0=row[0:1, g, 0:33], scalar1=m8[0:1, 15:16], scalar2=None, op0=ALU.is_ge), r=[row, m8], w=[row])
                kb.op("dve", lambda e: e.tensor_scalar(out=row[0:1, g, 0:33], in0=row[0:1, g, 0:33], scalar1=-1.0, scalar2=30000.0, op0=ALU.add, op1=ALU.mult), r=[row], w=[row])
                kb.op("dve", lambda e: e.tensor_copy(MBrow[0:1, :, :, g], row[0:1, g, 0:32].rearrange("o (p h) -> o h p", h=2)), r=[row], w=[MBrow])
            pmk = kb.ps()
            for hf in range(2):
                kb.op("pe", lambda e, hf=hf: e.matmul(pmk[:, 0:32], Ehalf[0:1, hf, :], MBrow[0:1, hf, :, :].rearrange("o p g -> o (p g)"), start=(hf == 0), stop=(hf == 1)),
                      r=[sc1, MBrow], w=[pmk])
            for g in range(2):
                kb.op("dve", lambda e: e.tensor_tensor(out=big[:], in0=bcast(PG[:, :, 256 + 64 * g:320 + 64 * g].unsqueeze(2), [128, 16, 4, 64]),
                                                       in1=bcast(qb[:, g, :, :].unsqueeze(1), [128, 16, 4, 64]), op=ALU.mult), r=[PG, qb], w=[big])
                kb.op("dve", lambda e: e.tensor_reduce(out=ssl[:, :, g, :], in_=big[:], axis=AX.X, op=ALU.add), r=[big], w=[ssl])
            kb.op("dve", lambda e: e.tensor_tensor(out=ssl[:], in0=ssl[:], in1=bcast(pmk[:, 0:32].rearrange("p (a g) -> p a g", g=2).unsqueeze(3), [128, 16, 2, 4]), op=ALU.add),
                  r=[ssl, pmk], w=[ssl])
            kb.op("act", lambda e: e.activation(out=ssl[:], in_=ssl[:], func=AF.Exp), r=[ssl], w=[ssl])
            for a, (Pt, ntile, vsrc, vofs, vnew0) in enumerate(((ssl, 16, PG, 384, 384), (Pw, 4, WT, 128, 640))):
                if a == 1:
                    for g in range(2):
                        kb.op("dve", lambda e: e.tensor_tensor(out=big[:, 0:4, :, :], in0=bcast(WT[:, :, 64 * g:64 + 64 * g].unsqueeze(2), [128, 4, 4, 64]),
                                                               in1=bcast(qb[:, g, :, :].unsqueeze(1), [128, 4, 4, 64]), op=ALU.mult), r=[WT, qb], w=[big])
                        kb.op("dve", lambda e: e.tensor_reduce(out=Pw[:, :, g, :], in_=big[:, 0:4, :, :], axis=AX.X, op=ALU.add), r=[big], w=[Pw])
                    kb.op("dve", lambda e: e.tensor_scalar(out=Pw[0:1, 0, :, :], in0=Pw[0:1, 0, :, :], scalar1=-30000.0, scalar2=None, op0=ALU.add), r=[Pw], w=[Pw])
                    kb.op("act", lambda e: e.activation(out=Pw[:], in_=Pw[:], func=AF.Exp), r=[Pw], w=[Pw])
                pnw = kb.ps()
                kb.op("pe", lambda e: e.matmul(pnw[0:NS, 0:8], kTn[:, a, :], q8f, start=True, stop=True), r=[kTn, Q8f], w=[pnw])
                kb.op("act", lambda e: e.activation(out=Pn[:], in_=pnw[0:NS, 0:8], func=AF.Exp), r=[pnw], w=[Pn])
                kb.op("dve", lambda e: e.tensor_scalar(out=Pn[:], in0=Pn[:], scalar1=ident_f[0:NS, si:si + 1], scalar2=None, op0=ALU.mult), r=[Pn, ident_f], w=[Pn])
                kb.op("dve", lambda e: e.tensor_reduce(out=Psum[:], in_=Pt[:].rearrange("p a g j -> p (g j) a"), axis=AX.X, op=ALU.add), r=[Pt], w=[Psum])
                pP, pD = kb.ps(), kb.ps()
                for t_ in range(ntile):
                    kb.op("pe", lambda e, t_=t_: e.matmul(pP[:, 0:8], vsrc[:, t_, vofs:vofs + 128], Pt[:, t_, :, :].rearrange("p g j -> p (g j)"), start=(t_ == 0), stop=False),
                          r=[vsrc, Pt], w=[pP])
                kb.op("pe", lambda e: e.matmul(pP[:, 0:8], skv[:, vnew0:vnew0 + 128], Pn[:, :], start=False, stop=True), r=[skv, Pn], w=[pP])
                kb.op("pe", lambda e: e.matmul(pD[:, 0:8], onesf[:, :], Psum[:, :], start=True, stop=False), r=[onesf, Psum], w=[pD])
                kb.op("pe", lambda e: e.matmul(pD[:, 0:8], onesf[0:NS, :], Pn[:, :], start=False, stop=True), r=[onesf, Pn], w=[pD])
                comb(pP, pD, 1 + a, False)
            for g in range(2):
                hr = slice(g * 64, (g + 1) * 64)
                kb.op("act", lambda e: e.activation(out=mixT[hr, 4:8, NOWN * 128 + si], in_=accn[hr, 4 * g:4 * g + 4], func=AF.Copy), r=[accn], w=[mix_r[16]])
        kb.barrier()

    with contextlib.ExitStack() as st:
        if "m" in DBG:
            kb.dma("pool", [(mixT[:, k, :], dbg_mixT[:, k, :]) for k in range(8)], w=mix_r)
        if "n" in DBG:
            kb.dma("pool", [(mixT[:, k, :], dbg_mixT[:, k, :]) for k in range(4, 8)], w=mix_r)
        xres = kb.sb([128, 17, D], F32, stack=st, name="xres")
        xr = [Res(f"xr{i}") for i in range(17)]
        for i, (t0, n) in enumerate(tiles):
            src = xp[(2 * i + 1) * 128:(2 * i + 2) * 128, :] if i < NOWN else xs[:, :]
            kb.dma("sp", [(xres[0:n, i, :], src)], w=[xr[i]])
        hT = kb.sb([128, 8, TOK], BF16, stack=st, name="hT")
        h_r = [Res(f"h{i}") for i in range(17)]
        stg = [(None, kb.sb([128, 1], F32, stack=st), kb.sb([128, 1], F32, stack=st), kb.sb([128, D], BF16, stack=st))] * 2
        wA = [kb.sb([128, 8, 1024], BF16, stack=st, name=f"wA{i}") for i in range(2)]

        def loadA(slot, dram):
            src = dram.rearrange("(k p) n -> p k n", p=128)
            kb.dma("pool", [(wA[slot][:, k, :], src[:, k, :]) for k in range(8)], w=[wA[slot]])
            return wA[slot]

        def linear_add(inT, in_res, w):
            for i, (t0, n) in enumerate(tiles):
                for half in range(2):
                    pa = kb.ps()
                    for kc in range(8):
                        kb.op("pe", lambda e, kc=kc: e.matmul(pa[0:n, :], inT[:, kc, t0:t0 + n], w[:, kc, half * 512:(half + 1) * 512],
                                                              start=(kc == 0), stop=(kc == 7)), r=[in_res[i], w], w=[pa])
                    kb.op("dve", lambda e: e.tensor_tensor(out=xres[0:n, i, half * 512:(half + 1) * 512], in0=pa[0:n, :],
                                                           in1=xres[0:n, i, half * 512:(half + 1) * 512], op=ALU.add), r=[pa, xr[i]], w=[xr[i]])

        def norm_all(ln_idx):
            for i, (t0, n) in enumerate(tiles):
                norm_transpose(xres[0:n, i, :], n, stg[i % 2], hT[:, :, t0:t0 + n], h_r[i], ln_idx, sb_src=xr[i])

        w = loadA(0, w_out)
        wq = loadA(1, w_mem_q)
        linear_add(mixT, mix_r, w)
        if "1" in DBG:
            for i, (t0, n) in enumerate(tiles):
                if i < NOWN:
                    kb.dma("sp", [(o_y[i, :, :], xres[:, i, :])], r=[xr[i]], final=True)
                else:
                    kb.dma("sp", [(o_ys[:, :], xres[0:NS, i, :])], r=[xr[i]], final=True)
        norm_all(1)
        wo = loadA(0, w_mem_o)
        aT = mixT
        a_r = mix_r
        with contextlib.ExitStack() as st2:
            onesb = kb.sb([128, 128], BF16, stack=st2, name="onesb")
            kb.op("pool", lambda e: e.memset(onesb[:], 1.0), w=[onesb])
            onesf = kb.sb([128, 128], F32, stack=st2, name="onesf")
            kb.op("pool", lambda e: e.memset(onesf[:], 1.0), w=[onesf])
            qT = kb.sb([128, 4, 512], BF16, stack=st2, name="qT")
            pT = [kb.sb([128, 512], BF16, stack=st2) for _ in range(2)]
            rden = kb.sb([128, 512], F32, stack=st2, name="rden")
            for blk in range(4):
                b0 = blk * 512
                rr = [h_r[4 * blk + j] for j in range(4)]
                for hd in range(4):
                    if hd % 2 == 0:
                        for c in range(4):
                            pa = kb.ps()
                            cc = 2 * hd + c
                            for kc in range(8):
                                kb.op("pe", lambda e, kc=kc: e.matmul(pa[:, :], wq[:, kc, cc * 128:(cc + 1) * 128], hT[:, kc, b0:b0 + 512],
                                                                      start=(kc == 0), stop=(kc == 7)), r=rr + [wq], w=[pa])
                            kb.op("act", lambda e: e.activation(out=qT[:, c, :], in_=pa[:, :], func=AF.Copy, scale=1.0 / 16), r=[pa], w=[qT])
                    for mt in range(2):
                        pa = kb.ps()
                        for j in range(2):
                            kb.op("pe", lambda e, j=j: e.matmul(pa[:, :], memKT[:, 2 * hd + j, mt * 128:(mt + 1) * 128], qT[:, 2 * (hd % 2) + j, :],
                                                                start=(j == 0), stop=(j == 1)), r=[memKT, qT], w=[pa])
                        kb.op("act", lambda e: e.activation(out=pT[mt][:, :], in_=pa[:, :], func=AF.Exp), r=[pa], w=[pT[mt]])
                    pd = kb.ps()
                    for mt in range(2):
                        kb.op("pe", lambda e, mt=mt: e.matmul(pd[:, :], onesb[:, :], pT[mt][:, :], start=(mt == 0), stop=(mt == 1)), r=[onesb, pT[mt]], w=[pd])
                    kb.op("dve", lambda e: e.reciprocal(rden[:, :], pd[:, :]), r=[pd], w=[rden])
                    for j in range(2):
                        po = kb.ps()
                        for mt in range(2):
                            kb.op("pe", lambda e, mt=mt: e.matmul(po[:, :], memV[:, mt, hd * 256 + j * 128: hd * 256 + (j + 1) * 128], pT[mt][:, :],
                                                                  start=(mt == 0), stop=(mt == 1)), r=[memV, pT[mt]], w=[po])
                        kb.op("dve", lambda e: e.tensor_tensor(out=aT[:, 2 * hd + j, b0:b0 + 512], in0=po[:, :], in1=rden[:, :], op=ALU.mult),
                              r=[po, rden], w=rr_a(blk))
            qs = kb.sb([NS, 1024], BF16, stack=st2, name="qs")
            for half in range(2):
                pa = kb.ps()
                for kc in range(8):
                    kb.op("pe", lambda e, kc=kc: e.matmul(pa[0:NS, :], hT[:, kc, NOWN * 128:TOK], wq[:, kc, half * 512:(half + 1) * 512],
                                                          start=(kc == 0), stop=(kc == 7)), r=[h_r[16], wq], w=[pa])
                kb.op("act", lambda e: e.activation(out=qs[:, half * 512:(half + 1) * 512], in_=pa[0:NS, :], func=AF.Copy, scale=1.0 / 16), r=[pa], w=[qs])
            selb = kb.sb([NS, NS, 128], BF16, stack=st2, name="selb")
            kb.op("dve", lambda e: e.tensor_copy(selb[:], bcast(ident_f[0:NS, 0:NS].unsqueeze(2), [NS, NS, 128])), r=[ident_f], w=[selb])
            ckv = [kb.sb([128, 2, 2, 1024], F32, stack=st2, name="ckv0")] * 2
            sc = kb.sb([128, 8], F32, stack=st2, name="sc")
            rd4 = kb.sb([128, 4], F32, stack=st2, name="rd4")
            for si in range(NS):
                kv = ckv[si % 2]
                kb.dma("sp", [(kv[:, mt, :, :], cmem[si, mt * 128:(mt + 1) * 128, :, :]) for mt in range(2)], w=[kv])
                pq = [kb.ps(), kb.ps()]
                for half in range(2):
                    kb.op("pe", lambda e: e.matmul(pq[half][:, :], selb[:, si, :], qs[:, half * 512:(half + 1) * 512], start=True, stop=True),
                          r=[selb, qs], w=[pq[half]])
                for half in range(2):
                    kb.op("dve", lambda e: e.tensor_tensor(out=kv[:, :, 0, half * 512:(half + 1) * 512], in0=kv[:, :, 0, half * 512:(half + 1) * 512],
                                                           in1=bcast(pq[half][:, :].unsqueeze(1), [128, 2, 512]), op=ALU.mult), r=[kv, pq[half]], w=[kv])
                kb.op("dve", lambda e: e.tensor_reduce(out=sc[:, :].rearrange("p (m h) -> p m h", h=4), in_=kv[:, :, 0, :].rearrange("p m (h d) -> p m h d", h=4), axis=AX.X, op=ALU.add), r=[kv], w=[sc])
                kb.op("act", lambda e: e.activation(out=sc[:, :], in_=sc[:, :], func=AF.Exp), r=[sc], w=[sc])
                pd = kb.ps()
                for mt in range(2):
                    kb.op("pe", lambda e, mt=mt: e.matmul(pd[:, 0:4], onesf[:, :], sc[:, mt * 4:(mt + 1) * 4], start=(mt == 0), stop=(mt == 1)), r=[onesf, sc], w=[pd])
                kb.op("dve", lambda e: e.reciprocal(rd4[:, :], pd[:, 0:4]), r=[pd], w=[rd4])
                po = kb.ps()
                for c in range(8):
                    for mt in range(2):
                        kb.op("pe", lambda e, mt=mt: e.matmul(po[:, c:c + 1], kv[:, mt, 1, c * 128:(c + 1) * 128], sc[:, mt * 4 + c // 2: mt * 4 + c // 2 + 1],
                                                              start=(mt == 0), stop=(mt == 1)), r=[kv, sc], w=[po])
                kb.op("dve", lambda e: e.tensor_tensor(out=aT[:, :, NOWN * 128 + si].rearrange("p (h j) -> p h j", j=2),
                                                       in0=po[:, 0:8].rearrange("p (h j) -> p h j", j=2),
                                                       in1=bcast(rd4[:, :].unsqueeze(2), [128, 4, 2]), op=ALU.mult), r=[po, rd4], w=[a_r[16]])
        kb.barrier()
        linear_add(aT, a_r, wo)
        if "2" in DBG:
            for i, (t0, n) in enumerate(tiles):
                if i < NOWN:
                    kb.dma("sp", [(o_y[i, :, :], xres[:, i, :])], r=[xr[i]], final=True)
                else:
                    kb.dma("sp", [(o_ys[:, :], xres[0:NS, i, :])], r=[xr[i]], final=True)
        norm_all(3)
        wupS = w_up.rearrange("(k p) n -> p k n", p=128)
        wdnS = w_down.rearrange("(c p) n -> p c n", p=128)
        aF = kb.sb([128, 8, 512], BF16, stack=st, name="aF")
        sq = [kb.sb([128, 512], F32, stack=st) for _ in range(2)]
        for qtr in range(4):
            wu, wd = wA[qtr % 2], wA[(qtr + 1) % 2]
            kb.dma("pool", [(wu[:, k, :], wupS[:, k, qtr * 1024:(qtr + 1) * 1024]) for k in range(8)], w=[wu])
            kb.dma("pool", [(wd[:, k, :], wdnS[:, qtr * 8 + k, :]) for k in range(8)], w=[wd])
            for blk in range(5):
                b0 = blk * 512
                bw = 512 if blk < 4 else NS
                rr = [h_r[4 * blk + j] for j in range(4)] if blk < 4 else [h_r[16]]
                for fc in range(8):
                    pa = kb.ps()
                    for kc in range(8):
                        kb.op("pe", lambda e, kc=kc: e.matmul(pa[:, 0:bw], wu[:, kc, fc * 128:(fc + 1) * 128], hT[:, kc, b0:b0 + bw],
                                                              start=(kc == 0), stop=(kc == 7)), r=rr + [wu], w=[pa])
                    sq_ = sq[fc % 2]
                    kb.op("act", lambda e: e.activation(out=sq_[:, 0:bw], in_=pa[:, 0:bw], func=AF.Square), r=[pa], w=[sq_])
                    kb.op("dve", lambda e: e.scalar_tensor_tensor(out=aF[:, fc, 0:bw], in0=pa[:, 0:bw], scalar=0.0, in1=sq_[:, 0:bw],
                                                                  op0=ALU.is_gt, op1=ALU.mult), r=[pa, sq_], w=[aF])
                tl = range(4 * blk, 4 * blk + 4) if blk < 4 else [16]
                for i in tl:
                    t0, n = tiles[i]
                    l0 = t0 - b0
                    for half in range(2):
                        pa = kb.ps()
                        for fc in range(8):
                            kb.op("pe", lambda e, fc=fc: e.matmul(pa[0:n, :], aF[:, fc, l0:l0 + n], wd[:, fc, half * 512:(half + 1) * 512],
                                                                  start=(fc == 0), stop=(fc == 7)), r=[aF, wd], w=[pa])
                        kb.op("dve", lambda e: e.tensor_tensor(out=xres[0:n, i, half * 512:(half + 1) * 512], in0=pa[0:n, :],
                                                               in1=xres[0:n, i, half * 512:(half + 1) * 512], op=ALU.add), r=[pa, xr[i]], w=[xr[i]])
        if "3" in DBG:
            for i, (t0, n) in enumerate(tiles):
                if i < NOWN:
                    kb.dma("sp", [(o_y[i, :, :], xres[:, i, :])], r=[xr[i]], final=True)
                else:
                    kb.dma("sp", [(o_ys[:, :], xres[0:NS, i, :])], r=[xr[i]], final=True)
        gfin = kb.sb([128, D], F32, stack=st, name="gfin")
        kb.dma("sp", [(gfin[:], bass.AP(ln_fin.tensor, 0, [[0, 128], [1, D]]))], w=[gfin])
        yo = [kb.sb([128, D], F32, stack=st) for _ in range(2)]
        for i, (t0, n) in enumerate(tiles):
            _, ss, rstd, xbf = stg[i % 2]
            y = yo[i % 2]
            kb.op("act", lambda e: e.activation(out=y[0:n, :], in_=xres[0:n, i, :], func=AF.Square, accum_out=ss[0:n, 0:1]), r=[xr[i]], w=[y, ss])
            kb.op("act", lambda e: e.activation(out=rstd[0:n, :], in_=ss[0:n, :], func=AF.Sqrt, scale=1.0 / D, bias=epsc[0:n, 0:1]), r=[ss, epsc], w=[rstd])
            kb.op("dve", lambda e: e.reciprocal(rstd[0:n, :], rstd[0:n, :]), r=[rstd], w=[rstd])
            kb.op("dve", lambda e: e.scalar_tensor_tensor(out=y[0:n, :], in0=xres[0:n, i, :], scalar=rstd[0:n, 0:1], in1=gfin[0:n, :],
                                                          op0=ALU.mult, op1=ALU.mult), r=[xr[i], rstd, gfin], w=[y])
            if "1" not in DBG and "2" not in DBG and "3" not in DBG:
                if i < NOWN:
                    kb.dma("sp", [(o_y[i, :, :], y[:, :])], r=[y], final=True)
                else:
                    kb.dma("sp", [(o_ys[:, :], y[0:NS, :])], r=[y], final=True)
        kb.barrier()

    kb.finish()
    return nc


_NC = None


def kernel(**inp):
    global _NC
    f = lambda k: np.ascontiguousarray(np.asarray(inp[k]))
    x_prompt, x_sample, mem_prompt = f("x_prompt"), f("x_sample"), f("mem_prompt")
    cache_win, sconv = f("cache_win_kv"), f("state_dn_conv")
    w_in = f("w_in")[0]
    lnv = np.zeros((128, 6, 8), np.float32)
    for i, k in enumerate(("ln_mix", "ln_mem", "ln_memkv", "ln_ffn")):
        lnv[:, i, :] = f(k)[0].reshape(8, 128).T
    lnv[:, 4, :] = f("ln_final").reshape(8, 128).T
    abrep = np.ascontiguousarray(np.tile(np.stack([f("dn_a_log")[0], f("dn_dt_bias")[0]], axis=1), (NS, 1)))
    dnnorm = np.ascontiguousarray(np.tile(f("dn_norm")[0][None, :], (128, 1)))
    perm = list(range(512))
    for j in range(4):
        perm += [512 + j * 64 + d for d in range(64)] + [512 + (j + 4) * 64 + d for d in range(64)]
    perm = np.array(perm)
    w_out_p = np.ascontiguousarray(f("w_out")[0][perm])
    ii = np.arange(128)
    cols = [(ii[:, None] <= ii[None, :]), np.where(ii[None, :] <= ii[:, None], 0.0, -1e30), np.where(ii[None, :] >= ii[:, None], 0.0, -1e30),
            (ii[None, :] < ii[:, None]), (ii[:, None] // 64 == ii[None, :] // 64)]
    for lv in range(7):
        cols.append((ii[:, None] // (2 << lv) == ii[None, :] // (2 << lv)) & (ii[:, None] // (1 << lv) != ii[None, :] // (1 << lv)))
    cw = f("dn_conv_w")[0]
    cols.append(cw.T.reshape(12, 128, 4).transpose(1, 0, 2).reshape(128, 48))
    cols.append(np.tile(f("dn_a_log")[0][None, :], (128, 1)))
    cols.append(np.tile(f("dn_dt_bias")[0][None, :], (128, 1)))
    cols.append((ii < 64)[:, None])
    cols.append((ii >= 64)[:, None])
    dnc = np.ascontiguousarray(np.concatenate([np.asarray(c, np.float32) for c in cols], axis=1))
    NEG = -30000.0
    w1 = f("cmp_w1")[0]; pe = f("cmp_pe")[0]
    w1rep = np.ascontiguousarray(np.tile(w1.reshape(2, 64, 64, 128).transpose(1, 0, 2, 3), (2, 1, 1, 1)))
    perep = np.ascontiguousarray(np.tile(np.repeat(pe.transpose(1, 0, 2)[:, :, None, :], 2, axis=2).reshape(64, 256), (2, 1)))
    eom = np.ascontiguousarray(np.stack([(ii < 64), (ii >= 64)], axis=1).astype(np.float32))
    b1in = np.ascontiguousarray(f("cmp_b1")[0].T)
    w2in = np.ascontiguousarray(f("cmp_w2")[0].transpose(1, 0, 2))
    qperm = []
    for j in range(4):
        qperm += [OFF_Q + j * 64 + d for d in range(64)] + [OFF_Q + (j + 4) * 64 + d for d in range(64)]
    wq_p = np.ascontiguousarray(w_in[:, qperm])
    keys = np.arange(NT * 128)
    nprime = ((keys % 128) // 64) * NT + keys // 128
    Ein = np.ascontiguousarray((np.arange(66)[:, None] == nprime[None, :]).astype(np.float32))
    tri = np.where(ii[:, None] <= ii[None, :], 0.0, NEG)
    wlo = np.where(ii[:, None] > ii[None, :], 0.0, NEG)
    cnsa_h = f("cache_nsa_kv").reshape(2560 * 128, 512)
    iotap_h = np.arange(128, dtype=np.float32).reshape(128, 1)
    sconst_h = np.zeros((1, 512), np.float32)
    sconst_h[0, 0:64] = 1.0
    sconst_h[0, 128 + 64:256] = 1.0
    sconst_h[0, 256] = 1000.0
    sconst_h[0, 256 + 31] = 1000.0
    sconst_h[0, 256 + 32] = 1000.0

    def nsa_consts(h):
        sh = 1 - h
        mk3 = np.stack([np.tile(tri, (1, 4)), np.tile(wlo, (1, 4)), np.full((128, 512), NEG if sh == 1 else 0.0)], axis=1).astype(np.float32)
        npr = np.arange(66)
        nglob = 2 * (npr % NT - sh) + npr // NT
        cmk = np.zeros((NOWN, 66, 512), np.float32)
        addc = np.zeros((NOWN, 128, 2, 66), np.float32)
        for i in range(NOWN):
            qpos = (2 * i + 1 - sh) * 128 + ii
            vis = (nglob[:, None] >= 0) & (64 * (nglob[:, None] + 1) - 1 <= qpos[None, :])
            cmk[i] = np.tile(np.where(vis, 0.0, NEG), (1, 4))
            cur = qpos // 64
            valid = (nglob[None, :] >= 0) & (nglob[None, :] <= cur[:, None]) & (nglob[None, :] < 64)
            forced = valid & ((nglob[None, :] == 0) | (cur[:, None] - nglob[None, :] < 2))
            addc[i, :, 0, :] = np.where(valid, np.where(forced, 1000.0, 0.0), -1.0)
            addc[i, :, 1, :] = np.where(valid, 0.0, NEG)
        return {"w1rep": w1rep, "perep": perep, "eomin": eom, "b1in": b1in, "w2in": w2in, "wq_p": wq_p, "Ein": Ein,
                "mk3in": np.ascontiguousarray(mk3), "cmkin": cmk, "addcin": addc}

    in_maps = []
    for c in range(8):
        b, h = c // 2, c % 2
        sh = 1 - h
        xpc = np.zeros((NT * 128, D), np.float32)
        xpc[sh * 128: sh * 128 + 4096] = x_prompt[b]
        sl = slice(c * NS, (c + 1) * NS)
        in_maps.append({
            "xp": xpc, "xs": x_sample[sl, 0], "memp": mem_prompt[b], "w_in": w_in,
            "w_mem_kv": f("w_mem_kv")[0], "lnv": lnv,
            "cwin": cache_win[0, sl].reshape(NS, 512, 256), "sconv": sconv[0, sl],
            "conv_w": f("dn_conv_w")[0], "abrep": abrep, "dnnorm": dnnorm,
            "srec": f("state_dn_rec")[0, sl].reshape(128, 64, 64),
            "dncin": dnc, "w_out": w_out_p, **nsa_consts(c % 2),
            "cnsa": cnsa_h, "ptab": np.ascontiguousarray(f("page_table")[sl].reshape(1, NS * 16).astype(np.int32)), "iotap": iotap_h, "sconst": sconst_h, "w_mem_q": f("w_mem_q")[0], "w_mem_o": f("w_mem_o")[0], "w_up": f("w_up")[0], "w_down": f("w_down")[0],
            "ln_fin": f("ln_final").reshape(1, D), "cmem": f("cache_mem_kv")[0, sl].reshape(NS, 256, 2, 1024),
        })
    if "m" in DBG or "n" in DBG:
        dm = inp["_dbg_mix"]
        for c in range(8):
            in_maps[c]["dbg_mixT"] = np.ascontiguousarray(dm[c][:, perm].reshape(-1, 8, 128).transpose(2, 1, 0))
    if _NC is None:
        _NC = build()
    res = run_bass_kernel_spmd(_NC, in_maps, core_ids=list(range(8)))
    R = res.results
    B, S = 4, 4096
    y_prompt = np.zeros((B, S, D), np.float32)
    y_sample = np.zeros((128, 1, D), np.float32)
    p_nsa = np.zeros((1, B, S, 512), np.float32)
    p_win = np.zeros((1, B, 512, 256), np.float32)
    p_mem = np.zeros((1, B, 256, 2048), np.float32)
    p_conv = np.zeros((1, B, 3, 1536), np.float32)
    p_rec = np.zeros((1, B, 8, 64, 64), np.float32)
    s_nsa = np.zeros((1, 128, 512), np.float32)
    s_win = np.zeros((1, 128, 512, 256), np.float32)
    s_conv = np.zeros((1, 128, 3, 1536), np.float32)
    s_rec = np.zeros((1, 128, 8, 64, 64), np.float32)
    for c in range(8):
        b, h = c // 2, c % 2
        sh = 1 - h
        r = R[c]
        sl = slice(c * NS, (c + 1) * NS)
        pn = p_nsa[0, b].reshape(32, 128, 512)
        for i in range(NOWN):
            pn[2 * i + 1 - sh] = r["o_nsa"][i]
        if h == 0:
            p_rec[0, b] = r["o_prec"]
            p_win[0, b] = r["o_win"][1:5].reshape(512, 256)
            p_mem[0, b] = r["o_mem"]
            p_conv[0, b] = r["o_conv"]
        s_nsa[0, sl] = r["o_snsa"]
        s_win[0, sl] = r["o_swin"]
        s_conv[0, sl] = r["o_sconv"]
        s_rec[0, sl] = r["o_srec"].reshape(NS, 8, 64, 64)
        y_sample[sl, 0] = r["o_ys"]
        yp = y_prompt[b].reshape(32, 128, D)
        for i in range(NOWN):
            yp[2 * i + 1 - sh] = r["o_y"][i]
    return (y_prompt, y_sample, p_nsa.reshape(1, B, S, 4, 2, 64), p_win.reshape(1, B, 512, 2, 2, 64),
            p_mem.reshape(1, B, 256, 2, 4, 256), p_conv, p_rec, s_nsa.reshape(1, 128, 1, 4, 2, 64),
            s_win.reshape(1, 128, 512, 2, 2, 64), s_conv, s_rec)
```
